# Optimizing a Trainium2 kernel written in Bass

```python
import jax, jax.numpy as jnp
from jax import lax
import numpy as np

D_MODEL = 1024
BATCH = 32
SEQ = 2048
DEPTH = 1

SB_HEADS = 8
SB_HEAD_DIM = 64
MLA_HEADS = 8
QK_NOPE_DIM = 64
QK_ROPE_DIM = 32
V_HEAD_DIM = 64
Q_LORA_RANK = 256
KV_LORA_RANK = 128
SB_WIDTH = SB_HEADS * SB_HEAD_DIM
MLA_WIDTH = MLA_HEADS * V_HEAD_DIM
D_MIX = SB_WIDTH + MLA_WIDTH
IN_COLS = 3 * SB_WIDTH + Q_LORA_RANK + KV_LORA_RANK + QK_ROPE_DIM
D_FF = 2816
CONV_WIDTH = 3
PLE_DIM = 256
Q_BLOCK = 128
ROPE_THETA = 10000.0
NORM_EPS = 1e-6
MAX_POS_OFFSET = 4096

kernel_name = 'hybrid_stickbreak_mla_block'


def rms_norm(x, gain):
    xf = x.astype(jnp.float32)
    y = xf * lax.rsqrt(jnp.mean(xf * xf, axis=-1, keepdims=True) + NORM_EPS)
    return (y * gain.astype(jnp.float32)).astype(x.dtype)


def rope_tables(positions, dtype):
    inv_freq = 1.0 / (ROPE_THETA ** (jnp.arange(0, QK_ROPE_DIM, 2, dtype=jnp.float32) / QK_ROPE_DIM))
    ang = positions.astype(jnp.float32)[..., None] * inv_freq
    return jnp.cos(ang).astype(dtype), jnp.sin(ang).astype(dtype)


def apply_rope(x, cos, sin):
    x1, x2 = jnp.split(x, 2, axis=-1)
    return jnp.concatenate([x1 * cos - x2 * sin, x2 * cos + x1 * sin], axis=-1)


def stick_breaking_attention(q, k, v):
    seq = q.shape[1]
    scale = SB_HEAD_DIM ** -0.5
    outs = []
    for start in range(0, seq, Q_BLOCK):
        end = start + Q_BLOCK
        z = jnp.einsum('bqhd,bkhd->bhqk', q[:, start:end], k[:, :end]).astype(jnp.float32) * scale
        t_idx = start + jnp.arange(Q_BLOCK)[:, None]
        s_idx = jnp.arange(end)[None, :]
        strict = s_idx < t_idx
        log_keep = jnp.where(strict, -jax.nn.softplus(z), 0.0)
        log_keep_after = lax.cumsum(log_keep, axis=3, reverse=True) - log_keep
        weights = jnp.where(strict, jnp.exp(jax.nn.log_sigmoid(z) + log_keep_after), 0.0)
        outs.append(jnp.einsum('bhqk,bkhd->bqhd', weights.astype(v.dtype), v[:, :end]))
    return jnp.concatenate(outs, axis=1)


def mla_attention(q_nope, q_rope, k_nope, k_rope, v):
    seq = q_nope.shape[1]
    scale = (QK_NOPE_DIM + QK_ROPE_DIM) ** -0.5
    outs = []
    for start in range(0, seq, Q_BLOCK):
        end = start + Q_BLOCK
        s = (jnp.einsum('bqhd,bkhd->bhqk', q_nope[:, start:end], k_nope[:, :end])
             + jnp.einsum('bqhd,bkd->bhqk', q_rope[:, start:end], k_rope[:, :end]))
        s = s.astype(jnp.float32) * scale
        causal = jnp.arange(end)[None, :] <= (start + jnp.arange(Q_BLOCK)[:, None])
        probs = jax.nn.softmax(jnp.where(causal, s, -jnp.inf), axis=-1)
        outs.append(jnp.einsum('bhqk,bkhd->bqhd', probs.astype(v.dtype), v[:, :end]))
    return jnp.concatenate(outs, axis=1)


def causal_depthwise_conv(u, w, b):
    c = u.shape[-1]
    y = lax.conv_general_dilated(u, w[:, None, :].astype(u.dtype), window_strides=(1,),
                                 padding=[(CONV_WIDTH - 1, 0)],
                                 dimension_numbers=('NWC', 'WIO', 'NWC'),
                                 feature_group_count=c)
    return y + b.astype(u.dtype)


def setup_inputs(seed: int = 0) -> dict:
    key = jax.random.key(seed)
    ks = jax.random.split(key, 24)
    f32 = jnp.float32

    def normal(k, shape, scale):
        return jax.random.normal(k, shape, f32) * scale

    def gain(k, shape):
        return 1.0 + 0.05 * jax.random.normal(k, shape, f32)

    L = DEPTH
    x = normal(ks[0], (BATCH, SEQ, D_MODEL), 1.0)
    p = normal(ks[1], (DEPTH, BATCH, SEQ, PLE_DIM), 1.0)
    offsets = jax.random.randint(ks[2], (BATCH, 1), 0, MAX_POS_OFFSET, dtype=jnp.int32)
    positions = offsets + jnp.arange(SEQ, dtype=jnp.int32)[None, :]
    return {
        'x': x,
        'p': p,
        'positions': positions,
        'w_in': normal(ks[3], (L, D_MODEL, IN_COLS), D_MODEL ** -0.5),
        'g_pre_mix': gain(ks[4], (L, D_MODEL)),
        'g_q_lat': gain(ks[5], (L, Q_LORA_RANK)),
        'w_q_up': normal(ks[6], (L, Q_LORA_RANK, MLA_HEADS * (QK_NOPE_DIM + QK_ROPE_DIM)), Q_LORA_RANK ** -0.5),
        'g_kv_lat': gain(ks[7], (L, KV_LORA_RANK)),
        'w_kv_up': normal(ks[8], (L, KV_LORA_RANK, MLA_HEADS * (QK_NOPE_DIM + V_HEAD_DIM)), KV_LORA_RANK ** -0.5),
        'g_grp_sb': gain(ks[9], (L, SB_WIDTH)),
        'g_grp_mla': gain(ks[10], (L, MLA_WIDTH)),
        'w_o': normal(ks[11], (L, D_MIX, D_MODEL), D_MIX ** -0.5),
        'g_post_mix': gain(ks[12], (L, D_MODEL)),
        'g_pre_ffn': gain(ks[13], (L, D_MODEL)),
        'w_up': normal(ks[14], (L, D_MODEL, 2 * D_FF), D_MODEL ** -0.5),
        'conv_w': normal(ks[15], (L, CONV_WIDTH, 2 * D_FF), CONV_WIDTH ** -0.5),
        'conv_b': normal(ks[16], (L, 2 * D_FF), 0.01),
        'w_down': normal(ks[17], (L, D_FF, D_MODEL), D_FF ** -0.5),
        'g_post_ffn': gain(ks[18], (L, D_MODEL)),
        'w_ple': normal(ks[19], (L, PLE_DIM, D_MODEL), PLE_DIM ** -0.5),
        'g_ple_gate': gain(ks[20], (L, D_MODEL)),
        'w_ple_gate': normal(ks[21], (L, D_MODEL, D_MODEL), D_MODEL ** -0.5),
        'b_ple_gate': normal(ks[22], (L, D_MODEL), 0.01),
        'g_post_ple': gain(ks[23], (L, D_MODEL)),
    }


def reference(x, p, positions, w_in, g_pre_mix, g_q_lat, w_q_up, g_kv_lat, w_kv_up,
              g_grp_sb, g_grp_mla, w_o, g_post_mix, g_pre_ffn, w_up, conv_w, conv_b,
              w_down, g_post_ffn, w_ple, g_ple_gate, w_ple_gate, b_ple_gate, g_post_ple):
    b, s, _ = x.shape
    cos, sin = rope_tables(positions, x.dtype)
    split_at = [SB_WIDTH, 2 * SB_WIDTH, 3 * SB_WIDTH,
                3 * SB_WIDTH + Q_LORA_RANK, 3 * SB_WIDTH + Q_LORA_RANK + KV_LORA_RANK]
    for i in range(DEPTH):
        h = rms_norm(x, g_pre_mix[i])
        proj = h @ w_in[i]
        sb_q, sb_k, sb_v, q_lat, kv_lat, k_rope = jnp.split(proj, split_at, axis=-1)

        o_sb = stick_breaking_attention(sb_q.reshape(b, s, SB_HEADS, SB_HEAD_DIM),
                                        sb_k.reshape(b, s, SB_HEADS, SB_HEAD_DIM),
                                        sb_v.reshape(b, s, SB_HEADS, SB_HEAD_DIM)).reshape(b, s, SB_WIDTH)

        q = (rms_norm(q_lat, g_q_lat[i]) @ w_q_up[i]).reshape(b, s, MLA_HEADS, QK_NOPE_DIM + QK_ROPE_DIM)
        q_nope, q_rope = jnp.split(q, [QK_NOPE_DIM], axis=-1)
        q_rope = apply_rope(q_rope, cos[:, :, None, :], sin[:, :, None, :])
        kv = (rms_norm(kv_lat, g_kv_lat[i]) @ w_kv_up[i]).reshape(b, s, MLA_HEADS, QK_NOPE_DIM + V_HEAD_DIM)
        k_nope, v_mla = jnp.split(kv, [QK_NOPE_DIM], axis=-1)
        k_rope = apply_rope(k_rope, cos, sin)
        o_mla = mla_attention(q_nope, q_rope, k_nope, k_rope, v_mla).reshape(b, s, MLA_WIDTH)

        mix = jnp.concatenate([rms_norm(o_sb, g_grp_sb[i]), rms_norm(o_mla, g_grp_mla[i])], axis=-1) @ w_o[i]
        x = x + rms_norm(mix, g_post_mix[i])

        h = rms_norm(x, g_pre_ffn[i])
        u = causal_depthwise_conv(h @ w_up[i], conv_w[i], conv_b[i])
        gate, val = jnp.split(u, 2, axis=-1)
        f = (jax.nn.gelu(gate, approximate=True) * val) @ w_down[i]
        x = x + rms_norm(f, g_post_ffn[i])

        e = p[i] @ w_ple[i]
        g = jax.nn.sigmoid(rms_norm(x, g_ple_gate[i]) @ w_ple_gate[i] + b_ple_gate[i])
        x = x + rms_norm(g * e, g_post_ple[i])
    return x
```

```python
import contextlib
import math
import numpy as np
import concourse.bass as bass
import concourse.mybir as mybir
from concourse.bass_utils import run_bass_kernel_spmd

F32 = mybir.dt.float32
BF16 = mybir.dt.bfloat16
I32 = mybir.dt.int32
ALU = mybir.AluOpType
AF = mybir.ActivationFunctionType

NCORES = 8
SEQ = 2048
DM = 1024
DFF = 2816
NCH = 22
EPS = 1e-6
SC_MLA = 96 ** -0.5
ARENA = 68000
CELL = 256
CUT = 99
SKIP = set()


class Key:
    __slots__ = ("name", "w", "r")

    def __init__(self, name):
        self.name = name
        self.w = None
        self.r = {}


class Op:
    __slots__ = ("eng", "fn", "deps", "signal", "sigval", "lane", "laneval", "is_dma")

    def __init__(self, eng, fn, is_dma=False):
        self.eng = eng
        self.fn = fn
        self.deps = []
        self.signal = False
        self.sigval = None
        self.lane = None
        self.laneval = None
        self.is_dma = is_dma


class Prog:
    ENGS = ("pe", "act", "dve", "pool", "sp")

    def __init__(self, nc, n_lanes_sp=24, n_lanes_pool=8):
        self.nc = nc
        self.q = {e: [] for e in self.ENGS}
        self.n_lanes = {"sp": n_lanes_sp, "pool": n_lanes_pool}
        self.lane_rr = {"sp": 0, "pool": 0}
        self.lane_cnt = {"sp": [0] * n_lanes_sp, "pool": [0] * n_lanes_pool}
        self.lane_last = {"sp": [None] * n_lanes_sp, "pool": [None] * n_lanes_pool}

    def key(self, name):
        return Key(name)

    def _add(self, op, reads, writes):
        ex = [k for k in reads if k.name.startswith("pk")]
        if ex:
            reads = [k for k in reads if not k.name.startswith("pk")]
            writes = list(writes) + ex
        deps = op.deps
        for k in reads:
            if k.w is not None:
                deps.append(k.w)
        for k in writes:
            if k.w is not None:
                deps.append(k.w)
            for o in k.r.values():
                deps.append(o)
        rk = ("dma", id(op)) if op.is_dma else op.eng
        for k in reads:
            k.r[rk] = op
        for k in writes:
            k.w = op
            k.r = {}
        self.q[op.eng].append(op)
        return op

    def op(self, eng, fn, reads=(), writes=()):
        return self._add(Op(eng, fn), reads, writes)

    def dma(self, fn, reads=(), writes=(), queue="sp"):
        op = Op(queue, fn, is_dma=True)
        rr = self.lane_rr[queue]
        self.lane_rr[queue] = (rr + 1) % self.n_lanes[queue]
        op.lane = (queue, rr)
        self.lane_cnt[queue][rr] += 1
        op.laneval = 16 * self.lane_cnt[queue][rr]
        prev = self.lane_last[queue][rr]
        if prev is not None:
            op.deps.append(prev)
        self.lane_last[queue][rr] = op
        return self._add(op, reads, writes)

    def emit(self):
        nc = self.nc
        for e in self.ENGS:
            for op in self.q[e]:
                for d in op.deps:
                    if d.is_dma:
                        continue
                    if d.eng == "pe" and op.eng == "pe" and not op.is_dma:
                        continue
                    d.signal = True
        for e in self.ENGS:
            c = 0
            for op in self.q[e]:
                if op.signal and not op.is_dma:
                    c += 1
                    op.sigval = c
        with contextlib.ExitStack() as es:
            sems = {e: es.enter_context(nc.semaphore(f"s_{e}")) for e in ("pe", "act", "dve", "pool")}
            lanes = {}
            for qn in ("sp", "pool"):
                for i in range(self.n_lanes[qn]):
                    lanes[(qn, i)] = es.enter_context(nc.semaphore(f"l_{qn}{i}"))
            block = es.enter_context(nc.Block())

            def run(ename, eng):
                waited = {}
                for op in self.q[ename]:
                    need = {}
                    for d in op.deps:
                        if d.is_dma:
                            s, v = lanes[d.lane], d.laneval
                        else:
                            if d.eng == "pe" and ename == "pe" and not op.is_dma:
                                continue
                            s, v = sems[d.eng], d.sigval
                        kk = id(s)
                        if need.get(kk, (None, 0))[1] < v:
                            need[kk] = (s, v)
                    for kk, (s, v) in need.items():
                        if waited.get(kk, 0) < v:
                            eng.wait_ge(s, v)
                            waited[kk] = v
                    ins = op.fn(eng)
                    if op.is_dma:
                        ins.then_inc(lanes[op.lane], 16)
                    elif op.signal:
                        ins.then_inc(sems[ename], 1)
                if ename == "sp":
                    for qn in ("sp", "pool"):
                        for i in range(self.n_lanes[qn]):
                            c = self.lane_cnt[qn][i]
                            if c:
                                eng.wait_ge(lanes[(qn, i)], 16 * c)

            @block.tensor
            def _(e):
                run("pe", e)

            @block.scalar
            def _(e):
                run("act", e)

            @block.vector
            def _(e):
                run("dve", e)

            @block.gpsimd
            def _(e):
                run("pool", e)

            @block.sync
            def _(e):
                run("sp", e)


class Buf:
    def __init__(self, ap, keys, off=0, w=1):
        self.ap = ap
        self.keys = keys
        self.off = off
        self.w = w

    def k(self, lo=None, hi=None):
        if lo is None:
            return self.keys
        a = (self.off + lo * self.w) // CELL
        b = (self.off + hi * self.w - 1) // CELL
        base = self.off // CELL
        return self.keys[a - base:b - base + 1]


def build_program(nseq, debug=False, upto=4):
    nc = bass.Bass("TRN2", target_bir_lowering=False)
    ntok = nseq * SEQ
    dkind = "ExternalOutput" if debug else "Internal"

    def din(name, shape, dt=F32):
        return nc.dram_tensor(name, list(shape), dt, kind="ExternalInput").ap()

    x_c = din("x_c", [ntok, DM])
    p_c = din("p_c", [ntok, 256])
    posr = din("posr", [nseq, 128, 256], I32)
    w_in = din("w_in", [DM, 1952])
    w_qx = din("w_qx", [256, 1024])
    w_kvx = din("w_kvx", [128, 1024])
    w_o = din("w_o", [DM, DM])
    w_pg = din("w_pg", [DM, DM])
    w_ple = din("w_ple", [256, DM])
    w_upr = din("w_upr", [NCH, 128, 2048])
    w_dn = din("w_dn", [DFF, DM])
    vecs = din("vecs", [128, 192])
    bcv = din("bcv", [128, 7 * 1024])
    cf32 = din("cf32", [128, 448])
    cbf = din("cbf", [128, 896])
    out_c = nc.dram_tensor("out_c", [ntok, DM], F32, kind="ExternalOutput").ap()

    def dsc(name, shape):
        return nc.dram_tensor(name, list(shape), BF16, kind=dkind).ap()

    sc_qs = dsc("sc_qs", [4, 128, SEQ])
    sc_ks = dsc("sc_ks", [4, 128, SEQ])
    sc_vs = dsc("sc_vs", [SEQ, 512])
    sc_qm = dsc("sc_qm", [8, 128, SEQ])
    sc_km = dsc("sc_km", [4, 128, SEQ])
    sc_kr = dsc("sc_kr", [64, SEQ])
    sc_vm = dsc("sc_vm", [SEQ, 512])
    sc_o = dsc("sc_o", [8, 128, SEQ])
    sc_wu = nc.dram_tensor("sc_wu", [NCH, 128, 2048], BF16, kind="Internal").ap()
    sc_wd = nc.dram_tensor("sc_wd", [DFF, DM], BF16, kind="Internal").ap()

    es = contextlib.ExitStack()
    with es:
        def sb(name, shape, dt):
            return es.enter_context(nc.sbuf_tensor(name, list(shape), dt))

        P = Prog(nc)
        arena_h = sb("arena", [128, ARENA], BF16)
        akeys = [Key(f"a{i}") for i in range((ARENA + CELL - 1) // CELL)]

        def abuf(off, n, dt=BF16, shape=None):
            w = 1 if dt == BF16 else 2
            assert off % 2 == 0 and off + n * w <= ARENA, (off, n, w)
            ap = arena_h[:, off:off + n * w]
            if dt != BF16:
                ap = ap.bitcast(dt)
            if shape is not None:
                names = " ".join(f"d{i}" for i in range(len(shape)))
                ap = ap.rearrange(f"p ({names}) -> p {names}", **{f"d{i}": shape[i] for i in range(len(shape))})
            a = off // CELL
            b = (off + n * w - 1) // CELL
            return Buf(ap, akeys[a:b + 1], off, w)

        def rbuf(name, shape, dt, nkeys=1):
            h = sb(name, shape, dt)
            return Buf(h, [Key(name)]), h

        wo_b, wo_h = rbuf("wo", [128, 8, 1024], BF16)
        wpg_b, wpg_h = rbuf("wpg", [128, 8, 1024], BF16)
        wple_b, wple_h = rbuf("wple", [128, 2, 1024], BF16)
        bc_b, bc_h = rbuf("bc", [128, 7, 1024], F32)
        cb_b, cb_h = rbuf("cb", [128, 896], BF16)
        cf_b, cf_h = rbuf("cf", [128, 448], F32)
        vec_b, vec_h = rbuf("vec", [128, 192], F32)
        stat_h = sb("stat", [128, 64], F32)
        stat_keys = [Key(f"st{i}") for i in range(64)]
        halo_h = sb("halo", [128, 44, 2], F32)
        halo_keys = [Key(f"halo{i}") for i in range(44)]
        ps_h = es.enter_context(nc.psum_tensor("ps", [128, 4096], F32))
        pk = [Key(f"pk{i}") for i in range(8)]

        def bank(b, n=512, lo=0):
            return ps_h[:, b * 512 + lo:b * 512 + lo + n]

        def bankbf(b):
            return ps_h[:, b * 512:(b + 1) * 512].bitcast(BF16)

        ident = cb_h[:, 0:128]
        mstrict = cb_h[:, 128:256]
        mle = cb_h[:, 256:384]
        negU = cb_h[:, 384:512]
        negOnes = cb_h[:, 512:640]
        gsum = cb_h[:, 640:768]
        zeros = cb_h[:, 768:896]
        ident32 = cf_h[:, 0:128]
        invf = cf_h[:, 128:384]
        esel = cf_h[:, 384:448]
        GPM, GPOM, GPF, GPOF, GPG, GPOP, BPG = range(7)
        V_GQ, V_GKV, V_GGRP, V_CW, V_CB = 0, 2, 3, 11, 143
        CK = [cb_b.keys[0], cf_b.keys[0], vec_b.keys[0], bc_b.keys[0]]

        stat_i = [0]

        def newcol():
            i = stat_i[0] % 64
            stat_i[0] += 1
            return stat_h[:, i:i + 1], stat_keys[i]

        def rstd_of(ss_ap, ss_key, n):
            c1, k1 = newcol()
            P.op("act", lambda e: e.activation(out=c1, in_=ss_ap, func=AF.Sqrt, scale=1.0 / n, bias=EPS), reads=[ss_key], writes=[k1])
            c2, k2 = newcol()
            P.op("dve", lambda e: e.reciprocal(out=c2, in_=c1), reads=[k1], writes=[k2])
            return c2, k2

        P.dma(lambda e: e.dma_start(out=cf_h[:], in_=cf32), writes=cf_b.keys)
        P.dma(lambda e: e.dma_start(out=vec_h[:], in_=vecs), writes=vec_b.keys)
        P.dma(lambda e: e.dma_start(out=bc_h[:].rearrange("p a b -> p (a b)"), in_=bcv), writes=bc_b.keys)
        P.dma(lambda e: e.dma_start(out=cb_h[:], in_=cbf), writes=cb_b.keys, queue="pool")
        def load_resident_weights():
            for k in range(8):
                P.dma(lambda e, k=k: e.dma_start(out=wo_h[:, k, :], in_=w_o[k * 128:(k + 1) * 128, :]), writes=wo_b.keys, queue="pool")
                P.dma(lambda e, k=k: e.dma_start(out=wpg_h[:, k, :], in_=w_pg[k * 128:(k + 1) * 128, :]), writes=wpg_b.keys, queue="pool")
            for k in range(2):
                P.dma(lambda e, k=k: e.dma_start(out=wple_h[:, k, :], in_=w_ple[k * 128:(k + 1) * 128, :]), writes=wple_b.keys, queue="pool")

        win_b = [abuf(k * 2048, 1952) for k in range(8)]
        wq_b = abuf(16384, 2048, shape=[2, 1024])
        wkv_b = abuf(18432, 1024)
        A1 = 19456
        xs_b = [abuf(A1 + i * 2048, 1024, F32) for i in range(2)]
        hb1_b = [abuf(A1 + 4096 + i * 1024, 1024) for i in range(2)] + [abuf(56832 + i * 1024, 1024) for i in range(2)] + [abuf(60416 + i * 1024, 1024) for i in range(4)]
        hT1_b = [abuf(A1 + 6144 + i * 4096, 4096, shape=[8, 512]) for i in range(2)]
        stg_b = [abuf(A1 + 14336 + i * 512, 512) for i in range(6)]
        vst_b = [abuf(A1 + 17408 + i * 2048, 2048, shape=[4, 512]) for i in range(2)]
        L0 = A1 + 21504
        qln_b = [abuf(L0 + 7168 + i * 256, 256) for i in range(4)]
        kvln_b = [abuf(L0 + 8192 + i * 128, 128) for i in range(4)]
        krt_b = [abuf(L0 + 8704 + i * 32, 32) for i in range(4)]
        ropeA_b = [abuf(L0 + i * 64, 32, F32) for i in range(4)]
        ropeB_b = [abuf(L0 + 256 + i * 64, 32, F32) for i in range(4)]
        qlT_b = abuf(L0 + 1024, 1024, shape=[2, 512])
        kvT_b = abuf(L0 + 2048, 512)
        krT_b = abuf(L0 + 2560, 512)
        TtT_b = abuf(L0 + 3072, 512, F32)
        TQ_b = [abuf(L0 + 4096 + i * 1024, 512, F32, shape=[4, 128]) for i in range(2)]
        cos_b = abuf(L0 + 6144, 256, F32)
        sin_b = abuf(L0 + 6656, 256, F32)
        rs_b = [abuf(L0 + 7168 + i * 512, 256, F32) for i in range(4)]
        rsi_b = abuf(L0 + 9216, 256, I32)
        assert L0 + 9728 <= ARENA
        A2 = A1
        QT_b = [abuf(A2 + i * 2048, 2048) for i in range(2)]
        KT_b = [abuf(A2 + 4096 + i * 2048, 2048) for i in range(2)]
        V_b = [abuf(A2 + 8192 + i * 2048, 2048) for i in range(2)]
        et_b = [abuf(A2 + 12288 + i * 1024, 512, F32) for i in range(2)]
        sp_b = [abuf(A2 + 14336 + i * 512, 512) for i in range(3)]
        At_b = [abuf(A2 + 15872 + i * 512, 512) for i in range(3)]
        sacc_b = [abuf(A2 + 23040 + i * 512, 512) for i in range(3)]
        KTz_b = [[KT_b[i], abuf(A2 + 24576 + i * 2048, 2048)] for i in range(2)]
        Vz_b = [[V_b[i], abuf(A2 + 28672 + i * 2048, 2048)] for i in range(2)]
        assert A2 + 32768 <= 59392
        oa_b = [abuf(A2 + 17920 + i * 1024, 512, F32) for i in range(2)]
        rden_b = [abuf(A2 + 19968 + i * 1024, 512, F32) for i in range(2)]
        ost_b = [abuf(A2 + 22016 + i * 512, 512) for i in range(2)]
        wd_b = [abuf(c * 1024, 1024) for c in range(NCH)]
        x1_b = [abuf(22528 + i * 2048, 1024, F32) for i in range(4)]
        hb3_b = [abuf(30720 + i * 1024, 1024) for i in range(2)]
        hT3_b = abuf(32768, 4096, shape=[8, 512])
        fin_b = abuf(36864, NCH * 512, shape=[NCH, 512])
        sq_b = abuf(36864, 4096, shape=[8, 512])
        OT_b = abuf(48128, 4096, shape=[8, 512])
        y_b = [abuf(52224 + i * 1024, 512, F32) for i in range(4)]
        gg_b = [abuf(56320 + i * 512, 512) for i in range(2)]
        tmpA_b = abuf(57344, 1024, F32)
        junk_b = abuf(59392, 1024)
        pb_b = abuf(60416, 1024, shape=[4, 256])
        pT_b = abuf(61440, 1024, shape=[2, 512])
        wu_b = [abuf(60416 + i * 2048, 2048, shape=[2, 8, 128]) for i in range(3)]
        tmpA2 = [tmpA_b, abuf(52224 + 2048, 1024, F32)]
        cst_b = [abuf(52224 + i * 2048, 2048) for i in range(2)]
        assert 66560 <= ARENA

        dk = {}

        def DK(*a):
            if a not in dk:
                dk[a] = Key(str(a))
            return dk[a]

        rr = {"pb": 0}
        def _conv_up(c):
            sl = cst_b[c % 2]
            P.dma(lambda e: e.dma_start(out=sl.ap, in_=w_upr[c]), writes=sl.k(), queue="pool")
            P.dma(lambda e: e.dma_start(out=sc_wu[c], in_=sl.ap), reads=sl.k(), writes=[DK("wu", c)])

        def _conv_dn(c):
            sl = cst_b[(NCH + c) % 2]
            P.dma(lambda e: e.dma_start(out=sl.ap[:, 0:1024], in_=w_dn[c * 128:(c + 1) * 128, :]), writes=sl.k(), queue="pool")
            P.dma(lambda e: e.dma_start(out=sc_wd[c * 128:(c + 1) * 128, :], in_=sl.ap[:, 0:1024]), reads=sl.k(), writes=[DK("wd", c)])

        def nextbank(banks):
            b = banks[rr["pb"] % len(banks)]
            rr["pb"] += 1
            return b

        conv_steps = []

        def stage1(s):
            tb = s * SEQ
            for k in range(8):
                P.dma(lambda e, k=k: e.dma_start(out=win_b[k].ap, in_=w_in[k * 128:(k + 1) * 128, :]), writes=win_b[k].k(), queue="pool")
            for k in range(2):
                P.dma(lambda e, k=k: e.dma_start(out=wq_b.ap[:, k, :], in_=w_qx[k * 128:(k + 1) * 128, :]), writes=wq_b.k(k * 1024, (k + 1) * 1024), queue="pool")
            P.dma(lambda e: e.dma_start(out=wkv_b.ap, in_=w_kvx), writes=wkv_b.k(), queue="pool")
            wq4 = wq_b.ap.rearrange("p k (h c) -> p k h c", h=8)
            for k in range(2):
                P.op("pool", lambda e, k=k: e.tensor_scalar(out=wq4[:, k, :, 96:112], in0=wq4[:, k, :, 96:112], scalar1=-1.0, scalar2=0.0, op0=ALU.mult, op1=ALU.add),
                     reads=wq_b.k(k * 1024, (k + 1) * 1024), writes=wq_b.k(k * 1024, (k + 1) * 1024))
            P.dma(lambda e: e.dma_start(out=rsi_b.ap, in_=posr[s]), writes=rsi_b.k())
            P.op("dve", lambda e: e.tensor_copy(out=rs_b[0].ap, in_=rsi_b.ap), reads=rsi_b.k(), writes=rs_b[0].k())
            P.op("dve", lambda e: e.tensor_tensor(out=rs_b[0].ap, in0=rs_b[0].ap, in1=invf, op=ALU.mult), reads=rs_b[0].k() + CK, writes=rs_b[0].k())
            for which, dst in ((0, sin_b), (1, cos_b)):
                P.op("dve", lambda e, which=which: e.tensor_scalar(out=rs_b[1].ap, in0=rs_b[0].ap, scalar1=1.0 / (2 * math.pi), scalar2=0.25 * which, op0=ALU.mult, op1=ALU.add),
                     reads=rs_b[0].k(), writes=rs_b[1].k())
                P.op("dve", lambda e: e.tensor_copy(out=rsi_b.ap, in_=rs_b[1].ap), reads=rs_b[1].k(), writes=rsi_b.k())
                P.op("dve", lambda e: e.tensor_copy(out=rs_b[2].ap, in_=rsi_b.ap), reads=rsi_b.k(), writes=rs_b[2].k())
                P.op("dve", lambda e: e.tensor_tensor(out=rs_b[3].ap, in0=rs_b[1].ap, in1=rs_b[2].ap, op=ALU.subtract), reads=rs_b[1].k() + rs_b[2].k(), writes=rs_b[3].k())
                P.op("act", lambda e, dst=dst: e.activation(out=dst.ap, in_=rs_b[3].ap, func=AF.Sin, scale=6.28318), reads=rs_b[3].k(), writes=dst.k())
            cos3 = cos_b.ap.rearrange("p (s i) -> p s i", i=16)
            sin3 = sin_b.ap.rearrange("p (s i) -> p s i", i=16)
            def passA(j):
                for i in range(4):
                    st = j * 4 + i
                    t0 = tb + st * 128
                    xs = xs_b[st % 2]
                    hb = hb1_b[(j % 2) * 4 + i]
                    P.dma(lambda e, xs=xs, t0=t0: e.dma_start(out=xs.ap, in_=x_c[t0:t0 + 128, :]), writes=xs.k())
                    ssc, ssk = newcol()
                    P.op("act", lambda e, xs=xs, ssc=ssc: e.activation(out=junk1.ap, in_=xs.ap, func=AF.Square, accum_out=ssc), reads=xs.k(), writes=junk1.k() + [ssk])
                    rc, rk = rstd_of(ssc, ssk, DM)
                    P.op("dve", lambda e, xs=xs, hb=hb, rc=rc: e.scalar_tensor_tensor(out=hb.ap, in0=xs.ap, scalar=rc, in1=bc_h[:, GPM, :], op0=ALU.mult, op1=ALU.mult),
                         reads=xs.k() + [rk] + CK, writes=hb.k())

            passA(0)
            for j in range(4):
                hT = hT1_b[j % 2]
                TQ = TQ_b[j % 2]
                cs = j * 512
                for _ in range(5):
                    if conv_steps:
                        conv_steps.pop(0)()
                P.op("pool", lambda e, TQ=TQ: e.memset(TQ.ap[:, :, 0:64], 1.0), writes=TQ.k())
                for q, src in ((64, cos3), (80, cos3), (96, sin3), (112, sin3)):
                    P.op("pool", lambda e, TQ=TQ, q=q, src=src, j=j: e.tensor_copy(out=TQ.ap[:, :, q:q + 16], in_=src[:, j * 4:(j + 1) * 4, :]),
                         reads=cos_b.k() + sin_b.k(), writes=TQ.k())
                for i in range(4):
                    hb = hb1_b[(j % 2) * 4 + i]
                    tbk = 6 + (i % 2)
                    for k in range(8):
                        P.op("pe", lambda e, hb=hb, k=k, tbk=tbk: e.transpose(out=bankbf(tbk)[:, k * 128:(k + 1) * 128], in_=hb.ap[:, k * 128:(k + 1) * 128], identity=ident),
                             reads=hb.k() + CK, writes=[pk[tbk]])
                    P.op("act", lambda e, hT=hT, i=i, tbk=tbk: e.activation(out=hT.ap[:, :, i * 128:(i + 1) * 128], in_=bankbf(tbk).rearrange("p (k t) -> p k t", k=8), func=AF.Copy),
                         reads=[pk[tbk]], writes=hT.k())
                if j < 3:
                    passA(j + 1)
                lat_banks = []
                for i in range(4):
                    b = [0, 1, 2, 3][i]
                    lat_banks.append(b)
                    for k in range(8):
                        P.op("pe", lambda e, b=b, k=k, i=i, hT=hT: e.matmul(out=bank(b, 416), lhsT=hT.ap[:, k, i * 128:(i + 1) * 128], rhs=win_b[k].ap[:, 1536:1952], start=(k == 0), stop=(k == 7)),
                             reads=win_b[k].k() + hT.k(), writes=[pk[b]])
                for i in range(4):
                    st = j * 4 + i
                    b = lat_banks[i]
                    qln, kvln, krt, rA, rB = qln_b[i], kvln_b[i], krt_b[i], ropeA_b[i], ropeB_b[i]
                    s1c, s1k = newcol()
                    P.op("act", lambda e, b=b, s1c=s1c: e.activation(out=junk1.ap[:, 0:256], in_=bank(b, 256), func=AF.Square, accum_out=s1c), reads=[pk[b]], writes=junk1.k() + [s1k])
                    s2c, s2k = newcol()
                    P.op("act", lambda e, b=b, s2c=s2c: e.activation(out=junk1.ap[:, 0:128], in_=bank(b, 128, 256), func=AF.Square, accum_out=s2c), reads=[pk[b]], writes=junk1.k() + [s2k])
                    r1c, r1k = rstd_of(s1c, s1k, 256)
                    r2c, r2k = rstd_of(s2c, s2k, 128)
                    P.op("act", lambda e, b=b, qln=qln, r1c=r1c: e.activation(out=qln.ap, in_=bank(b, 256), func=AF.Copy, scale=r1c), reads=[pk[b], r1k], writes=qln.k())
                    P.op("act", lambda e, b=b, kvln=kvln, r2c=r2c: e.activation(out=kvln.ap, in_=bank(b, 128, 256), func=AF.Copy, scale=r2c), reads=[pk[b], r2k], writes=kvln.k())
                    for half in range(2):
                        P.op("dve", lambda e, b=b, half=half, st=st, rA=rA: e.tensor_tensor(out=rA.ap[:, half * 16:(half + 1) * 16], in0=bank(b, 16, 384 + half * 16), in1=cos3[:, st, :], op=ALU.mult),
                             reads=[pk[b]] + cos_b.k(), writes=rA.k())
                        P.op("dve", lambda e, b=b, half=half, st=st, rB=rB: e.tensor_tensor(out=rB.ap[:, half * 16:(half + 1) * 16], in0=bank(b, 16, 384 + half * 16), in1=sin3[:, st, :], op=ALU.mult),
                             reads=[pk[b]] + sin_b.k(), writes=rB.k())
                    P.op("dve", lambda e, krt=krt, rA=rA, rB=rB: e.tensor_tensor(out=krt.ap[:, 0:16], in0=rA.ap[:, 0:16], in1=rB.ap[:, 16:32], op=ALU.subtract),
                         reads=rA.k() + rB.k(), writes=krt.k())
                    P.op("dve", lambda e, krt=krt, rA=rA, rB=rB: e.tensor_tensor(out=krt.ap[:, 16:32], in0=rA.ap[:, 16:32], in1=rB.ap[:, 0:16], op=ALU.add),
                         reads=rA.k() + rB.k(), writes=krt.k())
                for qk in range(2):
                    for m in range(4):
                        b = 4 + (rr["pb"] % 2)
                        rr["pb"] += 1
                        col = qk * 512 + m * 128
                        for k in range(8):
                            P.op("pe", lambda e, b=b, k=k, col=col, hT=hT: e.matmul(out=bank(b), lhsT=win_b[k].ap[:, col:col + 128], rhs=hT.ap[:, k, :], start=(k == 0), stop=(k == 7)),
                                 reads=win_b[k].k() + hT.k(), writes=[pk[b]])
                        sg = stg_b[rr["pb"] % 6]
                        if qk == 0:
                            P.op("act", lambda e, b=b, sg=sg: e.activation(out=sg.ap, in_=bank(b), func=AF.Copy, scale=0.125), reads=[pk[b]], writes=sg.k())
                            P.dma(lambda e, sg=sg, m=m, cs=cs: e.dma_start(out=sc_qs[m][:, cs:cs + 512], in_=sg.ap), reads=sg.k(), writes=[DK("qs", m, cs)])
                        else:
                            P.op("dve", lambda e, b=b, sg=sg: e.tensor_copy(out=sg.ap, in_=bank(b)), reads=[pk[b]], writes=sg.k())
                            P.dma(lambda e, sg=sg, m=m, cs=cs: e.dma_start(out=sc_ks[m][:, cs:cs + 512], in_=sg.ap), reads=sg.k(), writes=[DK("ks", m, cs)])
                for i in range(4):
                    qln, kvln, krt = qln_b[i], kvln_b[i], krt_b[i]
                    tbk = 6 + (i % 2)
                    for k in range(2):
                        P.op("pe", lambda e, k=k, qln=qln, tbk=tbk: e.transpose(out=bankbf(tbk)[:, k * 128:(k + 1) * 128], in_=qln.ap[:, k * 128:(k + 1) * 128], identity=ident), reads=qln.k() + CK, writes=[pk[tbk]])
                    P.op("pe", lambda e, kvln=kvln, tbk=tbk: e.transpose(out=bankbf(tbk)[:, 256:384], in_=kvln.ap, identity=ident), reads=kvln.k() + CK, writes=[pk[tbk]])
                    P.op("pe", lambda e, krt=krt, tbk=tbk: e.transpose(out=bankbf(tbk)[0:32, 384:512], in_=krt.ap, identity=ident), reads=krt.k() + CK, writes=[pk[tbk]])
                    for k in range(2):
                        P.op("act", lambda e, k=k, i=i, tbk=tbk: e.activation(out=qlT_b.ap[:, k, i * 128:(i + 1) * 128], in_=bankbf(tbk)[:, k * 128:(k + 1) * 128], func=AF.Copy, scale=vec_h[:, V_GQ + k:V_GQ + k + 1]),
                             reads=[pk[tbk]] + CK, writes=qlT_b.k(k * 512, (k + 1) * 512))
                    P.op("act", lambda e, i=i, tbk=tbk: e.activation(out=kvT_b.ap[:, i * 128:(i + 1) * 128], in_=bankbf(tbk)[:, 256:384], func=AF.Copy, scale=vec_h[:, V_GKV:V_GKV + 1]),
                         reads=[pk[tbk]] + CK, writes=kvT_b.k())
                    P.op("dve", lambda e, i=i, tbk=tbk: e.tensor_copy(out=krT_b.ap[0:32, i * 128:(i + 1) * 128], in_=bankbf(tbk)[0:32, 384:512]), reads=[pk[tbk]], writes=krT_b.k())
                    P.op("pe", lambda e, i=i, TQ=TQ: e.transpose(out=bank(2)[:, i * 128:(i + 1) * 128], in_=TQ.ap[:, i, :], identity=ident32), reads=TQ.k() + CK, writes=[pk[2]])
                P.op("act", lambda e: e.activation(out=TtT_b.ap, in_=bank(2), func=AF.Copy), reads=[pk[2]], writes=TtT_b.k())
                for rrow in range(2):
                    P.dma(lambda e, rrow=rrow, cs=cs: e.dma_start(out=sc_kr[rrow * 32:(rrow + 1) * 32, cs:cs + 512], in_=krT_b.ap[0:32, :]), reads=krT_b.k(), writes=[DK("kr", cs, rrow)])
                vst = vst_b[0]
                for i in range(4):
                    b = 4 + (rr["pb"] % 2)
                    rr["pb"] += 1
                    for k in range(8):
                        P.op("pe", lambda e, b=b, k=k, i=i, hT=hT: e.matmul(out=bank(b), lhsT=hT.ap[:, k, i * 128:(i + 1) * 128], rhs=win_b[k].ap[:, 1024:1536], start=(k == 0), stop=(k == 7)),
                             reads=win_b[k].k() + hT.k(), writes=[pk[b]])
                    if i % 2 == 0:
                        P.op("act", lambda e, b=b, i=i, vst=vst: e.activation(out=vst.ap[:, i, :], in_=bank(b), func=AF.Copy), reads=[pk[b]], writes=vst.k(i * 512, (i + 1) * 512))
                    else:
                        P.op("dve", lambda e, b=b, i=i, vst=vst: e.tensor_copy(out=vst.ap[:, i, :], in_=bank(b)), reads=[pk[b]], writes=vst.k(i * 512, (i + 1) * 512))
                P.dma(lambda e, vst=vst, cs=cs: e.dma_start(out=sc_vs[cs:cs + 512, :].rearrange("(i p) n -> p i n", p=128), in_=vst.ap), reads=vst.k(), writes=[DK("vs", cs)])
                for h in range(8):
                    b = [0, 1, 3][rr["pb"] % 3]
                    rr["pb"] += 1
                    for k in range(2):
                        P.op("pe", lambda e, b=b, k=k, h=h: e.matmul(out=bank(b), lhsT=wq_b.ap[:, k, h * 128:(h + 1) * 128], rhs=qlT_b.ap[:, k, :], start=(k == 0), stop=(k == 1)),
                             reads=wq_b.k() + qlT_b.k(), writes=[pk[b]])
                    sg = stg_b[rr["pb"] % 6]
                    P.op("dve", lambda e, b=b, sg=sg: e.tensor_tensor(out=sg.ap, in0=bank(b), in1=TtT_b.ap, op=ALU.mult), reads=[pk[b]] + TtT_b.k(), writes=sg.k())
                    P.dma(lambda e, sg=sg, h=h, cs=cs: e.dma_start(out=sc_qm[h][:, cs:cs + 512], in_=sg.ap), reads=sg.k(), writes=[DK("qm", h, cs)])
                for m in range(4):
                    b = [0, 1, 3][rr["pb"] % 3]
                    rr["pb"] += 1
                    P.op("pe", lambda e, b=b, m=m: e.matmul(out=bank(b), lhsT=wkv_b.ap[:, m * 128:(m + 1) * 128], rhs=kvT_b.ap, start=True, stop=True), reads=wkv_b.k() + kvT_b.k(), writes=[pk[b]])
                    sg = stg_b[rr["pb"] % 6]
                    P.op("act", lambda e, b=b, sg=sg: e.activation(out=sg.ap, in_=bank(b), func=AF.Copy), reads=[pk[b]], writes=sg.k())
                    P.dma(lambda e, sg=sg, m=m, cs=cs: e.dma_start(out=sc_km[m][:, cs:cs + 512], in_=sg.ap), reads=sg.k(), writes=[DK("km", m, cs)])
                vst = vst_b[1]
                for i in range(4):
                    b = [0, 1, 3][rr["pb"] % 3]
                    rr["pb"] += 1
                    P.op("pe", lambda e, b=b, i=i: e.matmul(out=bank(b), lhsT=kvT_b.ap[:, i * 128:(i + 1) * 128], rhs=wkv_b.ap[:, 512:1024], start=True, stop=True), reads=wkv_b.k() + kvT_b.k(), writes=[pk[b]])
                    P.op("dve", lambda e, b=b, i=i, vst=vst: e.tensor_copy(out=vst.ap[:, i, :], in_=bank(b)), reads=[pk[b]], writes=vst.k(i * 512, (i + 1) * 512))
                P.dma(lambda e, vst=vst, cs=cs: e.dma_start(out=sc_vm[cs:cs + 512, :].rearrange("(i p) n -> p i n", p=128), in_=vst.ap), reads=vst.k(), writes=[DK("vm", cs)])
                if s == 0 and j == 0:
                    load_resident_weights()

        conv_steps.extend([(lambda c=c: _conv_up(c)) for c in range(NCH)] + [(lambda c=c: _conv_dn(c)) for c in range(NCH)])
        junk1 = abuf(59392, 1024)

        def attn_blocks(qt):
            return [(kb, max(0, kb - 4 * qt) * 128) for kb in range(4 * qt + 3, -1, -1)]

        cnt = {"z": 0, "o": 0, "e": 0, "sp": 0, "a": 0, "oa": 0, "ost": 0}

        def run_pipeline(items, nst):
            n = len(items)
            for t in range(n + nst - 1):
                for k in range(nst):
                    i = t - k
                    if 0 <= i < n and items[i][k] is not None:
                        items[i][k]()

        gcount = {"g": 0, "b": 0}

        def stage2_sb(s):
            items = []

            def load_pair(m):
                QT = QT_b[m % 2]
                P.dma(lambda e: e.dma_start(out=QT.ap, in_=sc_qs[m]), reads=[DK("qs", m, c) for c in range(0, SEQ, 512)], writes=QT.k())
                for eh in range(2):
                    KTz, Vz = KTz_b[m % 2][eh], Vz_b[m % 2][eh]
                    Vz3 = Vz.ap.rearrange("p (kb n) -> p kb n", n=128)
                    oh = 1 - eh
                    P.dma(lambda e, KTz=KTz: e.dma_start(out=KTz.ap, in_=sc_ks[m]), reads=[DK("ks", m, c) for c in range(0, SEQ, 512)], writes=KTz.k())
                    P.op("pool", lambda e, KTz=KTz, oh=oh: e.memset(KTz.ap[oh * 64:(oh + 1) * 64, :], 0.0), reads=KTz.k(), writes=KTz.k())
                    P.dma(lambda e, Vz3=Vz3: e.dma_start(out=Vz3, in_=sc_vs[:, m * 128:(m + 1) * 128].rearrange("(kb p) n -> p kb n", p=128)),
                          reads=[DK("vs", c) for c in range(0, SEQ, 512)], writes=Vz.k())
                    P.op("pool", lambda e, Vz3=Vz3, oh=oh: e.memset(Vz3[:, :, oh * 64:(oh + 1) * 64], 0.0), reads=Vz.k(), writes=Vz.k())

            def make_item(m, qt, eh, bi, kb, c0, c0p, nblk, bo, pre):
                b = gcount["b"]
                gcount["b"] += 1
                QT, KT, V = QT_b[m % 2], KTz_b[m % 2][eh], Vz_b[m % 2][eh]
                V3 = V.ap.rearrange("p (kb n) -> p kb n", n=128)
                bz = b % 4
                et = et_b[b % 2]
                sp, spp = sp_b[b % 3], sp_b[(b - 1) % 3]
                At = At_b[b % 3]
                sa, sap = sacc_b[b % 3], sacc_b[(b - 1) % 3]
                diag = kb >= 4 * qt
                first = bi == 0
                last = bi == nblk - 1
                q0 = qt * 512 + c0
                q1 = (qt + 1) * 512

                def st0():
                    for f in pre:
                        f()
                    if first and eh == 0:
                        P.op("pe", lambda e: e.matmul(out=bank(bo), lhsT=zeros, rhs=QT.ap[:, 0:512], start=True, stop=False, skip_group_check=True),
                             reads=QT.k() + CK, writes=[pk[bo]])
                    P.op("pe", lambda e: e.matmul(out=bank(bz)[:, c0:512], lhsT=KT.ap[:, kb * 128:(kb + 1) * 128], rhs=QT.ap[:, q0:q1], start=True, stop=False, skip_group_check=True),
                         reads=KT.k() + QT.k(), writes=[pk[bz]])

                def st1():
                    P.op("act", lambda e: e.activation(out=et.ap[:, c0:512], in_=bank(bz)[:, c0:512], func=AF.Exp), reads=[pk[bz]], writes=et.k())
                    P.op("act", lambda e: e.activation(out=sp.ap[:, c0:512], in_=et.ap[:, c0:512], func=AF.Ln, bias=1.0), reads=et.k(), writes=sp.k())
                    if diag:
                        P.op("pool", lambda e: e.tensor_tensor(out=sp.ap[:, c0:c0 + 128], in0=sp.ap[:, c0:c0 + 128], in1=mstrict, op=ALU.mult), reads=sp.k() + CK, writes=sp.k())
                    if bi >= 1:
                        if c0 < c0p:
                            P.op("pool", lambda e: e.memset(sa.ap[:, c0:c0p], 0.0), writes=sa.k())
                        if bi == 1:
                            P.op("pool", lambda e: e.tensor_copy(out=sa.ap[:, c0p:512], in_=spp.ap[:, c0p:512]), reads=spp.k(), writes=sa.k())
                        else:
                            P.op("pool", lambda e: e.tensor_tensor(out=sa.ap[:, c0p:512], in0=sap.ap[:, c0p:512], in1=spp.ap[:, c0p:512], op=ALU.add), reads=sap.k() + spp.k(), writes=sa.k())

                def st2():
                    P.op("pe", lambda e: e.matmul(out=bank(bz)[:, c0:512], lhsT=negU, rhs=sp.ap[:, c0:512], start=False, stop=first, skip_group_check=True),
                         reads=sp.k() + CK, writes=[pk[bz]])
                    if not first:
                        P.op("pe", lambda e: e.matmul(out=bank(bz)[:, c0:512], lhsT=negOnes, rhs=sa.ap[:, c0:512], start=False, stop=True, skip_group_check=True),
                             reads=sa.k() + CK, writes=[pk[bz]])

                def st3():
                    P.op("act", lambda e: e.activation(out=At.ap[:, c0:512], in_=bank(bz)[:, c0:512], func=AF.Exp), reads=[pk[bz]], writes=At.k())
                    if diag:
                        P.op("pool", lambda e: e.tensor_tensor(out=At.ap[:, c0:c0 + 128], in0=At.ap[:, c0:c0 + 128], in1=mstrict, op=ALU.mult), reads=At.k() + CK, writes=At.k())

                def st4():
                    P.op("pe", lambda e: e.matmul(out=bank(bo)[:, c0:512], lhsT=V3[:, kb, :], rhs=At.ap[:, c0:512], start=False, stop=(last and eh == 1), skip_group_check=True),
                         reads=At.k() + V.k(), writes=[pk[bo]])
                    if last and eh == 1:
                        g = gcount["g"]
                        gcount["g"] += 1
                        ost = ost_b[g % 2]
                        P.op("dve", lambda e: e.tensor_copy(out=ost.ap, in_=bank(bo)), reads=[pk[bo]], writes=ost.k())
                        P.dma(lambda e: e.dma_start(out=sc_o[m][:, qt * 512:(qt + 1) * 512], in_=ost.ap), reads=ost.k(), writes=[DK("o", m, qt)])

                return [st0, st1, st2, st3, st4]

            load_pair(0)
            grp = 0
            for m in range(4):
                first_idx = len(items)
                for qt in range(4):
                    bo = 4 + (grp % 2)
                    grp += 1
                    for eh in range(2):
                        blocks = attn_blocks(qt)
                        for bi, (kb, c0) in enumerate(blocks):
                            pre = []
                            if m < 3 and len(items) == first_idx + 6:
                                pre.append(lambda m=m: load_pair(m + 1))
                            if conv_steps and len(items) % 12 == 3:
                                pre.append(conv_steps.pop(0))
                            c0p = blocks[bi - 1][1] if bi > 0 else 0
                            items.append(make_item(m, qt, eh, bi, kb, c0, c0p, len(blocks), bo, pre))
            run_pipeline(items, 5)

        def stage2_mla(s):
            items = []

            def load_head(h):
                QT, KT, V = QT_b[h % 2], KT_b[h % 2], V_b[h % 2]
                m, eh = h // 2, h % 2
                V3 = V.ap[:, 0:16 * 65].rearrange("p (kb n) -> p kb n", n=65)
                P.dma(lambda e: e.dma_start(out=QT.ap, in_=sc_qm[h]), reads=[DK("qm", h, c) for c in range(0, SEQ, 512)], writes=QT.k())
                P.dma(lambda e: e.dma_start(out=KT.ap[0:64, :], in_=sc_km[m][eh * 64:(eh + 1) * 64, :]), reads=[DK("km", m, c) for c in range(0, SEQ, 512)], writes=KT.k())
                P.dma(lambda e: e.dma_start(out=KT.ap[64:128, :], in_=sc_kr), reads=[DK("kr", c, r) for c in range(0, SEQ, 512) for r in range(2)], writes=KT.k())
                P.op("pool", lambda e: e.memset(V3[:, :, 64:65], 1.0), writes=V.k())
                P.dma(lambda e: e.dma_start(out=V3[:, :, 0:64], in_=sc_vm[:, h * 64:(h + 1) * 64].rearrange("(kb p) n -> p kb n", p=128)),
                      reads=[DK("vm", c) for c in range(0, SEQ, 512)], writes=V.k())

            def make_item(h, qt, bi, kb, c0, nblk, bo, g, pre):
                b = gcount["b"]
                gcount["b"] += 1
                QT, KT, V = QT_b[h % 2], KT_b[h % 2], V_b[h % 2]
                m, eh = h // 2, h % 2
                V3 = V.ap[:, 0:16 * 65].rearrange("p (kb n) -> p kb n", n=65)
                bz = b % 4
                At = At_b[b % 3]
                diag = kb >= 4 * qt
                first = bi == 0
                last = bi == nblk - 1
                q0 = qt * 512 + c0
                q1 = (qt + 1) * 512
                oa, rden, ost = oa_b[g % 2], rden_b[g % 2], ost_b[g % 2]

                def st0():
                    for f in pre:
                        f()
                    if first:
                        P.op("pe", lambda e: e.matmul(out=bank(bo)[0:65, :], lhsT=zeros[:, 0:65], rhs=QT.ap[:, 0:512], start=True, stop=False, skip_group_check=True),
                             reads=QT.k() + CK, writes=[pk[bo]])
                    P.op("pe", lambda e: e.matmul(out=bank(bz)[:, c0:512], lhsT=KT.ap[:, kb * 128:(kb + 1) * 128], rhs=QT.ap[:, q0:q1], start=True, stop=True),
                         reads=KT.k() + QT.k(), writes=[pk[bz]])

                def st1():
                    P.op("act", lambda e: e.activation(out=At.ap[:, c0:512], in_=bank(bz)[:, c0:512], func=AF.Exp, scale=SC_MLA), reads=[pk[bz]], writes=At.k())
                    if diag:
                        P.op("pool", lambda e: e.tensor_tensor(out=At.ap[:, c0:c0 + 128], in0=At.ap[:, c0:c0 + 128], in1=mle, op=ALU.mult), reads=At.k() + CK, writes=At.k())

                def st2():
                    P.op("pe", lambda e: e.matmul(out=bank(bo)[0:65, c0:512], lhsT=V3[:, kb, 0:65], rhs=At.ap[:, c0:512], start=False, stop=last, skip_group_check=True),
                         reads=At.k() + V.k(), writes=[pk[bo]])
                    if last:
                        P.op("act", lambda e: e.activation(out=oa.ap[0:65, :], in_=bank(bo)[0:65, :], func=AF.Copy), reads=[pk[bo]], writes=oa.k())

                def st3():
                    if last:
                        P.op("pe", lambda e: e.matmul(out=bank(6)[0:64, :], lhsT=esel[0:65, :], rhs=oa.ap[0:65, :], start=True, stop=True), reads=oa.k() + CK, writes=[pk[6]])

                def st4():
                    if last:
                        P.op("dve", lambda e: e.reciprocal(out=rden.ap[0:64, :], in_=bank(6)[0:64, :]), reads=[pk[6]], writes=rden.k())
                        P.op("dve", lambda e: e.tensor_tensor(out=ost.ap[0:64, :], in0=oa.ap[0:64, :], in1=rden.ap[0:64, :], op=ALU.mult), reads=oa.k() + rden.k(), writes=ost.k())
                        P.dma(lambda e: e.dma_start(out=sc_o[4 + m][eh * 64:(eh + 1) * 64, qt * 512:(qt + 1) * 512], in_=ost.ap[0:64, :]), reads=ost.k(), writes=[DK("o", 4 + m, qt, eh)])

                return [st0, st1, st2, st3, st4]

            load_head(0)
            for h in range(8):
                first_idx = len(items)
                for qt in range(4):
                    g = gcount["g"]
                    gcount["g"] += 1
                    bo = 4 + (g % 2)
                    blocks = attn_blocks(qt)
                    for bi, (kb, c0) in enumerate(blocks):
                        pre = []
                        if h < 7 and len(items) == first_idx + 6:
                            pre.append(lambda h=h: load_head(h + 1))
                        items.append(make_item(h, qt, bi, kb, c0, len(blocks), bo, g, pre))
            run_pipeline(items, 5)

        def norm_residual(src_ap, src_keys, gidx, xb, tA):
            ssc, ssk = newcol()
            P.op("act", lambda e: e.activation(out=junk_b.ap, in_=src_ap, func=AF.Square, accum_out=ssc), reads=src_keys, writes=junk_b.k() + [ssk])
            rc, rk = rstd_of(ssc, ssk, DM)
            P.op("dve", lambda e: e.scalar_tensor_tensor(out=tA.ap, in0=src_ap, scalar=rc, in1=bc_h[:, gidx, :], op0=ALU.mult, op1=ALU.mult), reads=src_keys + [rk] + CK, writes=tA.k())
            P.op("pool", lambda e: e.tensor_tensor(out=xb.ap, in0=xb.ap, in1=tA.ap, op=ALU.add), reads=xb.k() + tA.k(), writes=xb.k())

        def prenorm_T(xb, gidx, hb, i, tb_):
            ssc, ssk = newcol()
            P.op("act", lambda e: e.activation(out=junk_b.ap, in_=xb.ap, func=AF.Square, accum_out=ssc), reads=xb.k(), writes=junk_b.k() + [ssk])
            rc, rk = rstd_of(ssc, ssk, DM)
            P.op("dve", lambda e: e.scalar_tensor_tensor(out=hb.ap, in0=xb.ap, scalar=rc, in1=bc_h[:, gidx, :], op0=ALU.mult, op1=ALU.mult), reads=xb.k() + [rk] + CK, writes=hb.k())
            for k in range(8):
                P.op("pe", lambda e, k=k: e.transpose(out=bankbf(tb_)[:, k * 128:(k + 1) * 128], in_=hb.ap[:, k * 128:(k + 1) * 128], identity=ident), reads=hb.k() + CK, writes=[pk[tb_]])
            P.op("act", lambda e: e.activation(out=hT3_b.ap[:, :, i * 128:(i + 1) * 128], in_=bankbf(tb_).rearrange("p (k t) -> p k t", k=8), func=AF.Copy), reads=[pk[tb_]], writes=hT3_b.k())

        def wide(b0):
            return ps_h[:, b0 * 512:(b0 + 2) * 512], [pk[b0], pk[b0 + 1]]

        def stage3(s):
            tb = s * SEQ
            def prologue(j):
                cs = j * 512
                okeys = [DK("o", m, j) for m in range(4)] + [DK("o", 4 + m, j, eh) for m in range(4) for eh in range(2)]
                P.dma(lambda e, cs=cs: e.dma_start(out=OT_b.ap, in_=sc_o[:, :, cs:cs + 512].rearrange("c p t -> p c t")), reads=okeys, writes=OT_b.k())
                for g in range(2):
                    if g == 0:
                        P.op("act", lambda e: e.activation(out=sq_b.ap[:, 0:4, :], in_=OT_b.ap[:, 0:4, :], func=AF.Square), reads=OT_b.k(0, 2048), writes=sq_b.k(0, 2048))
                    else:
                        P.op("pool", lambda e: e.tensor_tensor(out=sq_b.ap[:, 4:8, :], in0=OT_b.ap[:, 4:8, :], in1=OT_b.ap[:, 4:8, :], op=ALU.mult), reads=OT_b.k(2048, 4096), writes=sq_b.k(2048, 4096))
                for g in range(2):
                    for c in range(4):
                        P.op("pe", lambda e, g=g, c=c: e.matmul(out=bank(6 + g), lhsT=gsum, rhs=sq_b.ap[:, g * 4 + c, :], start=(c == 0), stop=(c == 3)), reads=sq_b.k(g * 2048, (g + 1) * 2048) + CK, writes=[pk[6 + g]])
                    P.op("act", lambda e, g=g: e.activation(out=y_b[g].ap, in_=bank(6 + g), func=AF.Sqrt, bias=EPS), reads=[pk[6 + g]], writes=y_b[g].k())
                    P.op("dve", lambda e, g=g: e.reciprocal(out=y_b[g].ap, in_=y_b[g].ap), reads=y_b[g].k(), writes=y_b[g].k())
                for c in range(8):
                    P.op("dve", lambda e, c=c: e.scalar_tensor_tensor(out=OT_b.ap[:, c, :], in0=OT_b.ap[:, c, :], scalar=vec_h[:, V_GGRP + c:V_GGRP + c + 1], in1=y_b[c // 4].ap, op0=ALU.mult, op1=ALU.mult),
                         reads=OT_b.k(c * 512, (c + 1) * 512) + y_b[c // 4].k() + CK, writes=OT_b.k(c * 512, (c + 1) * 512))

            for j in range(4):
                cs = j * 512
                if j == 0:
                    prologue(0)
                for i in range(4):
                    P.dma(lambda e, i=i, cs=cs: e.dma_start(out=x1_b[i].ap, in_=x_c[tb + cs + i * 128:tb + cs + (i + 1) * 128, :]), writes=x1_b[i].k())
                for c in range(NCH):
                    P.dma(lambda e, c=c: e.dma_start(out=wd_b[c].ap, in_=sc_wd[c * 128:(c + 1) * 128, :]), reads=[DK("wd", c)], writes=wd_b[c].k())
                for i in range(4):
                    t0 = tb + cs + i * 128
                    xb = x1_b[i]
                    b0 = (i % 2) * 2
                    for half in range(2):
                        for c in range(8):
                            P.op("pe", lambda e, half=half, c=c, i=i, b0=b0: e.matmul(out=bank(b0 + half), lhsT=OT_b.ap[:, c, i * 128:(i + 1) * 128], rhs=wo_h[:, c, half * 512:(half + 1) * 512], start=(c == 0), stop=(c == 7)),
                                 reads=OT_b.k(c * 512, (c + 1) * 512) + wo_b.keys, writes=[pk[b0 + half]])
                    wa, wk = wide(b0)
                    norm_residual(wa, wk, GPOM, xb, tmpA2[i % 2])
                for i in range(4):
                    prenorm_T(x1_b[i], GPF, hb3_b[i % 2], i, 6 + (i % 2))
                for c in range(NCH):
                    wu = wu_b[c % 3]
                    P.dma(lambda e, wu=wu, c=c: e.dma_start(out=wu.ap.rearrange("p a k n -> p (a k n)"), in_=sc_wu[c]), reads=[DK("wu", c)], writes=wu.k())
                    ys = []
                    for gv in range(2):
                        b = 4 + (rr["pb"] % 4)
                        rr["pb"] += 1
                        for k in range(8):
                            P.op("pe", lambda e, b=b, k=k, gv=gv, wu=wu: e.matmul(out=bank(b), lhsT=wu.ap[:, gv, k, :], rhs=hT3_b.ap[:, k, :], start=(k == 0), stop=(k == 7)), reads=wu.k() + hT3_b.k(), writes=[pk[b]])
                        cc = gv * NCH + c
                        y = y_b[(c % 2) * 2 + gv]
                        w0 = vec_h[:, V_CW + cc * 3:V_CW + cc * 3 + 1]
                        w1 = vec_h[:, V_CW + cc * 3 + 1:V_CW + cc * 3 + 2]
                        w2 = vec_h[:, V_CW + cc * 3 + 2:V_CW + cc * 3 + 3]
                        bb = vec_h[:, V_CB + cc:V_CB + cc + 1]
                        P.op("act", lambda e, b=b, y=y, w2=w2, bb=bb: e.activation(out=y.ap, in_=bank(b), func=AF.Identity, scale=w2, bias=bb), reads=[pk[b]] + CK, writes=y.k())
                        P.op("dve", lambda e, b=b, y=y, w1=w1: e.scalar_tensor_tensor(out=y.ap[:, 1:512], in0=bank(b)[:, 0:511], scalar=w1, in1=y.ap[:, 1:512], op0=ALU.mult, op1=ALU.add), reads=[pk[b]] + y.k() + CK, writes=y.k())
                        P.op("dve", lambda e, b=b, y=y, w0=w0: e.scalar_tensor_tensor(out=y.ap[:, 2:512], in0=bank(b)[:, 0:510], scalar=w0, in1=y.ap[:, 2:512], op0=ALU.mult, op1=ALU.add), reads=[pk[b]] + y.k() + CK, writes=y.k())
                        if j > 0:
                            P.op("dve", lambda e, y=y, w0=w0, cc=cc: e.scalar_tensor_tensor(out=y.ap[:, 0:2], in0=halo_h[:, cc, :], scalar=w0, in1=y.ap[:, 0:2], op0=ALU.mult, op1=ALU.add), reads=[halo_keys[cc]] + y.k() + CK, writes=y.k())
                            P.op("dve", lambda e, y=y, w1=w1, cc=cc: e.scalar_tensor_tensor(out=y.ap[:, 0:1], in0=halo_h[:, cc, 1:2], scalar=w1, in1=y.ap[:, 0:1], op0=ALU.mult, op1=ALU.add), reads=[halo_keys[cc]] + y.k() + CK, writes=y.k())
                        P.op("dve", lambda e, b=b, cc=cc: e.tensor_copy(out=halo_h[:, cc, :], in_=bank(b)[:, 510:512]), reads=[pk[b]], writes=[halo_keys[cc]])
                        ys.append(y)
                    gg = gg_b[c % 2]
                    P.op("act", lambda e, gg=gg, y=ys[0]: e.activation(out=gg.ap, in_=y.ap, func=AF.Gelu_apprx_tanh), reads=ys[0].k(), writes=gg.k())
                    P.op("pool", lambda e, gg=gg, y=ys[1], c=c: e.tensor_tensor(out=fin_b.ap[:, c, :], in0=gg.ap, in1=y.ap, op=ALU.mult), reads=gg.k() + ys[1].k(), writes=fin_b.k(c * 512, (c + 1) * 512))
                for pp in range(2):
                    for c in range(NCH):
                        wd = wd_b[c]
                        for il in range(2):
                            i = pp * 2 + il
                            for half in range(2):
                                b = pp * 4 + il * 2 + half
                                P.op("pe", lambda e, b=b, c=c, i=i, half=half, wd=wd: e.matmul(out=bank(b), lhsT=fin_b.ap[:, c, i * 128:(i + 1) * 128], rhs=wd.ap[:, half * 512:(half + 1) * 512], start=(c == 0), stop=(c == NCH - 1)),
                                     reads=fin_b.k(c * 512, (c + 1) * 512) + wd.k(), writes=[pk[b]])
                for i in range(4):
                    wa, wk = wide(i * 2)
                    norm_residual(wa, wk, GPOF, x1_b[i], tmpA2[i % 2])
                for i in range(4):
                    prenorm_T(x1_b[i], GPG, hb3_b[i % 2], i, i % 2)
                if j < 3:
                    prologue(j + 1)
                P.dma(lambda e, cs=cs: e.dma_start(out=pb_b.ap, in_=p_c[tb + cs:tb + cs + 512, :].rearrange("(i p) n -> p i n", p=128)), writes=pb_b.k(), queue="pool")
                for i in range(4):
                    tbk = 2 + (i % 2)
                    for k in range(2):
                        P.op("pe", lambda e, i=i, k=k, tbk=tbk: e.transpose(out=bankbf(tbk)[:, k * 128:(k + 1) * 128], in_=pb_b.ap[:, i, k * 128:(k + 1) * 128], identity=ident), reads=pb_b.k() + CK, writes=[pk[tbk]])
                    P.op("act", lambda e, i=i, tbk=tbk: e.activation(out=pT_b.ap[:, :, i * 128:(i + 1) * 128], in_=bankbf(tbk)[:, 0:256].rearrange("p (k t) -> p k t", k=2), func=AF.Copy), reads=[pk[tbk]], writes=pT_b.k())
                for i in range(4):
                    t0 = tb + cs + i * 128
                    xb = x1_b[i]
                    bs = (i % 2) * 4
                    tA = tmpA2[i % 2]
                    for half in range(2):
                        for k in range(8):
                            P.op("pe", lambda e, half=half, k=k, i=i, bs=bs: e.matmul(out=bank(bs + half), lhsT=hT3_b.ap[:, k, i * 128:(i + 1) * 128], rhs=wpg_h[:, k, half * 512:(half + 1) * 512], start=(k == 0), stop=(k == 7)),
                                 reads=hT3_b.k() + wpg_b.keys, writes=[pk[bs + half]])
                        for k in range(2):
                            P.op("pe", lambda e, half=half, k=k, i=i, bs=bs: e.matmul(out=bank(bs + 2 + half), lhsT=pT_b.ap[:, k, i * 128:(i + 1) * 128], rhs=wple_h[:, k, half * 512:(half + 1) * 512], start=(k == 0), stop=(k == 1)),
                                 reads=pT_b.k() + wple_b.keys, writes=[pk[bs + 2 + half]])
                    gla, glk = wide(bs)
                    ea, ek = wide(bs + 2)
                    P.op("dve", lambda e, gla=gla, tA=tA: e.tensor_tensor(out=tA.ap, in0=gla, in1=bc_h[:, BPG, :], op=ALU.add), reads=glk + CK, writes=tA.k())
                    P.op("act", lambda e, tA=tA: e.activation(out=tA.ap, in_=tA.ap, func=AF.Sigmoid), reads=tA.k(), writes=tA.k())
                    P.op("dve", lambda e, ea=ea, tA=tA: e.tensor_tensor(out=tA.ap, in0=ea, in1=tA.ap, op=ALU.mult), reads=ek + tA.k(), writes=tA.k())
                    ssc, ssk = newcol()
                    P.op("act", lambda e, ssc=ssc, tA=tA: e.activation(out=junk_b.ap, in_=tA.ap, func=AF.Square, accum_out=ssc), reads=tA.k(), writes=junk_b.k() + [ssk])
                    rc, rk = rstd_of(ssc, ssk, DM)
                    P.op("dve", lambda e, rc=rc, tA=tA: e.scalar_tensor_tensor(out=tA.ap, in0=tA.ap, scalar=rc, in1=bc_h[:, GPOP, :], op0=ALU.mult, op1=ALU.mult), reads=tA.k() + [rk] + CK, writes=tA.k())
                    P.op("pool", lambda e, xb=xb, tA=tA: e.tensor_tensor(out=xb.ap, in0=xb.ap, in1=tA.ap, op=ALU.add), reads=xb.k() + tA.k(), writes=xb.k())
                    P.dma(lambda e, xb=xb, t0=t0: e.dma_start(out=out_c[t0:t0 + 128, :], in_=xb.ap), reads=xb.k())

        for s in range(nseq):
            if upto >= 1:
                stage1(s)
            if upto >= 2:
                stage2_sb(s)
            if upto >= 3:
                stage2_mla(s)
            if upto >= 4:
                stage3(s)
        P.emit()
    return nc


def _consts():
    j = np.arange(128)[:, None]
    t = np.arange(128)[None, :]
    cb = np.zeros((128, 896), np.float32)
    cb[:, 0:128] = np.eye(128)
    cb[:, 128:256] = (j < t)
    cb[:, 256:384] = (j <= t)
    cb[:, 384:512] = -1.0 * (j >= t)
    cb[:, 512:640] = -1.0
    cb[:, 640:768] = 1.0 / 512
    cf = np.zeros((128, 448), np.float32)
    cf[:, 0:128] = np.eye(128)
    inv = (1.0 / (np.float32(10000.0) ** (np.arange(0, 32, 2, dtype=np.float32) / np.float32(32)))).astype(np.float32)
    cf[:, 128:384] = np.tile(inv, 16)[None, :]
    cf[64, 384:448] = 1.0
    return cb, cf


def prep_shared(w_in, g_pre_mix, g_q_lat, w_q_up, g_kv_lat, w_kv_up, g_grp_sb, g_grp_mla, w_o, g_post_mix,
                g_pre_ffn, w_up, conv_w, conv_b, w_down, g_post_ffn, w_ple, g_ple_gate, w_ple_gate, b_ple_gate, g_post_ple):
    f = np.float32
    c = np.ascontiguousarray
    wq = w_q_up[0].reshape(256, 8, 96)
    wqx = np.concatenate([wq, wq[:, :, 80:96], wq[:, :, 64:80]], axis=2).reshape(256, 1024)
    wkv = w_kv_up[0].reshape(128, 8, 128)
    wkvx = np.concatenate([wkv[:, :, 0:64].reshape(128, 512), wkv[:, :, 64:128].reshape(128, 512)], axis=1)
    wu = w_up[0].reshape(8, 128, 2, NCH, 128)
    wupr = c(wu.transpose(3, 1, 2, 0, 4)).reshape(NCH, 128, 2048)
    vec = np.zeros((128, 192), f)
    vec[:, 0:2] = g_q_lat[0].reshape(2, 128).T
    vec[:, 2:3] = g_kv_lat[0].reshape(1, 128).T
    vec[:, 3:11] = np.concatenate([g_grp_sb[0], g_grp_mla[0]]).reshape(8, 128).T
    cw = conv_w[0].reshape(3, 44, 128)
    vec[:, 11:143] = cw.transpose(2, 1, 0).reshape(128, 132)
    vec[:, 143:187] = conv_b[0].reshape(44, 128).T
    bcv = np.stack([g_pre_mix[0], g_post_mix[0], g_pre_ffn[0], g_post_ffn[0], g_ple_gate[0], g_post_ple[0], b_ple_gate[0]], 0)
    bcv = c(np.broadcast_to(bcv.reshape(1, 7 * 1024), (128, 7 * 1024))).astype(f)
    cb, cf = _consts()
    return {
        "w_in": c(w_in[0]), "w_qx": c(wqx), "w_kvx": c(wkvx), "w_o": c(w_o[0]), "w_pg": c(w_ple_gate[0]),
        "w_ple": c(w_ple[0]), "w_upr": wupr, "w_dn": c(w_down[0]), "vecs": vec, "bcv": bcv, "cf32": cf, "cbf": cb,
    }


def prep_core(x, p, positions, b0, nseq):
    xc = np.ascontiguousarray(x[b0:b0 + nseq].reshape(nseq * SEQ, DM))
    pc = np.ascontiguousarray(p[0, b0:b0 + nseq].reshape(nseq * SEQ, 256))
    pos = positions[b0:b0 + nseq].reshape(nseq, 16, 128).transpose(0, 2, 1)
    posr = np.ascontiguousarray(np.repeat(pos[:, :, :, None], 16, axis=3).reshape(nseq, 128, 256)).astype(np.int32)
    return {"x_c": xc, "p_c": pc, "posr": posr}


_NC_CACHE = {}


def kernel(x, p, positions, **w):
    x = np.asarray(x)
    p = np.asarray(p)
    positions = np.asarray(positions)
    w = {k: np.asarray(v) for k, v in w.items()}
    nseq = x.shape[0] // NCORES
    shared = prep_shared(**w)
    if nseq not in _NC_CACHE:
        _NC_CACHE[nseq] = build_program(nseq)
    nc = _NC_CACHE[nseq]
    in_maps = []
    for c in range(NCORES):
        m = dict(shared)
        m.update(prep_core(x, p, positions, c * nseq, nseq))
        in_maps.append(m)
    res = run_bass_kernel_spmd(nc, in_maps, core_ids=list(range(NCORES)))
    out = np.concatenate([r["out_c"].reshape(nseq, SEQ, DM) for r in res.results], axis=0)
    return out.astype(np.float32)
```

```python
import contextlib
import math
import numpy as np
import concourse.bass as bass
import concourse.mybir as mybir
from concourse.bass_utils import run_bass_kernel_spmd

F32 = mybir.dt.float32
BF16 = mybir.dt.bfloat16
I32 = mybir.dt.int32
ALU = mybir.AluOpType
AF = mybir.ActivationFunctionType

NCORES = 8
SEQ = 2048
DM = 1024
DFF = 2816
NCH = 22
EPS = 1e-6
SC_MLA = 96 ** -0.5
ARENA = 68000
CELL = 256
CUT = 99
SKIP = set()


class Key:
    __slots__ = ("name", "w", "r")

    def __init__(self, name):
        self.name = name
        self.w = None
        self.r = {}


class Op:
    __slots__ = ("eng", "fn", "deps", "signal", "sigval", "lane", "laneval", "is_dma")

    def __init__(self, eng, fn, is_dma=False):
        self.eng = eng
        self.fn = fn
        self.deps = []
        self.signal = False
        self.sigval = None
        self.lane = None
        self.laneval = None
        self.is_dma = is_dma


class Prog:
    ENGS = ("pe", "act", "dve", "pool", "sp")

    def __init__(self, nc, n_lanes_sp=24, n_lanes_pool=8):
        self.nc = nc
        self.q = {e: [] for e in self.ENGS}
        self.n_lanes = {"sp": n_lanes_sp, "pool": n_lanes_pool}
        self.lane_rr = {"sp": 0, "pool": 0}
        self.lane_cnt = {"sp": [0] * n_lanes_sp, "pool": [0] * n_lanes_pool}
        self.lane_last = {"sp": [None] * n_lanes_sp, "pool": [None] * n_lanes_pool}

    def key(self, name):
        return Key(name)

    def _add(self, op, reads, writes):
        ex = [k for k in reads if k.name.startswith("pk")]
        if ex:
            reads = [k for k in reads if not k.name.startswith("pk")]
            writes = list(writes) + ex
        deps = op.deps
        for k in reads:
            if k.w is not None:
                deps.append(k.w)
        for k in writes:
            if k.w is not None:
                deps.append(k.w)
            for o in k.r.values():
                deps.append(o)
        rk = ("dma", id(op)) if op.is_dma else op.eng
        for k in reads:
            k.r[rk] = op
        for k in writes:
            k.w = op
            k.r = {}
        self.q[op.eng].append(op)
        return op

    def op(self, eng, fn, reads=(), writes=()):
        return self._add(Op(eng, fn), reads, writes)

    def dma(self, fn, reads=(), writes=(), queue="sp"):
        op = Op(queue, fn, is_dma=True)
        rr = self.lane_rr[queue]
        self.lane_rr[queue] = (rr + 1) % self.n_lanes[queue]
        op.lane = (queue, rr)
        self.lane_cnt[queue][rr] += 1
        op.laneval = 16 * self.lane_cnt[queue][rr]
        prev = self.lane_last[queue][rr]
        if prev is not None:
            op.deps.append(prev)
        self.lane_last[queue][rr] = op
        return self._add(op, reads, writes)

    def emit(self):
        nc = self.nc
        for e in self.ENGS:
            for op in self.q[e]:
                for d in op.deps:
                    if d.is_dma:
                        continue
                    if d.eng == "pe" and op.eng == "pe" and not op.is_dma:
                        continue
                    d.signal = True
        for e in self.ENGS:
            c = 0
            for op in self.q[e]:
                if op.signal and not op.is_dma:
                    c += 1
                    op.sigval = c
        with contextlib.ExitStack() as es:
            sems = {e: es.enter_context(nc.semaphore(f"s_{e}")) for e in ("pe", "act", "dve", "pool")}
            lanes = {}
            for qn in ("sp", "pool"):
                for i in range(self.n_lanes[qn]):
                    lanes[(qn, i)] = es.enter_context(nc.semaphore(f"l_{qn}{i}"))
            block = es.enter_context(nc.Block())

            def run(ename, eng):
                waited = {}
                for op in self.q[ename]:
                    need = {}
                    for d in op.deps:
                        if d.is_dma:
                            s, v = lanes[d.lane], d.laneval
                        else:
                            if d.eng == "pe" and ename == "pe" and not op.is_dma:
                                continue
                            s, v = sems[d.eng], d.sigval
                        kk = id(s)
                        if need.get(kk, (None, 0))[1] < v:
                            need[kk] = (s, v)
                    for kk, (s, v) in need.items():
                        if waited.get(kk, 0) < v:
                            eng.wait_ge(s, v)
                            waited[kk] = v
                    ins = op.fn(eng)
                    if op.is_dma:
                        ins.then_inc(lanes[op.lane], 16)
                    elif op.signal:
                        ins.then_inc(sems[ename], 1)
                if ename == "sp":
                    for qn in ("sp", "pool"):
                        for i in range(self.n_lanes[qn]):
                            c = self.lane_cnt[qn][i]
                            if c:
                                eng.wait_ge(lanes[(qn, i)], 16 * c)

            @block.tensor
            def _(e):
                run("pe", e)

            @block.scalar
            def _(e):
                run("act", e)

            @block.vector
            def _(e):
                run("dve", e)

            @block.gpsimd
            def _(e):
                run("pool", e)

            @block.sync
            def _(e):
                run("sp", e)


class Buf:
    def __init__(self, ap, keys, off=0, w=1):
        self.ap = ap
        self.keys = keys
        self.off = off
        self.w = w

    def k(self, lo=None, hi=None):
        if lo is None:
            return self.keys
        a = (self.off + lo * self.w) // CELL
        b = (self.off + hi * self.w - 1) // CELL
        base = self.off // CELL
        return self.keys[a - base:b - base + 1]


def build_program(nseq, debug=False, upto=4):
    nc = bass.Bass("TRN2", target_bir_lowering=False)
    ntok = nseq * SEQ
    dkind = "ExternalOutput" if debug else "Internal"

    def din(name, shape, dt=F32):
        return nc.dram_tensor(name, list(shape), dt, kind="ExternalInput").ap()

    x_c = din("x_c", [ntok, DM])
    p_c = din("p_c", [ntok, 256])
    posr = din("posr", [nseq, 128, 256], I32)
    w_in = din("w_in", [DM, 1952])
    w_qx = din("w_qx", [256, 1024])
    w_kvx = din("w_kvx", [128, 1024])
    w_o = din("w_o", [DM, DM])
    w_pg = din("w_pg", [DM, DM])
    w_ple = din("w_ple", [256, DM])
    w_upr = din("w_upr", [NCH, 128, 2048])
    w_dn = din("w_dn", [DFF, DM])
    vecs = din("vecs", [128, 192])
    bcv = din("bcv", [128, 7 * 1024])
    cf32 = din("cf32", [128, 448])
    cbf = din("cbf", [128, 896])
    out_c = nc.dram_tensor("out_c", [ntok, DM], F32, kind="ExternalOutput").ap()

    def dsc(name, shape):
        return nc.dram_tensor(name, list(shape), BF16, kind=dkind).ap()

    sc_qs = dsc("sc_qs", [4, 128, SEQ])
    sc_ks = dsc("sc_ks", [4, 128, SEQ])
    sc_vs = dsc("sc_vs", [SEQ, 512])
    sc_qm = dsc("sc_qm", [8, 128, SEQ])
    sc_km = dsc("sc_km", [4, 128, SEQ])
    sc_kr = dsc("sc_kr", [64, SEQ])
    sc_vm = dsc("sc_vm", [SEQ, 512])
    sc_o = dsc("sc_o", [8, 128, SEQ])
    sc_wu = nc.dram_tensor("sc_wu", [NCH, 128, 2048], BF16, kind="Internal").ap()
    sc_wd = nc.dram_tensor("sc_wd", [DFF, DM], BF16, kind="Internal").ap()

    es = contextlib.ExitStack()
    with es:
        def sb(name, shape, dt):
            return es.enter_context(nc.sbuf_tensor(name, list(shape), dt))

        P = Prog(nc)
        arena_h = sb("arena", [128, ARENA], BF16)
        akeys = [Key(f"a{i}") for i in range((ARENA + CELL - 1) // CELL)]

        def abuf(off, n, dt=BF16, shape=None):
            w = 1 if dt == BF16 else 2
            assert off % 2 == 0 and off + n * w <= ARENA, (off, n, w)
            ap = arena_h[:, off:off + n * w]
            if dt != BF16:
                ap = ap.bitcast(dt)
            if shape is not None:
                names = " ".join(f"d{i}" for i in range(len(shape)))
                ap = ap.rearrange(f"p ({names}) -> p {names}", **{f"d{i}": shape[i] for i in range(len(shape))})
            a = off // CELL
            b = (off + n * w - 1) // CELL
            return Buf(ap, akeys[a:b + 1], off, w)

        def rbuf(name, shape, dt, nkeys=1):
            h = sb(name, shape, dt)
            return Buf(h, [Key(name)]), h

        wo_b, wo_h = rbuf("wo", [128, 8, 1024], BF16)
        wpg_b, wpg_h = rbuf("wpg", [128, 8, 1024], BF16)
        wple_b, wple_h = rbuf("wple", [128, 2, 1024], BF16)
        bc_b, bc_h = rbuf("bc", [128, 7, 1024], F32)
        cb_b, cb_h = rbuf("cb", [128, 896], BF16)
        cf_b, cf_h = rbuf("cf", [128, 448], F32)
        vec_b, vec_h = rbuf("vec", [128, 192], F32)
        stat_h = sb("stat", [128, 64], F32)
        stat_keys = [Key(f"st{i}") for i in range(64)]
        halo_h = sb("halo", [128, 44, 2], F32)
        halo_keys = [Key(f"halo{i}") for i in range(44)]
        ps_h = es.enter_context(nc.psum_tensor("ps", [128, 4096], F32))
        pk = [Key(f"pk{i}") for i in range(8)]

        def bank(b, n=512, lo=0):
            return ps_h[:, b * 512 + lo:b * 512 + lo + n]

        def bankbf(b):
            return ps_h[:, b * 512:(b + 1) * 512].bitcast(BF16)

        ident = cb_h[:, 0:128]
        mstrict = cb_h[:, 128:256]
        mle = cb_h[:, 256:384]
        negU = cb_h[:, 384:512]
        negOnes = cb_h[:, 512:640]
        gsum = cb_h[:, 640:768]
        zeros = cb_h[:, 768:896]
        ident32 = cf_h[:, 0:128]
        invf = cf_h[:, 128:384]
        esel = cf_h[:, 384:448]
        GPM, GPOM, GPF, GPOF, GPG, GPOP, BPG = range(7)
        V_GQ, V_GKV, V_GGRP, V_CW, V_CB = 0, 2, 3, 11, 143
        CK = [cb_b.keys[0], cf_b.keys[0], vec_b.keys[0], bc_b.keys[0]]

        stat_i = [0]

        def newcol():
            i = stat_i[0] % 64
            stat_i[0] += 1
            return stat_h[:, i:i + 1], stat_keys[i]

        def rstd_of(ss_ap, ss_key, n):
            c1, k1 = newcol()
            P.op("act", lambda e: e.activation(out=c1, in_=ss_ap, func=AF.Sqrt, scale=1.0 / n, bias=EPS), reads=[ss_key], writes=[k1])
            c2, k2 = newcol()
            P.op("dve", lambda e: e.reciprocal(out=c2, in_=c1), reads=[k1], writes=[k2])
            return c2, k2

        P.dma(lambda e: e.dma_start(out=cf_h[:], in_=cf32), writes=cf_b.keys)
        P.dma(lambda e: e.dma_start(out=vec_h[:], in_=vecs), writes=vec_b.keys)
        P.dma(lambda e: e.dma_start(out=bc_h[:].rearrange("p a b -> p (a b)"), in_=bcv), writes=bc_b.keys)
        P.dma(lambda e: e.dma_start(out=cb_h[:], in_=cbf), writes=cb_b.keys, queue="pool")
        def load_resident_weights():
            for k in range(8):
                P.dma(lambda e, k=k: e.dma_start(out=wo_h[:, k, :], in_=w_o[k * 128:(k + 1) * 128, :]), writes=wo_b.keys, queue="pool")
                P.dma(lambda e, k=k: e.dma_start(out=wpg_h[:, k, :], in_=w_pg[k * 128:(k + 1) * 128, :]), writes=wpg_b.keys, queue="pool")
            for k in range(2):
                P.dma(lambda e, k=k: e.dma_start(out=wple_h[:, k, :], in_=w_ple[k * 128:(k + 1) * 128, :]), writes=wple_b.keys, queue="pool")

        win_b = [abuf(k * 2048, 1952) for k in range(8)]
        wq_b = abuf(16384, 2048, shape=[2, 1024])
        wkv_b = abuf(18432, 1024)
        A1 = 19456
        xs_b = [abuf(A1 + i * 2048, 1024, F32) for i in range(2)]
        hb1_b = [abuf(A1 + 4096 + i * 1024, 1024) for i in range(2)] + [abuf(56832 + i * 1024, 1024) for i in range(2)] + [abuf(60416 + i * 1024, 1024) for i in range(4)]
        hT1_b = [abuf(A1 + 6144 + i * 4096, 4096, shape=[8, 512]) for i in range(2)]
        stg_b = [abuf(A1 + 14336 + i * 512, 512) for i in range(6)]
        vst_b = [abuf(A1 + 17408 + i * 2048, 2048, shape=[4, 512]) for i in range(2)]
        L0 = A1 + 21504
        qln_b = [abuf(L0 + 7168 + i * 256, 256) for i in range(4)]
        kvln_b = [abuf(L0 + 8192 + i * 128, 128) for i in range(4)]
        krt_b = [abuf(L0 + 8704 + i * 32, 32) for i in range(4)]
        ropeA_b = [abuf(L0 + i * 64, 32, F32) for i in range(4)]
        ropeB_b = [abuf(L0 + 256 + i * 64, 32, F32) for i in range(4)]
        qlT_b = abuf(L0 + 1024, 1024, shape=[2, 512])
        kvT_b = abuf(L0 + 2048, 512)
        krT_b = abuf(L0 + 2560, 512)
        TtT_b = abuf(L0 + 3072, 512, F32)
        TQ_b = [abuf(L0 + 4096 + i * 1024, 512, F32, shape=[4, 128]) for i in range(2)]
        cos_b = abuf(L0 + 6144, 256, F32)
        sin_b = abuf(L0 + 6656, 256, F32)
        rs_b = [abuf(L0 + 7168 + i * 512, 256, F32) for i in range(4)]
        rsi_b = abuf(L0 + 9216, 256, I32)
        assert L0 + 9728 <= ARENA
        A2 = A1
        QT_b = [abuf(A2 + i * 2048, 2048) for i in range(2)]
        KT_b = [abuf(A2 + 4096 + i * 2048, 2048) for i in range(2)]
        V_b = [abuf(A2 + 8192 + i * 2048, 2048) for i in range(2)]
        et_b = [abuf(A2 + 12288 + i * 1024, 512, F32) for i in range(2)]
        sp_b = [abuf(A2 + 14336 + i * 512, 512) for i in range(3)]
        At_b = [abuf(A2 + 15872 + i * 512, 512) for i in range(3)]
        sacc_b = [abuf(A2 + 23040 + i * 512, 512) for i in range(3)]
        KTz_b = [[KT_b[i], abuf(A2 + 24576 + i * 2048, 2048)] for i in range(2)]
        Vz_b = [[V_b[i], abuf(A2 + 28672 + i * 2048, 2048)] for i in range(2)]
        assert A2 + 32768 <= 59392
        oa_b = [abuf(A2 + 17920 + i * 1024, 512, F32) for i in range(2)]
        rden_b = [abuf(A2 + 19968 + i * 1024, 512, F32) for i in range(2)]
        ost_b = [abuf(A2 + 22016 + i * 512, 512) for i in range(2)]
        wd_b = [abuf(c * 1024, 1024) for c in range(NCH)]
        x1_b = [abuf(22528 + i * 2048, 1024, F32) for i in range(4)]
        hb3_b = [abuf(30720 + i * 1024, 1024) for i in range(2)]
        hT3_b = abuf(32768, 4096, shape=[8, 512])
        fin_b = abuf(36864, NCH * 512, shape=[NCH, 512])
        sq_b = abuf(36864, 4096, shape=[8, 512])
        OT_b = abuf(48128, 4096, shape=[8, 512])
        y_b = [abuf(52224 + i * 1024, 512, F32) for i in range(4)]
        gg_b = [abuf(56320 + i * 512, 512) for i in range(2)]
        tmpA_b = abuf(57344, 1024, F32)
        junk_b = abuf(59392, 1024)
        pb_b = abuf(60416, 1024, shape=[4, 256])
        pT_b = abuf(61440, 1024, shape=[2, 512])
        wu_b = [abuf(60416 + i * 2048, 2048, shape=[2, 8, 128]) for i in range(3)]
        tmpA2 = [tmpA_b, abuf(52224 + 2048, 1024, F32)]
        cst_b = [abuf(52224 + i * 2048, 2048) for i in range(2)]
        assert 66560 <= ARENA

        dk = {}

        def DK(*a):
            if a not in dk:
                dk[a] = Key(str(a))
            return dk[a]

        rr = {"pb": 0}
        def _conv_up(c):
            sl = cst_b[c % 2]
            P.dma(lambda e: e.dma_start(out=sl.ap, in_=w_upr[c]), writes=sl.k(), queue="pool")
            P.dma(lambda e: e.dma_start(out=sc_wu[c], in_=sl.ap), reads=sl.k(), writes=[DK("wu", c)])

        def _conv_dn(c):
            sl = cst_b[(NCH + c) % 2]
            P.dma(lambda e: e.dma_start(out=sl.ap[:, 0:1024], in_=w_dn[c * 128:(c + 1) * 128, :]), writes=sl.k(), queue="pool")
            P.dma(lambda e: e.dma_start(out=sc_wd[c * 128:(c + 1) * 128, :], in_=sl.ap[:, 0:1024]), reads=sl.k(), writes=[DK("wd", c)])

        def nextbank(banks):
            b = banks[rr["pb"] % len(banks)]
            rr["pb"] += 1
            return b

        conv_steps = []

        def stage1(s):
            tb = s * SEQ
            for k in range(8):
                P.dma(lambda e, k=k: e.dma_start(out=win_b[k].ap, in_=w_in[k * 128:(k + 1) * 128, :]), writes=win_b[k].k(), queue="pool")
            for k in range(2):
                P.dma(lambda e, k=k: e.dma_start(out=wq_b.ap[:, k, :], in_=w_qx[k * 128:(k + 1) * 128, :]), writes=wq_b.k(k * 1024, (k + 1) * 1024), queue="pool")
            P.dma(lambda e: e.dma_start(out=wkv_b.ap, in_=w_kvx), writes=wkv_b.k(), queue="pool")
            wq4 = wq_b.ap.rearrange("p k (h c) -> p k h c", h=8)
            for k in range(2):
                P.op("pool", lambda e, k=k: e.tensor_scalar(out=wq4[:, k, :, 96:112], in0=wq4[:, k, :, 96:112], scalar1=-1.0, scalar2=0.0, op0=ALU.mult, op1=ALU.add),
                     reads=wq_b.k(k * 1024, (k + 1) * 1024), writes=wq_b.k(k * 1024, (k + 1) * 1024))
            P.dma(lambda e: e.dma_start(out=rsi_b.ap, in_=posr[s]), writes=rsi_b.k())
            P.op("dve", lambda e: e.tensor_copy(out=rs_b[0].ap, in_=rsi_b.ap), reads=rsi_b.k(), writes=rs_b[0].k())
            P.op("dve", lambda e: e.tensor_tensor(out=rs_b[0].ap, in0=rs_b[0].ap, in1=invf, op=ALU.mult), reads=rs_b[0].k() + CK, writes=rs_b[0].k())
            for which, dst in ((0, sin_b), (1, cos_b)):
                P.op("dve", lambda e, which=which: e.tensor_scalar(out=rs_b[1].ap, in0=rs_b[0].ap, scalar1=1.0 / (2 * math.pi), scalar2=0.25 * which, op0=ALU.mult, op1=ALU.add),
                     reads=rs_b[0].k(), writes=rs_b[1].k())
                P.op("dve", lambda e: e.tensor_copy(out=rsi_b.ap, in_=rs_b[1].ap), reads=rs_b[1].k(), writes=rsi_b.k())
                P.op("dve", lambda e: e.tensor_copy(out=rs_b[2].ap, in_=rsi_b.ap), reads=rsi_b.k(), writes=rs_b[2].k())
                P.op("dve", lambda e: e.tensor_tensor(out=rs_b[3].ap, in0=rs_b[1].ap, in1=rs_b[2].ap, op=ALU.subtract), reads=rs_b[1].k() + rs_b[2].k(), writes=rs_b[3].k())
                P.op("act", lambda e, dst=dst: e.activation(out=dst.ap, in_=rs_b[3].ap, func=AF.Sin, scale=6.28318), reads=rs_b[3].k(), writes=dst.k())
            cos3 = cos_b.ap.rearrange("p (s i) -> p s i", i=16)
            sin3 = sin_b.ap.rearrange("p (s i) -> p s i", i=16)
            def passA(j):
                for i in range(4):
                    st = j * 4 + i
                    t0 = tb + st * 128
                    xs = xs_b[st % 2]
                    hb = hb1_b[(j % 2) * 4 + i]
                    P.dma(lambda e, xs=xs, t0=t0: e.dma_start(out=xs.ap, in_=x_c[t0:t0 + 128, :]), writes=xs.k())
                    ssc, ssk = newcol()
                    P.op("act", lambda e, xs=xs, ssc=ssc: e.activation(out=junk1.ap, in_=xs.ap, func=AF.Square, accum_out=ssc), reads=xs.k(), writes=junk1.k() + [ssk])
                    rc, rk = rstd_of(ssc, ssk, DM)
                    P.op("dve", lambda e, xs=xs, hb=hb, rc=rc: e.scalar_tensor_tensor(out=hb.ap, in0=xs.ap, scalar=rc, in1=bc_h[:, GPM, :], op0=ALU.mult, op1=ALU.mult),
                         reads=xs.k() + [rk] + CK, writes=hb.k())

            passA(0)
            for j in range(4):
                hT = hT1_b[j % 2]
                TQ = TQ_b[j % 2]
                cs = j * 512
                P.op("pool", lambda e, TQ=TQ: e.memset(TQ.ap[:, :, 0:64], 1.0), writes=TQ.k())
                for q, src in ((64, cos3), (80, cos3), (96, sin3), (112, sin3)):
                    P.op("pool", lambda e, TQ=TQ, q=q, src=src, j=j: e.tensor_copy(out=TQ.ap[:, :, q:q + 16], in_=src[:, j * 4:(j + 1) * 4, :]),
                         reads=cos_b.k() + sin_b.k(), writes=TQ.k())
                for i in range(4):
                    hb = hb1_b[(j % 2) * 4 + i]
                    tbk = 6 + (i % 2)
                    for k in range(8):
                        P.op("pe", lambda e, hb=hb, k=k, tbk=tbk: e.transpose(out=bankbf(tbk)[:, k * 128:(k + 1) * 128], in_=hb.ap[:, k * 128:(k + 1) * 128], identity=ident),
                             reads=hb.k() + CK, writes=[pk[tbk]])
                    P.op("act", lambda e, hT=hT, i=i, tbk=tbk: e.activation(out=hT.ap[:, :, i * 128:(i + 1) * 128], in_=bankbf(tbk).rearrange("p (k t) -> p k t", k=8), func=AF.Copy),
                         reads=[pk[tbk]], writes=hT.k())
                if j < 3:
                    passA(j + 1)
                lat_banks = []
                for i in range(4):
                    b = [0, 1, 2, 3][i]
                    lat_banks.append(b)
                    for k in range(8):
                        P.op("pe", lambda e, b=b, k=k, i=i, hT=hT: e.matmul(out=bank(b, 416), lhsT=hT.ap[:, k, i * 128:(i + 1) * 128], rhs=win_b[k].ap[:, 1536:1952], start=(k == 0), stop=(k == 7)),
                             reads=win_b[k].k() + hT.k(), writes=[pk[b]])
                for i in range(4):
                    st = j * 4 + i
                    b = lat_banks[i]
                    qln, kvln, krt, rA, rB = qln_b[i], kvln_b[i], krt_b[i], ropeA_b[i], ropeB_b[i]
                    s1c, s1k = newcol()
                    P.op("act", lambda e, b=b, s1c=s1c: e.activation(out=junk1.ap[:, 0:256], in_=bank(b, 256), func=AF.Square, accum_out=s1c), reads=[pk[b]], writes=junk1.k() + [s1k])
                    s2c, s2k = newcol()
                    P.op("act", lambda e, b=b, s2c=s2c: e.activation(out=junk1.ap[:, 0:128], in_=bank(b, 128, 256), func=AF.Square, accum_out=s2c), reads=[pk[b]], writes=junk1.k() + [s2k])
                    r1c, r1k = rstd_of(s1c, s1k, 256)
                    r2c, r2k = rstd_of(s2c, s2k, 128)
                    P.op("act", lambda e, b=b, qln=qln, r1c=r1c: e.activation(out=qln.ap, in_=bank(b, 256), func=AF.Copy, scale=r1c), reads=[pk[b], r1k], writes=qln.k())
                    P.op("act", lambda e, b=b, kvln=kvln, r2c=r2c: e.activation(out=kvln.ap, in_=bank(b, 128, 256), func=AF.Copy, scale=r2c), reads=[pk[b], r2k], writes=kvln.k())
                    for half in range(2):
                        P.op("dve", lambda e, b=b, half=half, st=st, rA=rA: e.tensor_tensor(out=rA.ap[:, half * 16:(half + 1) * 16], in0=bank(b, 16, 384 + half * 16), in1=cos3[:, st, :], op=ALU.mult),
                             reads=[pk[b]] + cos_b.k(), writes=rA.k())
                        P.op("dve", lambda e, b=b, half=half, st=st, rB=rB: e.tensor_tensor(out=rB.ap[:, half * 16:(half + 1) * 16], in0=bank(b, 16, 384 + half * 16), in1=sin3[:, st, :], op=ALU.mult),
                             reads=[pk[b]] + sin_b.k(), writes=rB.k())
                    P.op("dve", lambda e, krt=krt, rA=rA, rB=rB: e.tensor_tensor(out=krt.ap[:, 0:16], in0=rA.ap[:, 0:16], in1=rB.ap[:, 16:32], op=ALU.subtract),
                         reads=rA.k() + rB.k(), writes=krt.k())
                    P.op("dve", lambda e, krt=krt, rA=rA, rB=rB: e.tensor_tensor(out=krt.ap[:, 16:32], in0=rA.ap[:, 16:32], in1=rB.ap[:, 0:16], op=ALU.add),
                         reads=rA.k() + rB.k(), writes=krt.k())
                for qk in range(2):
                    for m in range(4):
                        b = 4 + (rr["pb"] % 2)
                        rr["pb"] += 1
                        col = qk * 512 + m * 128
                        for k in range(8):
                            P.op("pe", lambda e, b=b, k=k, col=col, hT=hT: e.matmul(out=bank(b), lhsT=win_b[k].ap[:, col:col + 128], rhs=hT.ap[:, k, :], start=(k == 0), stop=(k == 7)),
                                 reads=win_b[k].k() + hT.k(), writes=[pk[b]])
                        sg = stg_b[rr["pb"] % 6]
                        if qk == 0:
                            P.op("act", lambda e, b=b, sg=sg: e.activation(out=sg.ap, in_=bank(b), func=AF.Copy, scale=0.125), reads=[pk[b]], writes=sg.k())
                            P.dma(lambda e, sg=sg, m=m, cs=cs: e.dma_start(out=sc_qs[m][:, cs:cs + 512], in_=sg.ap), reads=sg.k(), writes=[DK("qs", m, cs)])
                        else:
                            P.op("dve", lambda e, b=b, sg=sg: e.tensor_copy(out=sg.ap, in_=bank(b)), reads=[pk[b]], writes=sg.k())
                            P.dma(lambda e, sg=sg, m=m, cs=cs: e.dma_start(out=sc_ks[m][:, cs:cs + 512], in_=sg.ap), reads=sg.k(), writes=[DK("ks", m, cs)])
                for i in range(4):
                    qln, kvln, krt = qln_b[i], kvln_b[i], krt_b[i]
                    tbk = 6 + (i % 2)
                    for k in range(2):
                        P.op("pe", lambda e, k=k, qln=qln, tbk=tbk: e.transpose(out=bankbf(tbk)[:, k * 128:(k + 1) * 128], in_=qln.ap[:, k * 128:(k + 1) * 128], identity=ident), reads=qln.k() + CK, writes=[pk[tbk]])
                    P.op("pe", lambda e, kvln=kvln, tbk=tbk: e.transpose(out=bankbf(tbk)[:, 256:384], in_=kvln.ap, identity=ident), reads=kvln.k() + CK, writes=[pk[tbk]])
                    P.op("pe", lambda e, krt=krt, tbk=tbk: e.transpose(out=bankbf(tbk)[0:32, 384:512], in_=krt.ap, identity=ident), reads=krt.k() + CK, writes=[pk[tbk]])
                    for k in range(2):
                        P.op("act", lambda e, k=k, i=i, tbk=tbk: e.activation(out=qlT_b.ap[:, k, i * 128:(i + 1) * 128], in_=bankbf(tbk)[:, k * 128:(k + 1) * 128], func=AF.Copy, scale=vec_h[:, V_GQ + k:V_GQ + k + 1]),
                             reads=[pk[tbk]] + CK, writes=qlT_b.k(k * 512, (k + 1) * 512))
                    P.op("act", lambda e, i=i, tbk=tbk: e.activation(out=kvT_b.ap[:, i * 128:(i + 1) * 128], in_=bankbf(tbk)[:, 256:384], func=AF.Copy, scale=vec_h[:, V_GKV:V_GKV + 1]),
                         reads=[pk[tbk]] + CK, writes=kvT_b.k())
                    P.op("dve", lambda e, i=i, tbk=tbk: e.tensor_copy(out=krT_b.ap[0:32, i * 128:(i + 1) * 128], in_=bankbf(tbk)[0:32, 384:512]), reads=[pk[tbk]], writes=krT_b.k())
                    P.op("pe", lambda e, i=i, TQ=TQ: e.transpose(out=bank(2)[:, i * 128:(i + 1) * 128], in_=TQ.ap[:, i, :], identity=ident32), reads=TQ.k() + CK, writes=[pk[2]])
                P.op("act", lambda e: e.activation(out=TtT_b.ap, in_=bank(2), func=AF.Copy), reads=[pk[2]], writes=TtT_b.k())
                for rrow in range(2):
                    P.dma(lambda e, rrow=rrow, cs=cs: e.dma_start(out=sc_kr[rrow * 32:(rrow + 1) * 32, cs:cs + 512], in_=krT_b.ap[0:32, :]), reads=krT_b.k(), writes=[DK("kr", cs, rrow)])
                vst = vst_b[0]
                for i in range(4):
                    b = 4 + (rr["pb"] % 2)
                    rr["pb"] += 1
                    for k in range(8):
                        P.op("pe", lambda e, b=b, k=k, i=i, hT=hT: e.matmul(out=bank(b), lhsT=hT.ap[:, k, i * 128:(i + 1) * 128], rhs=win_b[k].ap[:, 1024:1536], start=(k == 0), stop=(k == 7)),
                             reads=win_b[k].k() + hT.k(), writes=[pk[b]])
                    if i % 2 == 0:
                        P.op("act", lambda e, b=b, i=i, vst=vst: e.activation(out=vst.ap[:, i, :], in_=bank(b), func=AF.Copy), reads=[pk[b]], writes=vst.k(i * 512, (i + 1) * 512))
                    else:
                        P.op("dve", lambda e, b=b, i=i, vst=vst: e.tensor_copy(out=vst.ap[:, i, :], in_=bank(b)), reads=[pk[b]], writes=vst.k(i * 512, (i + 1) * 512))
                P.dma(lambda e, vst=vst, cs=cs: e.dma_start(out=sc_vs[cs:cs + 512, :].rearrange("(i p) n -> p i n", p=128), in_=vst.ap), reads=vst.k(), writes=[DK("vs", cs)])
                for h in range(8):
                    b = [0, 1, 3][rr["pb"] % 3]
                    rr["pb"] += 1
                    for k in range(2):
                        P.op("pe", lambda e, b=b, k=k, h=h: e.matmul(out=bank(b), lhsT=wq_b.ap[:, k, h * 128:(h + 1) * 128], rhs=qlT_b.ap[:, k, :], start=(k == 0), stop=(k == 1)),
                             reads=wq_b.k() + qlT_b.k(), writes=[pk[b]])
                    sg = stg_b[rr["pb"] % 6]
                    P.op("dve", lambda e, b=b, sg=sg: e.tensor_tensor(out=sg.ap, in0=bank(b), in1=TtT_b.ap, op=ALU.mult), reads=[pk[b]] + TtT_b.k(), writes=sg.k())
                    P.dma(lambda e, sg=sg, h=h, cs=cs: e.dma_start(out=sc_qm[h][:, cs:cs + 512], in_=sg.ap), reads=sg.k(), writes=[DK("qm", h, cs)])
                for m in range(4):
                    b = [0, 1, 3][rr["pb"] % 3]
                    rr["pb"] += 1
                    P.op("pe", lambda e, b=b, m=m: e.matmul(out=bank(b), lhsT=wkv_b.ap[:, m * 128:(m + 1) * 128], rhs=kvT_b.ap, start=True, stop=True), reads=wkv_b.k() + kvT_b.k(), writes=[pk[b]])
                    sg = stg_b[rr["pb"] % 6]
                    P.op("act", lambda e, b=b, sg=sg: e.activation(out=sg.ap, in_=bank(b), func=AF.Copy), reads=[pk[b]], writes=sg.k())
                    P.dma(lambda e, sg=sg, m=m, cs=cs: e.dma_start(out=sc_km[m][:, cs:cs + 512], in_=sg.ap), reads=sg.k(), writes=[DK("km", m, cs)])
                vst = vst_b[1]
                for i in range(4):
                    b = [0, 1, 3][rr["pb"] % 3]
                    rr["pb"] += 1
                    P.op("pe", lambda e, b=b, i=i: e.matmul(out=bank(b), lhsT=kvT_b.ap[:, i * 128:(i + 1) * 128], rhs=wkv_b.ap[:, 512:1024], start=True, stop=True), reads=wkv_b.k() + kvT_b.k(), writes=[pk[b]])
                    P.op("dve", lambda e, b=b, i=i, vst=vst: e.tensor_copy(out=vst.ap[:, i, :], in_=bank(b)), reads=[pk[b]], writes=vst.k(i * 512, (i + 1) * 512))
                P.dma(lambda e, vst=vst, cs=cs: e.dma_start(out=sc_vm[cs:cs + 512, :].rearrange("(i p) n -> p i n", p=128), in_=vst.ap), reads=vst.k(), writes=[DK("vm", cs)])
                if s == 0 and j == 0:
                    load_resident_weights()

        conv_steps.extend([(lambda c=c: _conv_up(c)) for c in range(NCH)] + [(lambda c=c: _conv_dn(c)) for c in range(NCH)])
        junk1 = abuf(59392, 1024)

        def attn_blocks(qt):
            return [(kb, max(0, kb - 4 * qt) * 128) for kb in range(4 * qt + 3, -1, -1)]

        cnt = {"z": 0, "o": 0, "e": 0, "sp": 0, "a": 0, "oa": 0, "ost": 0}

        def run_pipeline(items, nst):
            n = len(items)
            for t in range(n + nst - 1):
                for k in range(nst):
                    i = t - k
                    if 0 <= i < n and items[i][k] is not None:
                        items[i][k]()

        gcount = {"g": 0, "b": 0}

        def stage2_sb(s):
            items = []

            def load_pair(m):
                QT = QT_b[m % 2]
                P.dma(lambda e: e.dma_start(out=QT.ap, in_=sc_qs[m]), reads=[DK("qs", m, c) for c in range(0, SEQ, 512)], writes=QT.k())
                for eh in range(2):
                    KTz, Vz = KTz_b[m % 2][eh], Vz_b[m % 2][eh]
                    Vz3 = Vz.ap.rearrange("p (kb n) -> p kb n", n=128)
                    oh = 1 - eh
                    P.dma(lambda e, KTz=KTz: e.dma_start(out=KTz.ap, in_=sc_ks[m]), reads=[DK("ks", m, c) for c in range(0, SEQ, 512)], writes=KTz.k())
                    P.op("pool", lambda e, KTz=KTz, oh=oh: e.memset(KTz.ap[oh * 64:(oh + 1) * 64, :], 0.0), reads=KTz.k(), writes=KTz.k())
                    P.dma(lambda e, Vz3=Vz3: e.dma_start(out=Vz3, in_=sc_vs[:, m * 128:(m + 1) * 128].rearrange("(kb p) n -> p kb n", p=128)),
                          reads=[DK("vs", c) for c in range(0, SEQ, 512)], writes=Vz.k())
                    P.op("pool", lambda e, Vz3=Vz3, oh=oh: e.memset(Vz3[:, :, oh * 64:(oh + 1) * 64], 0.0), reads=Vz.k(), writes=Vz.k())

            def make_item(m, qt, eh, bi, kb, c0, c0p, nblk, bo, pre):
                b = gcount["b"]
                gcount["b"] += 1
                QT, KT, V = QT_b[m % 2], KTz_b[m % 2][eh], Vz_b[m % 2][eh]
                V3 = V.ap.rearrange("p (kb n) -> p kb n", n=128)
                bz = b % 4
                et = et_b[b % 2]
                sp, spp = sp_b[b % 3], sp_b[(b - 1) % 3]
                At = At_b[b % 3]
                sa, sap = sacc_b[b % 3], sacc_b[(b - 1) % 3]
                diag = kb >= 4 * qt
                first = bi == 0
                last = bi == nblk - 1
                q0 = qt * 512 + c0
                q1 = (qt + 1) * 512

                def st0():
                    for f in pre:
                        f()
                    if first and eh == 0:
                        P.op("pe", lambda e: e.matmul(out=bank(bo), lhsT=zeros, rhs=QT.ap[:, 0:512], start=True, stop=False, skip_group_check=True),
                             reads=QT.k() + CK, writes=[pk[bo]])
                    P.op("pe", lambda e: e.matmul(out=bank(bz)[:, c0:512], lhsT=KT.ap[:, kb * 128:(kb + 1) * 128], rhs=QT.ap[:, q0:q1], start=True, stop=False, skip_group_check=True),
                         reads=KT.k() + QT.k(), writes=[pk[bz]])

                def st1():
                    P.op("act", lambda e: e.activation(out=et.ap[:, c0:512], in_=bank(bz)[:, c0:512], func=AF.Exp), reads=[pk[bz]], writes=et.k())
                    P.op("act", lambda e: e.activation(out=sp.ap[:, c0:512], in_=et.ap[:, c0:512], func=AF.Ln, bias=1.0), reads=et.k(), writes=sp.k())
                    if diag:
                        P.op("pool", lambda e: e.tensor_tensor(out=sp.ap[:, c0:c0 + 128], in0=sp.ap[:, c0:c0 + 128], in1=mstrict, op=ALU.mult), reads=sp.k() + CK, writes=sp.k())
                    if bi >= 1:
                        if c0 < c0p:
                            P.op("pool", lambda e: e.memset(sa.ap[:, c0:c0p], 0.0), writes=sa.k())
                        if bi == 1:
                            P.op("pool", lambda e: e.tensor_copy(out=sa.ap[:, c0p:512], in_=spp.ap[:, c0p:512]), reads=spp.k(), writes=sa.k())
                        else:
                            P.op("pool", lambda e: e.tensor_tensor(out=sa.ap[:, c0p:512], in0=sap.ap[:, c0p:512], in1=spp.ap[:, c0p:512], op=ALU.add), reads=sap.k() + spp.k(), writes=sa.k())

                def st2():
                    P.op("pe", lambda e: e.matmul(out=bank(bz)[:, c0:512], lhsT=negU, rhs=sp.ap[:, c0:512], start=False, stop=first, skip_group_check=True),
                         reads=sp.k() + CK, writes=[pk[bz]])
                    if not first:
                        P.op("pe", lambda e: e.matmul(out=bank(bz)[:, c0:512], lhsT=negOnes, rhs=sa.ap[:, c0:512], start=False, stop=True, skip_group_check=True),
                             reads=sa.k() + CK, writes=[pk[bz]])

                def st3():
                    P.op("act", lambda e: e.activation(out=At.ap[:, c0:512], in_=bank(bz)[:, c0:512], func=AF.Exp), reads=[pk[bz]], writes=At.k())
                    if diag:
                        P.op("pool", lambda e: e.tensor_tensor(out=At.ap[:, c0:c0 + 128], in0=At.ap[:, c0:c0 + 128], in1=mstrict, op=ALU.mult), reads=At.k() + CK, writes=At.k())

                def st4():
                    P.op("pe", lambda e: e.matmul(out=bank(bo)[:, c0:512], lhsT=V3[:, kb, :], rhs=At.ap[:, c0:512], start=False, stop=(last and eh == 1), skip_group_check=True),
                         reads=At.k() + V.k(), writes=[pk[bo]])
                    if last and eh == 1:
                        g = gcount["g"]
                        gcount["g"] += 1
                        ost = ost_b[g % 2]
                        P.op("dve", lambda e: e.tensor_copy(out=ost.ap, in_=bank(bo)), reads=[pk[bo]], writes=ost.k())
                        P.dma(lambda e: e.dma_start(out=sc_o[m][:, qt * 512:(qt + 1) * 512], in_=ost.ap), reads=ost.k(), writes=[DK("o", m, qt)])

                return [st0, st1, st2, st3, st4]

            load_pair(0)
            grp = 0
            for m in range(4):
                first_idx = len(items)
                for qt in range(4):
                    bo = 4 + (grp % 2)
                    grp += 1
                    for eh in range(2):
                        blocks = attn_blocks(qt)
                        for bi, (kb, c0) in enumerate(blocks):
                            pre = []
                            if m < 3 and len(items) == first_idx + 6:
                                pre.append(lambda m=m: load_pair(m + 1))
                            if conv_steps and len(items) % 7 == 3:
                                pre.append(conv_steps.pop(0))
                            c0p = blocks[bi - 1][1] if bi > 0 else 0
                            items.append(make_item(m, qt, eh, bi, kb, c0, c0p, len(blocks), bo, pre))
            run_pipeline(items, 5)

        def stage2_mla(s):
            items = []

            def load_head(h):
                QT, KT, V = QT_b[h % 2], KT_b[h % 2], V_b[h % 2]
                m, eh = h // 2, h % 2
                V3 = V.ap[:, 0:16 * 65].rearrange("p (kb n) -> p kb n", n=65)
                P.dma(lambda e: e.dma_start(out=QT.ap, in_=sc_qm[h]), reads=[DK("qm", h, c) for c in range(0, SEQ, 512)], writes=QT.k())
                P.dma(lambda e: e.dma_start(out=KT.ap[0:64, :], in_=sc_km[m][eh * 64:(eh + 1) * 64, :]), reads=[DK("km", m, c) for c in range(0, SEQ, 512)], writes=KT.k())
                P.dma(lambda e: e.dma_start(out=KT.ap[64:128, :], in_=sc_kr), reads=[DK("kr", c, r) for c in range(0, SEQ, 512) for r in range(2)], writes=KT.k())
                P.op("pool", lambda e: e.memset(V3[:, :, 64:65], 1.0), writes=V.k())
                P.dma(lambda e: e.dma_start(out=V3[:, :, 0:64], in_=sc_vm[:, h * 64:(h + 1) * 64].rearrange("(kb p) n -> p kb n", p=128)),
                      reads=[DK("vm", c) for c in range(0, SEQ, 512)], writes=V.k())

            def make_item(h, qt, bi, kb, c0, nblk, bo, g, pre):
                b = gcount["b"]
                gcount["b"] += 1
                QT, KT, V = QT_b[h % 2], KT_b[h % 2], V_b[h % 2]
                m, eh = h // 2, h % 2
                V3 = V.ap[:, 0:16 * 65].rearrange("p (kb n) -> p kb n", n=65)
                bz = b % 4
                At = At_b[b % 3]
                diag = kb >= 4 * qt
                first = bi == 0
                last = bi == nblk - 1
                q0 = qt * 512 + c0
                q1 = (qt + 1) * 512
                oa, rden, ost = oa_b[g % 2], rden_b[g % 2], ost_b[g % 2]

                def st0():
                    for f in pre:
                        f()
                    if first:
                        P.op("pe", lambda e: e.matmul(out=bank(bo)[0:65, :], lhsT=zeros[:, 0:65], rhs=QT.ap[:, 0:512], start=True, stop=False, skip_group_check=True),
                             reads=QT.k() + CK, writes=[pk[bo]])
                    P.op("pe", lambda e: e.matmul(out=bank(bz)[:, c0:512], lhsT=KT.ap[:, kb * 128:(kb + 1) * 128], rhs=QT.ap[:, q0:q1], start=True, stop=True),
                         reads=KT.k() + QT.k(), writes=[pk[bz]])

                def st1():
                    P.op("act", lambda e: e.activation(out=At.ap[:, c0:512], in_=bank(bz)[:, c0:512], func=AF.Exp, scale=SC_MLA), reads=[pk[bz]], writes=At.k())
                    if diag:
                        P.op("pool", lambda e: e.tensor_tensor(out=At.ap[:, c0:c0 + 128], in0=At.ap[:, c0:c0 + 128], in1=mle, op=ALU.mult), reads=At.k() + CK, writes=At.k())

                def st2():
                    P.op("pe", lambda e: e.matmul(out=bank(bo)[0:65, c0:512], lhsT=V3[:, kb, 0:65], rhs=At.ap[:, c0:512], start=False, stop=last, skip_group_check=True),
                         reads=At.k() + V.k(), writes=[pk[bo]])
                    if last:
                        P.op("act", lambda e: e.activation(out=oa.ap[0:65, :], in_=bank(bo)[0:65, :], func=AF.Copy), reads=[pk[bo]], writes=oa.k())

                def st3():
                    if last:
                        P.op("pe", lambda e: e.matmul(out=bank(6)[0:64, :], lhsT=esel[0:65, :], rhs=oa.ap[0:65, :], start=True, stop=True), reads=oa.k() + CK, writes=[pk[6]])

                def st4():
                    if last:
                        P.op("dve", lambda e: e.reciprocal(out=rden.ap[0:64, :], in_=bank(6)[0:64, :]), reads=[pk[6]], writes=rden.k())
                        P.op("dve", lambda e: e.tensor_tensor(out=ost.ap[0:64, :], in0=oa.ap[0:64, :], in1=rden.ap[0:64, :], op=ALU.mult), reads=oa.k() + rden.k(), writes=ost.k())
                        P.dma(lambda e: e.dma_start(out=sc_o[4 + m][eh * 64:(eh + 1) * 64, qt * 512:(qt + 1) * 512], in_=ost.ap[0:64, :]), reads=ost.k(), writes=[DK("o", 4 + m, qt, eh)])

                return [st0, st1, st2, st3, st4]

            load_head(0)
            for h in range(8):
                first_idx = len(items)
                for qt in range(4):
                    g = gcount["g"]
                    gcount["g"] += 1
                    bo = 4 + (g % 2)
                    blocks = attn_blocks(qt)
                    for bi, (kb, c0) in enumerate(blocks):
                        pre = []
                        if h < 7 and len(items) == first_idx + 6:
                            pre.append(lambda h=h: load_head(h + 1))
                        items.append(make_item(h, qt, bi, kb, c0, len(blocks), bo, g, pre))
            run_pipeline(items, 5)

        def norm_residual(src_ap, src_keys, gidx, xb, tA):
            ssc, ssk = newcol()
            P.op("act", lambda e: e.activation(out=junk_b.ap, in_=src_ap, func=AF.Square, accum_out=ssc), reads=src_keys, writes=junk_b.k() + [ssk])
            rc, rk = rstd_of(ssc, ssk, DM)
            P.op("dve", lambda e: e.scalar_tensor_tensor(out=tA.ap, in0=src_ap, scalar=rc, in1=bc_h[:, gidx, :], op0=ALU.mult, op1=ALU.mult), reads=src_keys + [rk] + CK, writes=tA.k())
            P.op("pool", lambda e: e.tensor_tensor(out=xb.ap, in0=xb.ap, in1=tA.ap, op=ALU.add), reads=xb.k() + tA.k(), writes=xb.k())

        def prenorm_T(xb, gidx, hb, i, tb_):
            ssc, ssk = newcol()
            P.op("act", lambda e: e.activation(out=junk_b.ap, in_=xb.ap, func=AF.Square, accum_out=ssc), reads=xb.k(), writes=junk_b.k() + [ssk])
            rc, rk = rstd_of(ssc, ssk, DM)
            P.op("dve", lambda e: e.scalar_tensor_tensor(out=hb.ap, in0=xb.ap, scalar=rc, in1=bc_h[:, gidx, :], op0=ALU.mult, op1=ALU.mult), reads=xb.k() + [rk] + CK, writes=hb.k())
            for k in range(8):
                P.op("pe", lambda e, k=k: e.transpose(out=bankbf(tb_)[:, k * 128:(k + 1) * 128], in_=hb.ap[:, k * 128:(k + 1) * 128], identity=ident), reads=hb.k() + CK, writes=[pk[tb_]])
            P.op("act", lambda e: e.activation(out=hT3_b.ap[:, :, i * 128:(i + 1) * 128], in_=bankbf(tb_).rearrange("p (k t) -> p k t", k=8), func=AF.Copy), reads=[pk[tb_]], writes=hT3_b.k())

        def wide(b0):
            return ps_h[:, b0 * 512:(b0 + 2) * 512], [pk[b0], pk[b0 + 1]]

        def stage3(s):
            tb = s * SEQ
            def prologue(j):
                cs = j * 512
                okeys = [DK("o", m, j) for m in range(4)] + [DK("o", 4 + m, j, eh) for m in range(4) for eh in range(2)]
                P.dma(lambda e, cs=cs: e.dma_start(out=OT_b.ap, in_=sc_o[:, :, cs:cs + 512].rearrange("c p t -> p c t")), reads=okeys, writes=OT_b.k())
                for g in range(2):
                    if g == 0:
                        P.op("act", lambda e: e.activation(out=sq_b.ap[:, 0:4, :], in_=OT_b.ap[:, 0:4, :], func=AF.Square), reads=OT_b.k(0, 2048), writes=sq_b.k(0, 2048))
                    else:
                        P.op("pool", lambda e: e.tensor_tensor(out=sq_b.ap[:, 4:8, :], in0=OT_b.ap[:, 4:8, :], in1=OT_b.ap[:, 4:8, :], op=ALU.mult), reads=OT_b.k(2048, 4096), writes=sq_b.k(2048, 4096))
                for g in range(2):
                    for c in range(4):
                        P.op("pe", lambda e, g=g, c=c: e.matmul(out=bank(6 + g), lhsT=gsum, rhs=sq_b.ap[:, g * 4 + c, :], start=(c == 0), stop=(c == 3)), reads=sq_b.k(g * 2048, (g + 1) * 2048) + CK, writes=[pk[6 + g]])
                    P.op("act", lambda e, g=g: e.activation(out=y_b[g].ap, in_=bank(6 + g), func=AF.Sqrt, bias=EPS), reads=[pk[6 + g]], writes=y_b[g].k())
                    P.op("dve", lambda e, g=g: e.reciprocal(out=y_b[g].ap, in_=y_b[g].ap), reads=y_b[g].k(), writes=y_b[g].k())
                for c in range(8):
                    P.op("dve", lambda e, c=c: e.scalar_tensor_tensor(out=OT_b.ap[:, c, :], in0=OT_b.ap[:, c, :], scalar=vec_h[:, V_GGRP + c:V_GGRP + c + 1], in1=y_b[c // 4].ap, op0=ALU.mult, op1=ALU.mult),
                         reads=OT_b.k(c * 512, (c + 1) * 512) + y_b[c // 4].k() + CK, writes=OT_b.k(c * 512, (c + 1) * 512))

            for j in range(4):
                cs = j * 512
                if j == 0:
                    prologue(0)
                for i in range(4):
                    P.dma(lambda e, i=i, cs=cs: e.dma_start(out=x1_b[i].ap, in_=x_c[tb + cs + i * 128:tb + cs + (i + 1) * 128, :]), writes=x1_b[i].k())
                for c in range(NCH):
                    P.dma(lambda e, c=c: e.dma_start(out=wd_b[c].ap, in_=sc_wd[c * 128:(c + 1) * 128, :]), reads=[DK("wd", c)], writes=wd_b[c].k())
                for i in range(4):
                    t0 = tb + cs + i * 128
                    xb = x1_b[i]
                    b0 = (i % 2) * 2
                    for half in range(2):
                        for c in range(8):
                            P.op("pe", lambda e, half=half, c=c, i=i, b0=b0: e.matmul(out=bank(b0 + half), lhsT=OT_b.ap[:, c, i * 128:(i + 1) * 128], rhs=wo_h[:, c, half * 512:(half + 1) * 512], start=(c == 0), stop=(c == 7)),
                                 reads=OT_b.k(c * 512, (c + 1) * 512) + wo_b.keys, writes=[pk[b0 + half]])
                    wa, wk = wide(b0)
                    norm_residual(wa, wk, GPOM, xb, tmpA2[i % 2])
                for i in range(4):
                    prenorm_T(x1_b[i], GPF, hb3_b[i % 2], i, 6 + (i % 2))
                for c in range(NCH):
                    wu = wu_b[c % 3]
                    P.dma(lambda e, wu=wu, c=c: e.dma_start(out=wu.ap.rearrange("p a k n -> p (a k n)"), in_=sc_wu[c]), reads=[DK("wu", c)], writes=wu.k())
                    ys = []
                    for gv in range(2):
                        b = 4 + (rr["pb"] % 4)
                        rr["pb"] += 1
                        for k in range(8):
                            P.op("pe", lambda e, b=b, k=k, gv=gv, wu=wu: e.matmul(out=bank(b), lhsT=wu.ap[:, gv, k, :], rhs=hT3_b.ap[:, k, :], start=(k == 0), stop=(k == 7)), reads=wu.k() + hT3_b.k(), writes=[pk[b]])
                        cc = gv * NCH + c
                        y = y_b[(c % 2) * 2 + gv]
                        w0 = vec_h[:, V_CW + cc * 3:V_CW + cc * 3 + 1]
                        w1 = vec_h[:, V_CW + cc * 3 + 1:V_CW + cc * 3 + 2]
                        w2 = vec_h[:, V_CW + cc * 3 + 2:V_CW + cc * 3 + 3]
                        bb = vec_h[:, V_CB + cc:V_CB + cc + 1]
                        P.op("act", lambda e, b=b, y=y, w2=w2, bb=bb: e.activation(out=y.ap, in_=bank(b), func=AF.Identity, scale=w2, bias=bb), reads=[pk[b]] + CK, writes=y.k())
                        P.op("dve", lambda e, b=b, y=y, w1=w1: e.scalar_tensor_tensor(out=y.ap[:, 1:512], in0=bank(b)[:, 0:511], scalar=w1, in1=y.ap[:, 1:512], op0=ALU.mult, op1=ALU.add), reads=[pk[b]] + y.k() + CK, writes=y.k())
                        P.op("dve", lambda e, b=b, y=y, w0=w0: e.scalar_tensor_tensor(out=y.ap[:, 2:512], in0=bank(b)[:, 0:510], scalar=w0, in1=y.ap[:, 2:512], op0=ALU.mult, op1=ALU.add), reads=[pk[b]] + y.k() + CK, writes=y.k())
                        if j > 0:
                            P.op("dve", lambda e, y=y, w0=w0, cc=cc: e.scalar_tensor_tensor(out=y.ap[:, 0:2], in0=halo_h[:, cc, :], scalar=w0, in1=y.ap[:, 0:2], op0=ALU.mult, op1=ALU.add), reads=[halo_keys[cc]] + y.k() + CK, writes=y.k())
                            P.op("dve", lambda e, y=y, w1=w1, cc=cc: e.scalar_tensor_tensor(out=y.ap[:, 0:1], in0=halo_h[:, cc, 1:2], scalar=w1, in1=y.ap[:, 0:1], op0=ALU.mult, op1=ALU.add), reads=[halo_keys[cc]] + y.k() + CK, writes=y.k())
                        P.op("dve", lambda e, b=b, cc=cc: e.tensor_copy(out=halo_h[:, cc, :], in_=bank(b)[:, 510:512]), reads=[pk[b]], writes=[halo_keys[cc]])
                        ys.append(y)
                    gg = gg_b[c % 2]
                    P.op("act", lambda e, gg=gg, y=ys[0]: e.activation(out=gg.ap, in_=y.ap, func=AF.Gelu_apprx_tanh), reads=ys[0].k(), writes=gg.k())
                    P.op("pool", lambda e, gg=gg, y=ys[1], c=c: e.tensor_tensor(out=fin_b.ap[:, c, :], in0=gg.ap, in1=y.ap, op=ALU.mult), reads=gg.k() + ys[1].k(), writes=fin_b.k(c * 512, (c + 1) * 512))
                for pp in range(2):
                    for c in range(NCH):
                        wd = wd_b[c]
                        for il in range(2):
                            i = pp * 2 + il
                            for half in range(2):
                                b = pp * 4 + il * 2 + half
                                P.op("pe", lambda e, b=b, c=c, i=i, half=half, wd=wd: e.matmul(out=bank(b), lhsT=fin_b.ap[:, c, i * 128:(i + 1) * 128], rhs=wd.ap[:, half * 512:(half + 1) * 512], start=(c == 0), stop=(c == NCH - 1)),
                                     reads=fin_b.k(c * 512, (c + 1) * 512) + wd.k(), writes=[pk[b]])
                for i in range(4):
                    wa, wk = wide(i * 2)
                    norm_residual(wa, wk, GPOF, x1_b[i], tmpA2[i % 2])
                for i in range(4):
                    prenorm_T(x1_b[i], GPG, hb3_b[i % 2], i, i % 2)
                if j < 3:
                    prologue(j + 1)
                P.dma(lambda e, cs=cs: e.dma_start(out=pb_b.ap, in_=p_c[tb + cs:tb + cs + 512, :].rearrange("(i p) n -> p i n", p=128)), writes=pb_b.k(), queue="pool")
                for i in range(4):
                    tbk = 2 + (i % 2)
                    for k in range(2):
                        P.op("pe", lambda e, i=i, k=k, tbk=tbk: e.transpose(out=bankbf(tbk)[:, k * 128:(k + 1) * 128], in_=pb_b.ap[:, i, k * 128:(k + 1) * 128], identity=ident), reads=pb_b.k() + CK, writes=[pk[tbk]])
                    P.op("act", lambda e, i=i, tbk=tbk: e.activation(out=pT_b.ap[:, :, i * 128:(i + 1) * 128], in_=bankbf(tbk)[:, 0:256].rearrange("p (k t) -> p k t", k=2), func=AF.Copy), reads=[pk[tbk]], writes=pT_b.k())
                for pr in range(2):
                    subs = (pr * 2, pr * 2 + 1)
                    ctx = {}
                    for i in subs:
                        bs = (i % 2) * 4
                        for half in range(2):
                            for k in range(8):
                                P.op("pe", lambda e, half=half, k=k, i=i, bs=bs: e.matmul(out=bank(bs + half), lhsT=hT3_b.ap[:, k, i * 128:(i + 1) * 128], rhs=wpg_h[:, k, half * 512:(half + 1) * 512], start=(k == 0), stop=(k == 7)),
                                     reads=hT3_b.k() + wpg_b.keys, writes=[pk[bs + half]])
                            for k in range(2):
                                P.op("pe", lambda e, half=half, k=k, i=i, bs=bs: e.matmul(out=bank(bs + 2 + half), lhsT=pT_b.ap[:, k, i * 128:(i + 1) * 128], rhs=wple_h[:, k, half * 512:(half + 1) * 512], start=(k == 0), stop=(k == 1)),
                                     reads=pT_b.k() + wple_b.keys, writes=[pk[bs + 2 + half]])
                        ctx[i] = (wide(bs), wide(bs + 2), tmpA2[i % 2])
                    for i in subs:
                        (gla, glk), _, tA = ctx[i]
                        P.op("dve", lambda e, gla=gla, tA=tA: e.tensor_tensor(out=tA.ap, in0=gla, in1=bc_h[:, BPG, :], op=ALU.add), reads=glk + CK, writes=tA.k())
                    for i in subs:
                        tA = ctx[i][2]
                        P.op("act", lambda e, tA=tA: e.activation(out=tA.ap, in_=tA.ap, func=AF.Sigmoid), reads=tA.k(), writes=tA.k())
                    for i in subs:
                        _, (ea, ek), tA = ctx[i]
                        P.op("dve", lambda e, ea=ea, tA=tA: e.tensor_tensor(out=tA.ap, in0=ea, in1=tA.ap, op=ALU.mult), reads=ek + tA.k(), writes=tA.k())
                    sss = {}
                    for i in subs:
                        tA = ctx[i][2]
                        ssc, ssk = newcol()
                        sss[i] = (ssc, ssk)
                        P.op("act", lambda e, ssc=ssc, tA=tA: e.activation(out=junk_b.ap, in_=tA.ap, func=AF.Square, accum_out=ssc), reads=tA.k(), writes=junk_b.k() + [ssk])
                    rcs = {}
                    for i in subs:
                        rcs[i] = rstd_of(sss[i][0], sss[i][1], DM)
                    for i in subs:
                        tA = ctx[i][2]
                        rc, rk = rcs[i]
                        xb = x1_b[i]
                        t0 = tb + cs + i * 128
                        P.op("dve", lambda e, rc=rc, tA=tA: e.scalar_tensor_tensor(out=tA.ap, in0=tA.ap, scalar=rc, in1=bc_h[:, GPOP, :], op0=ALU.mult, op1=ALU.mult), reads=tA.k() + [rk] + CK, writes=tA.k())
                        P.op("pool", lambda e, xb=xb, tA=tA: e.tensor_tensor(out=xb.ap, in0=xb.ap, in1=tA.ap, op=ALU.add), reads=xb.k() + tA.k(), writes=xb.k())
                        P.dma(lambda e, xb=xb, t0=t0: e.dma_start(out=out_c[t0:t0 + 128, :], in_=xb.ap), reads=xb.k())

        for s in range(nseq):
            if upto >= 1:
                stage1(s)
            if upto >= 2:
                stage2_sb(s)
            if upto >= 3:
                stage2_mla(s)
            if upto >= 4:
                stage3(s)
        P.emit()
    return nc


def _consts():
    j = np.arange(128)[:, None]
    t = np.arange(128)[None, :]
    cb = np.zeros((128, 896), np.float32)
    cb[:, 0:128] = np.eye(128)
    cb[:, 128:256] = (j < t)
    cb[:, 256:384] = (j <= t)
    cb[:, 384:512] = -1.0 * (j >= t)
    cb[:, 512:640] = -1.0
    cb[:, 640:768] = 1.0 / 512
    cf = np.zeros((128, 448), np.float32)
    cf[:, 0:128] = np.eye(128)
    inv = (1.0 / (np.float32(10000.0) ** (np.arange(0, 32, 2, dtype=np.float32) / np.float32(32)))).astype(np.float32)
    cf[:, 128:384] = np.tile(inv, 16)[None, :]
    cf[64, 384:448] = 1.0
    return cb, cf


def prep_shared(w_in, g_pre_mix, g_q_lat, w_q_up, g_kv_lat, w_kv_up, g_grp_sb, g_grp_mla, w_o, g_post_mix,
                g_pre_ffn, w_up, conv_w, conv_b, w_down, g_post_ffn, w_ple, g_ple_gate, w_ple_gate, b_ple_gate, g_post_ple):
    f = np.float32
    c = np.ascontiguousarray
    wq = w_q_up[0].reshape(256, 8, 96)
    wqx = np.concatenate([wq, wq[:, :, 80:96], wq[:, :, 64:80]], axis=2).reshape(256, 1024)
    wkv = w_kv_up[0].reshape(128, 8, 128)
    wkvx = np.concatenate([wkv[:, :, 0:64].reshape(128, 512), wkv[:, :, 64:128].reshape(128, 512)], axis=1)
    wu = w_up[0].reshape(8, 128, 2, NCH, 128)
    wupr = c(wu.transpose(3, 1, 2, 0, 4)).reshape(NCH, 128, 2048)
    vec = np.zeros((128, 192), f)
    vec[:, 0:2] = g_q_lat[0].reshape(2, 128).T
    vec[:, 2:3] = g_kv_lat[0].reshape(1, 128).T
    vec[:, 3:11] = np.concatenate([g_grp_sb[0], g_grp_mla[0]]).reshape(8, 128).T
    cw = conv_w[0].reshape(3, 44, 128)
    vec[:, 11:143] = cw.transpose(2, 1, 0).reshape(128, 132)
    vec[:, 143:187] = conv_b[0].reshape(44, 128).T
    bcv = np.stack([g_pre_mix[0], g_post_mix[0], g_pre_ffn[0], g_post_ffn[0], g_ple_gate[0], g_post_ple[0], b_ple_gate[0]], 0)
    bcv = c(np.broadcast_to(bcv.reshape(1, 7 * 1024), (128, 7 * 1024))).astype(f)
    cb, cf = _consts()
    return {
        "w_in": c(w_in[0]), "w_qx": c(wqx), "w_kvx": c(wkvx), "w_o": c(w_o[0]), "w_pg": c(w_ple_gate[0]),
        "w_ple": c(w_ple[0]), "w_upr": wupr, "w_dn": c(w_down[0]), "vecs": vec, "bcv": bcv, "cf32": cf, "cbf": cb,
    }


def prep_core(x, p, positions, b0, nseq):
    xc = np.ascontiguousarray(x[b0:b0 + nseq].reshape(nseq * SEQ, DM))
    pc = np.ascontiguousarray(p[0, b0:b0 + nseq].reshape(nseq * SEQ, 256))
    pos = positions[b0:b0 + nseq].reshape(nseq, 16, 128).transpose(0, 2, 1)
    posr = np.ascontiguousarray(np.repeat(pos[:, :, :, None], 16, axis=3).reshape(nseq, 128, 256)).astype(np.int32)
    return {"x_c": xc, "p_c": pc, "posr": posr}


_NC_CACHE = {}


def kernel(x, p, positions, **w):
    x = np.asarray(x)
    p = np.asarray(p)
    positions = np.asarray(positions)
    w = {k: np.asarray(v) for k, v in w.items()}
    nseq = x.shape[0] // NCORES
    shared = prep_shared(**w)
    if nseq not in _NC_CACHE:
        _NC_CACHE[nseq] = build_program(nseq)
    nc = _NC_CACHE[nseq]
    in_maps = []
    for c in range(NCORES):
        m = dict(shared)
        m.update(prep_core(x, p, positions, c * nseq, nseq))
        in_maps.append(m)
    res = run_bass_kernel_spmd(nc, in_maps, core_ids=list(range(NCORES)))
    out = np.concatenate([r["out_c"].reshape(nseq, SEQ, DM) for r in res.results], axis=0)
    return out.astype(np.float32)
```

```python
import contextlib
import math
import numpy as np
import concourse.bass as bass
import concourse.mybir as mybir
from concourse.bass_utils import run_bass_kernel_spmd

F32 = mybir.dt.float32
BF16 = mybir.dt.bfloat16
I32 = mybir.dt.int32
ALU = mybir.AluOpType
AF = mybir.ActivationFunctionType

NCORES = 8
SEQ = 2048
DM = 1024
DFF = 2816
NCH = 22
EPS = 1e-6
SC_MLA = 96 ** -0.5
ARENA = 68000
CELL = 256
CUT = 99
SKIP = set()


class Key:
    __slots__ = ("name", "w", "r")

    def __init__(self, name):
        self.name = name
        self.w = None
        self.r = {}


class Op:
    __slots__ = ("eng", "fn", "deps", "signal", "sigval", "lane", "laneval", "is_dma")

    def __init__(self, eng, fn, is_dma=False):
        self.eng = eng
        self.fn = fn
        self.deps = []
        self.signal = False
        self.sigval = None
        self.lane = None
        self.laneval = None
        self.is_dma = is_dma


class Prog:
    ENGS = ("pe", "act", "dve", "pool", "sp")

    def __init__(self, nc, n_lanes_sp=24, n_lanes_pool=8):
        self.nc = nc
        self.q = {e: [] for e in self.ENGS}
        self.n_lanes = {"sp": n_lanes_sp, "pool": n_lanes_pool}
        self.lane_rr = {"sp": 0, "pool": 0}
        self.lane_cnt = {"sp": [0] * n_lanes_sp, "pool": [0] * n_lanes_pool}
        self.lane_last = {"sp": [None] * n_lanes_sp, "pool": [None] * n_lanes_pool}

    def key(self, name):
        return Key(name)

    def _add(self, op, reads, writes):
        ex = [k for k in reads if k.name.startswith("pk")]
        if ex:
            reads = [k for k in reads if not k.name.startswith("pk")]
            writes = list(writes) + ex
        deps = op.deps
        for k in reads:
            if k.w is not None:
                deps.append(k.w)
        for k in writes:
            if k.w is not None:
                deps.append(k.w)
            for o in k.r.values():
                deps.append(o)
        rk = ("dma", id(op)) if op.is_dma else op.eng
        for k in reads:
            k.r[rk] = op
        for k in writes:
            k.w = op
            k.r = {}
        self.q[op.eng].append(op)
        return op

    def op(self, eng, fn, reads=(), writes=()):
        return self._add(Op(eng, fn), reads, writes)

    def dma(self, fn, reads=(), writes=(), queue="sp"):
        op = Op(queue, fn, is_dma=True)
        rr = self.lane_rr[queue]
        self.lane_rr[queue] = (rr + 1) % self.n_lanes[queue]
        op.lane = (queue, rr)
        self.lane_cnt[queue][rr] += 1
        op.laneval = 16 * self.lane_cnt[queue][rr]
        prev = self.lane_last[queue][rr]
        if prev is not None:
            op.deps.append(prev)
        self.lane_last[queue][rr] = op
        return self._add(op, reads, writes)

    def emit(self):
        nc = self.nc
        for e in self.ENGS:
            for op in self.q[e]:
                for d in op.deps:
                    if d.is_dma:
                        continue
                    if d.eng == "pe" and op.eng == "pe" and not op.is_dma:
                        continue
                    d.signal = True
        for e in self.ENGS:
            c = 0
            for op in self.q[e]:
                if op.signal and not op.is_dma:
                    c += 1
                    op.sigval = c
        with contextlib.ExitStack() as es:
            sems = {e: es.enter_context(nc.semaphore(f"s_{e}")) for e in ("pe", "act", "dve", "pool")}
            lanes = {}
            for qn in ("sp", "pool"):
                for i in range(self.n_lanes[qn]):
                    lanes[(qn, i)] = es.enter_context(nc.semaphore(f"l_{qn}{i}"))
            block = es.enter_context(nc.Block())

            def run(ename, eng):
                waited = {}
                for op in self.q[ename]:
                    need = {}
                    for d in op.deps:
                        if d.is_dma:
                            s, v = lanes[d.lane], d.laneval
                        else:
                            if d.eng == "pe" and ename == "pe" and not op.is_dma:
                                continue
                            s, v = sems[d.eng], d.sigval
                        kk = id(s)
                        if need.get(kk, (None, 0))[1] < v:
                            need[kk] = (s, v)
                    for kk, (s, v) in need.items():
                        if waited.get(kk, 0) < v:
                            eng.wait_ge(s, v)
                            waited[kk] = v
                    ins = op.fn(eng)
                    if op.is_dma:
                        ins.then_inc(lanes[op.lane], 16)
                    elif op.signal:
                        ins.then_inc(sems[ename], 1)
                if ename == "sp":
                    for qn in ("sp", "pool"):
                        for i in range(self.n_lanes[qn]):
                            c = self.lane_cnt[qn][i]
                            if c:
                                eng.wait_ge(lanes[(qn, i)], 16 * c)

            @block.tensor
            def _(e):
                run("pe", e)

            @block.scalar
            def _(e):
                run("act", e)

            @block.vector
            def _(e):
                run("dve", e)

            @block.gpsimd
            def _(e):
                run("pool", e)

            @block.sync
            def _(e):
                run("sp", e)


class Buf:
    def __init__(self, ap, keys, off=0, w=1):
        self.ap = ap
        self.keys = keys
        self.off = off
        self.w = w

    def k(self, lo=None, hi=None):
        if lo is None:
            return self.keys
        a = (self.off + lo * self.w) // CELL
        b = (self.off + hi * self.w - 1) // CELL
        base = self.off // CELL
        return self.keys[a - base:b - base + 1]


def build_program(nseq, debug=False, upto=4):
    nc = bass.Bass("TRN2", target_bir_lowering=False)
    ntok = nseq * SEQ
    dkind = "ExternalOutput" if debug else "Internal"

    def din(name, shape, dt=F32):
        return nc.dram_tensor(name, list(shape), dt, kind="ExternalInput").ap()

    x_c = din("x_c", [ntok, DM])
    p_c = din("p_c", [ntok, 256])
    posr = din("posr", [nseq, 128, 256], I32)
    w_in = din("w_in", [DM, 1952])
    w_qx = din("w_qx", [256, 1024])
    w_kvx = din("w_kvx", [128, 1024])
    w_o = din("w_o", [DM, DM])
    w_pg = din("w_pg", [DM, DM])
    w_ple = din("w_ple", [256, DM])
    w_upr = din("w_upr", [NCH, 128, 2048])
    w_dn = din("w_dn", [DFF, DM])
    vecs = din("vecs", [128, 192])
    bcv = din("bcv", [128, 7 * 1024])
    cf32 = din("cf32", [128, 448])
    cbf = din("cbf", [128, 896])
    out_c = nc.dram_tensor("out_c", [ntok, DM], F32, kind="ExternalOutput").ap()

    def dsc(name, shape):
        return nc.dram_tensor(name, list(shape), BF16, kind=dkind).ap()

    sc_qs = dsc("sc_qs", [4, 128, SEQ])
    sc_ks = dsc("sc_ks", [4, 128, SEQ])
    sc_vs = dsc("sc_vs", [SEQ, 512])
    sc_qm = dsc("sc_qm", [8, 128, SEQ])
    sc_km = dsc("sc_km", [4, 128, SEQ])
    sc_kr = dsc("sc_kr", [64, SEQ])
    sc_vm = dsc("sc_vm", [SEQ, 512])
    sc_o = dsc("sc_o", [8, 128, SEQ])
    sc_wu = nc.dram_tensor("sc_wu", [NCH, 128, 2048], BF16, kind="Internal").ap()
    sc_wd = nc.dram_tensor("sc_wd", [DFF, DM], BF16, kind="Internal").ap()

    es = contextlib.ExitStack()
    with es:
        def sb(name, shape, dt):
            return es.enter_context(nc.sbuf_tensor(name, list(shape), dt))

        P = Prog(nc)
        arena_h = sb("arena", [128, ARENA], BF16)
        akeys = [Key(f"a{i}") for i in range((ARENA + CELL - 1) // CELL)]

        def abuf(off, n, dt=BF16, shape=None):
            w = 1 if dt == BF16 else 2
            assert off % 2 == 0 and off + n * w <= ARENA, (off, n, w)
            ap = arena_h[:, off:off + n * w]
            if dt != BF16:
                ap = ap.bitcast(dt)
            if shape is not None:
                names = " ".join(f"d{i}" for i in range(len(shape)))
                ap = ap.rearrange(f"p ({names}) -> p {names}", **{f"d{i}": shape[i] for i in range(len(shape))})
            a = off // CELL
            b = (off + n * w - 1) // CELL
            return Buf(ap, akeys[a:b + 1], off, w)

        def rbuf(name, shape, dt, nkeys=1):
            h = sb(name, shape, dt)
            return Buf(h, [Key(name)]), h

        wo_b, wo_h = rbuf("wo", [128, 8, 1024], BF16)
        wpg_b, wpg_h = rbuf("wpg", [128, 8, 1024], BF16)
        wple_b, wple_h = rbuf("wple", [128, 2, 1024], BF16)
        bc_b, bc_h = rbuf("bc", [128, 7, 1024], F32)
        cb_b, cb_h = rbuf("cb", [128, 896], BF16)
        cf_b, cf_h = rbuf("cf", [128, 448], F32)
        vec_b, vec_h = rbuf("vec", [128, 192], F32)
        stat_h = sb("stat", [128, 64], F32)
        stat_keys = [Key(f"st{i}") for i in range(64)]
        halo_h = sb("halo", [128, 44, 2], F32)
        halo_keys = [Key(f"halo{i}") for i in range(44)]
        ps_h = es.enter_context(nc.psum_tensor("ps", [128, 4096], F32))
        pk = [Key(f"pk{i}") for i in range(8)]

        def bank(b, n=512, lo=0):
            return ps_h[:, b * 512 + lo:b * 512 + lo + n]

        def bankbf(b):
            return ps_h[:, b * 512:(b + 1) * 512].bitcast(BF16)

        ident = cb_h[:, 0:128]
        mstrict = cb_h[:, 128:256]
        mle = cb_h[:, 256:384]
        negU = cb_h[:, 384:512]
        negOnes = cb_h[:, 512:640]
        gsum = cb_h[:, 640:768]
        zeros = cb_h[:, 768:896]
        ident32 = cf_h[:, 0:128]
        invf = cf_h[:, 128:384]
        esel = cf_h[:, 384:448]
        GPM, GPOM, GPF, GPOF, GPG, GPOP, BPG = range(7)
        V_GQ, V_GKV, V_GGRP, V_CW, V_CB = 0, 2, 3, 11, 143
        CK = [cb_b.keys[0], cf_b.keys[0], vec_b.keys[0], bc_b.keys[0]]

        stat_i = [0]

        def newcol():
            i = stat_i[0] % 64
            stat_i[0] += 1
            return stat_h[:, i:i + 1], stat_keys[i]

        def rstd_of(ss_ap, ss_key, n):
            c1, k1 = newcol()
            P.op("act", lambda e: e.activation(out=c1, in_=ss_ap, func=AF.Sqrt, scale=1.0 / n, bias=EPS), reads=[ss_key], writes=[k1])
            c2, k2 = newcol()
            P.op("dve", lambda e: e.reciprocal(out=c2, in_=c1), reads=[k1], writes=[k2])
            return c2, k2

        P.dma(lambda e: e.dma_start(out=cf_h[:], in_=cf32), writes=cf_b.keys)
        P.dma(lambda e: e.dma_start(out=vec_h[:], in_=vecs), writes=vec_b.keys)
        P.dma(lambda e: e.dma_start(out=bc_h[:].rearrange("p a b -> p (a b)"), in_=bcv), writes=bc_b.keys)
        P.dma(lambda e: e.dma_start(out=cb_h[:], in_=cbf), writes=cb_b.keys, queue="pool")
        def load_resident_weights():
            for k in range(8):
                P.dma(lambda e, k=k: e.dma_start(out=wo_h[:, k, :], in_=w_o[k * 128:(k + 1) * 128, :]), writes=wo_b.keys, queue="pool")
                P.dma(lambda e, k=k: e.dma_start(out=wpg_h[:, k, :], in_=w_pg[k * 128:(k + 1) * 128, :]), writes=wpg_b.keys, queue="pool")
            for k in range(2):
                P.dma(lambda e, k=k: e.dma_start(out=wple_h[:, k, :], in_=w_ple[k * 128:(k + 1) * 128, :]), writes=wple_b.keys, queue="pool")

        win_b = [abuf(k * 2048, 1952) for k in range(8)]
        wq_b = abuf(16384, 2048, shape=[2, 1024])
        wkv_b = abuf(18432, 1024)
        A1 = 19456
        xs_b = [abuf(A1 + i * 2048, 1024, F32) for i in range(2)]
        hb1_b = [abuf(A1 + 4096 + i * 1024, 1024) for i in range(2)] + [abuf(56832 + i * 1024, 1024) for i in range(2)] + [abuf(60416 + i * 1024, 1024) for i in range(4)]
        hT1_b = [abuf(A1 + 6144 + i * 4096, 4096, shape=[8, 512]) for i in range(2)]
        stg_b = [abuf(A1 + 14336 + i * 512, 512) for i in range(6)]
        vst_b = [abuf(A1 + 17408 + i * 2048, 2048, shape=[4, 512]) for i in range(2)]
        L0 = A1 + 21504
        qln_b = [abuf(L0 + 7168 + i * 256, 256) for i in range(4)]
        kvln_b = [abuf(L0 + 8192 + i * 128, 128) for i in range(4)]
        krt_b = [abuf(L0 + 8704 + i * 32, 32) for i in range(4)]
        ropeA_b = [abuf(L0 + i * 64, 32, F32) for i in range(4)]
        ropeB_b = [abuf(L0 + 256 + i * 64, 32, F32) for i in range(4)]
        qlT_b = abuf(L0 + 1024, 1024, shape=[2, 512])
        kvT_b = abuf(L0 + 2048, 512)
        krT_b = abuf(L0 + 2560, 512)
        TtT_b = abuf(L0 + 3072, 512, F32)
        TQ_b = [abuf(L0 + 4096 + i * 1024, 512, F32, shape=[4, 128]) for i in range(2)]
        cos_b = abuf(L0 + 6144, 256, F32)
        sin_b = abuf(L0 + 6656, 256, F32)
        rs_b = [abuf(L0 + 7168 + i * 512, 256, F32) for i in range(4)]
        rsi_b = abuf(L0 + 9216, 256, I32)
        assert L0 + 9728 <= ARENA
        A2 = A1
        QT_b = [abuf(A2 + i * 2048, 2048) for i in range(2)]
        KT_b = [abuf(A2 + 4096 + i * 2048, 2048) for i in range(2)]
        V_b = [abuf(A2 + 8192 + i * 2048, 2048) for i in range(2)]
        et_b = [abuf(A2 + 12288 + i * 1024, 512, F32) for i in range(2)]
        sp_b = [abuf(A2 + 14336 + i * 512, 512) for i in range(3)]
        At_b = [abuf(A2 + 15872 + i * 512, 512) for i in range(3)]
        sacc_b = [abuf(A2 + 23040 + i * 512, 512) for i in range(3)]
        KTz_b = [[KT_b[i], abuf(A2 + 24576 + i * 2048, 2048)] for i in range(2)]
        Vz_b = [[V_b[i], abuf(A2 + 28672 + i * 2048, 2048)] for i in range(2)]
        assert A2 + 32768 <= 59392
        oa_b = [abuf(A2 + 17920 + i * 1024, 512, F32) for i in range(2)]
        rden_b = [abuf(A2 + 19968 + i * 1024, 512, F32) for i in range(2)]
        ost_b = [abuf(A2 + 22016 + i * 512, 512) for i in range(2)]
        wd_b = [abuf(c * 1024, 1024) for c in range(NCH)]
        x1_b = [abuf(22528 + i * 2048, 1024, F32) for i in range(4)]
        hb3_b = [abuf(30720 + i * 1024, 1024) for i in range(2)]
        hT3_b = abuf(32768, 4096, shape=[8, 512])
        fin_b = abuf(36864, NCH * 512, shape=[NCH, 512])
        sq_b = abuf(36864, 4096, shape=[8, 512])
        OT_b = abuf(48128, 4096, shape=[8, 512])
        y_b = [abuf(52224 + i * 1024, 512, F32) for i in range(4)]
        gg_b = [abuf(56320 + i * 512, 512) for i in range(2)]
        tmpA_b = abuf(57344, 1024, F32)
        junk_b = abuf(59392, 1024)
        pb_b = abuf(60416, 1024, shape=[4, 256])
        pT_b = abuf(61440, 1024, shape=[2, 512])
        wu_b = [abuf(60416 + i * 2048, 2048, shape=[2, 8, 128]) for i in range(3)]
        tmpA2 = [tmpA_b, abuf(52224 + 2048, 1024, F32)]
        cst_b = [abuf(52224 + i * 2048, 2048) for i in range(2)]
        assert 66560 <= ARENA

        dk = {}

        def DK(*a):
            if a not in dk:
                dk[a] = Key(str(a))
            return dk[a]

        rr = {"pb": 0}
        def _conv_up(c):
            sl = cst_b[c % 2]
            P.dma(lambda e: e.dma_start(out=sl.ap, in_=w_upr[c]), writes=sl.k(), queue="pool")
            P.dma(lambda e: e.dma_start(out=sc_wu[c], in_=sl.ap), reads=sl.k(), writes=[DK("wu", c)])

        def _conv_dn(c):
            sl = cst_b[(NCH + c) % 2]
            P.dma(lambda e: e.dma_start(out=sl.ap[:, 0:1024], in_=w_dn[c * 128:(c + 1) * 128, :]), writes=sl.k(), queue="pool")
            P.dma(lambda e: e.dma_start(out=sc_wd[c * 128:(c + 1) * 128, :], in_=sl.ap[:, 0:1024]), reads=sl.k(), writes=[DK("wd", c)])

        def nextbank(banks):
            b = banks[rr["pb"] % len(banks)]
            rr["pb"] += 1
            return b

        conv_steps = []

        def stage1(s):
            tb = s * SEQ
            for k in range(8):
                P.dma(lambda e, k=k: e.dma_start(out=win_b[k].ap, in_=w_in[k * 128:(k + 1) * 128, :]), writes=win_b[k].k(), queue="pool")
            for k in range(2):
                P.dma(lambda e, k=k: e.dma_start(out=wq_b.ap[:, k, :], in_=w_qx[k * 128:(k + 1) * 128, :]), writes=wq_b.k(k * 1024, (k + 1) * 1024), queue="pool")
            P.dma(lambda e: e.dma_start(out=wkv_b.ap, in_=w_kvx), writes=wkv_b.k(), queue="pool")
            wq4 = wq_b.ap.rearrange("p k (h c) -> p k h c", h=8)
            for k in range(2):
                P.op("pool", lambda e, k=k: e.tensor_scalar(out=wq4[:, k, :, 96:112], in0=wq4[:, k, :, 96:112], scalar1=-1.0, scalar2=0.0, op0=ALU.mult, op1=ALU.add),
                     reads=wq_b.k(k * 1024, (k + 1) * 1024), writes=wq_b.k(k * 1024, (k + 1) * 1024))
            P.dma(lambda e: e.dma_start(out=rsi_b.ap, in_=posr[s]), writes=rsi_b.k())
            P.op("dve", lambda e: e.tensor_copy(out=rs_b[0].ap, in_=rsi_b.ap), reads=rsi_b.k(), writes=rs_b[0].k())
            P.op("dve", lambda e: e.tensor_tensor(out=rs_b[0].ap, in0=rs_b[0].ap, in1=invf, op=ALU.mult), reads=rs_b[0].k() + CK, writes=rs_b[0].k())
            for which, dst in ((0, sin_b), (1, cos_b)):
                P.op("dve", lambda e, which=which: e.tensor_scalar(out=rs_b[1].ap, in0=rs_b[0].ap, scalar1=1.0 / (2 * math.pi), scalar2=0.25 * which, op0=ALU.mult, op1=ALU.add),
                     reads=rs_b[0].k(), writes=rs_b[1].k())
                P.op("dve", lambda e: e.tensor_copy(out=rsi_b.ap, in_=rs_b[1].ap), reads=rs_b[1].k(), writes=rsi_b.k())
                P.op("dve", lambda e: e.tensor_copy(out=rs_b[2].ap, in_=rsi_b.ap), reads=rsi_b.k(), writes=rs_b[2].k())
                P.op("dve", lambda e: e.tensor_tensor(out=rs_b[3].ap, in0=rs_b[1].ap, in1=rs_b[2].ap, op=ALU.subtract), reads=rs_b[1].k() + rs_b[2].k(), writes=rs_b[3].k())
                P.op("act", lambda e, dst=dst: e.activation(out=dst.ap, in_=rs_b[3].ap, func=AF.Sin, scale=6.28318), reads=rs_b[3].k(), writes=dst.k())
            cos3 = cos_b.ap.rearrange("p (s i) -> p s i", i=16)
            sin3 = sin_b.ap.rearrange("p (s i) -> p s i", i=16)
            def passA(j):
                for i in range(4):
                    st = j * 4 + i
                    t0 = tb + st * 128
                    xs = xs_b[st % 2]
                    hb = hb1_b[(j % 2) * 4 + i]
                    P.dma(lambda e, xs=xs, t0=t0: e.dma_start(out=xs.ap, in_=x_c[t0:t0 + 128, :]), writes=xs.k())
                    ssc, ssk = newcol()
                    P.op("act", lambda e, xs=xs, ssc=ssc: e.activation(out=junk1.ap, in_=xs.ap, func=AF.Square, accum_out=ssc), reads=xs.k(), writes=junk1.k() + [ssk])
                    rc, rk = rstd_of(ssc, ssk, DM)
                    P.op("dve", lambda e, xs=xs, hb=hb, rc=rc: e.scalar_tensor_tensor(out=hb.ap, in0=xs.ap, scalar=rc, in1=bc_h[:, GPM, :], op0=ALU.mult, op1=ALU.mult),
                         reads=xs.k() + [rk] + CK, writes=hb.k())

            passA(0)
            for j in range(4):
                hT = hT1_b[j % 2]
                TQ = TQ_b[j % 2]
                cs = j * 512
                P.op("pool", lambda e, TQ=TQ: e.memset(TQ.ap[:, :, 0:64], 1.0), writes=TQ.k())
                for q, src in ((64, cos3), (80, cos3), (96, sin3), (112, sin3)):
                    P.op("pool", lambda e, TQ=TQ, q=q, src=src, j=j: e.tensor_copy(out=TQ.ap[:, :, q:q + 16], in_=src[:, j * 4:(j + 1) * 4, :]),
                         reads=cos_b.k() + sin_b.k(), writes=TQ.k())
                for i in range(4):
                    hb = hb1_b[(j % 2) * 4 + i]
                    tbk = 6 + (i % 2)
                    for k in range(8):
                        P.op("pe", lambda e, hb=hb, k=k, tbk=tbk: e.transpose(out=bankbf(tbk)[:, k * 128:(k + 1) * 128], in_=hb.ap[:, k * 128:(k + 1) * 128], identity=ident),
                             reads=hb.k() + CK, writes=[pk[tbk]])
                    P.op("act", lambda e, hT=hT, i=i, tbk=tbk: e.activation(out=hT.ap[:, :, i * 128:(i + 1) * 128], in_=bankbf(tbk).rearrange("p (k t) -> p k t", k=8), func=AF.Copy),
                         reads=[pk[tbk]], writes=hT.k())
                if j < 3:
                    passA(j + 1)
                lat_banks = []
                for i in range(4):
                    b = [0, 1, 2, 3][i]
                    lat_banks.append(b)
                    for k in range(8):
                        P.op("pe", lambda e, b=b, k=k, i=i, hT=hT: e.matmul(out=bank(b, 416), lhsT=hT.ap[:, k, i * 128:(i + 1) * 128], rhs=win_b[k].ap[:, 1536:1952], start=(k == 0), stop=(k == 7)),
                             reads=win_b[k].k() + hT.k(), writes=[pk[b]])
                for i in range(4):
                    st = j * 4 + i
                    b = lat_banks[i]
                    qln, kvln, krt, rA, rB = qln_b[i], kvln_b[i], krt_b[i], ropeA_b[i], ropeB_b[i]
                    s1c, s1k = newcol()
                    P.op("act", lambda e, b=b, s1c=s1c: e.activation(out=junk1.ap[:, 0:256], in_=bank(b, 256), func=AF.Square, accum_out=s1c), reads=[pk[b]], writes=junk1.k() + [s1k])
                    s2c, s2k = newcol()
                    P.op("act", lambda e, b=b, s2c=s2c: e.activation(out=junk1.ap[:, 0:128], in_=bank(b, 128, 256), func=AF.Square, accum_out=s2c), reads=[pk[b]], writes=junk1.k() + [s2k])
                    r1c, r1k = rstd_of(s1c, s1k, 256)
                    r2c, r2k = rstd_of(s2c, s2k, 128)
                    P.op("act", lambda e, b=b, qln=qln, r1c=r1c: e.activation(out=qln.ap, in_=bank(b, 256), func=AF.Copy, scale=r1c), reads=[pk[b], r1k], writes=qln.k())
                    P.op("act", lambda e, b=b, kvln=kvln, r2c=r2c: e.activation(out=kvln.ap, in_=bank(b, 128, 256), func=AF.Copy, scale=r2c), reads=[pk[b], r2k], writes=kvln.k())
                    for half in range(2):
                        P.op("dve", lambda e, b=b, half=half, st=st, rA=rA: e.tensor_tensor(out=rA.ap[:, half * 16:(half + 1) * 16], in0=bank(b, 16, 384 + half * 16), in1=cos3[:, st, :], op=ALU.mult),
                             reads=[pk[b]] + cos_b.k(), writes=rA.k())
                        P.op("dve", lambda e, b=b, half=half, st=st, rB=rB: e.tensor_tensor(out=rB.ap[:, half * 16:(half + 1) * 16], in0=bank(b, 16, 384 + half * 16), in1=sin3[:, st, :], op=ALU.mult),
                             reads=[pk[b]] + sin_b.k(), writes=rB.k())
                    P.op("dve", lambda e, krt=krt, rA=rA, rB=rB: e.tensor_tensor(out=krt.ap[:, 0:16], in0=rA.ap[:, 0:16], in1=rB.ap[:, 16:32], op=ALU.subtract),
                         reads=rA.k() + rB.k(), writes=krt.k())
                    P.op("dve", lambda e, krt=krt, rA=rA, rB=rB: e.tensor_tensor(out=krt.ap[:, 16:32], in0=rA.ap[:, 16:32], in1=rB.ap[:, 0:16], op=ALU.add),
                         reads=rA.k() + rB.k(), writes=krt.k())
                for qk in range(2):
                    for m in range(4):
                        b = 4 + (rr["pb"] % 2)
                        rr["pb"] += 1
                        col = qk * 512 + m * 128
                        for k in range(8):
                            P.op("pe", lambda e, b=b, k=k, col=col, hT=hT: e.matmul(out=bank(b), lhsT=win_b[k].ap[:, col:col + 128], rhs=hT.ap[:, k, :], start=(k == 0), stop=(k == 7)),
                                 reads=win_b[k].k() + hT.k(), writes=[pk[b]])
                        sg = stg_b[rr["pb"] % 6]
                        if qk == 0:
                            P.op("act", lambda e, b=b, sg=sg: e.activation(out=sg.ap, in_=bank(b), func=AF.Copy, scale=0.125), reads=[pk[b]], writes=sg.k())
                            P.dma(lambda e, sg=sg, m=m, cs=cs: e.dma_start(out=sc_qs[m][:, cs:cs + 512], in_=sg.ap), reads=sg.k(), writes=[DK("qs", m, cs)])
                        else:
                            P.op("dve", lambda e, b=b, sg=sg: e.tensor_copy(out=sg.ap, in_=bank(b)), reads=[pk[b]], writes=sg.k())
                            P.dma(lambda e, sg=sg, m=m, cs=cs: e.dma_start(out=sc_ks[m][:, cs:cs + 512], in_=sg.ap), reads=sg.k(), writes=[DK("ks", m, cs)])
                for i in range(4):
                    qln, kvln, krt = qln_b[i], kvln_b[i], krt_b[i]
                    tbk = 6 + (i % 2)
                    for k in range(2):
                        P.op("pe", lambda e, k=k, qln=qln, tbk=tbk: e.transpose(out=bankbf(tbk)[:, k * 128:(k + 1) * 128], in_=qln.ap[:, k * 128:(k + 1) * 128], identity=ident), reads=qln.k() + CK, writes=[pk[tbk]])
                    P.op("pe", lambda e, kvln=kvln, tbk=tbk: e.transpose(out=bankbf(tbk)[:, 256:384], in_=kvln.ap, identity=ident), reads=kvln.k() + CK, writes=[pk[tbk]])
                    P.op("pe", lambda e, krt=krt, tbk=tbk: e.transpose(out=bankbf(tbk)[0:32, 384:512], in_=krt.ap, identity=ident), reads=krt.k() + CK, writes=[pk[tbk]])
                    for k in range(2):
                        P.op("act", lambda e, k=k, i=i, tbk=tbk: e.activation(out=qlT_b.ap[:, k, i * 128:(i + 1) * 128], in_=bankbf(tbk)[:, k * 128:(k + 1) * 128], func=AF.Copy, scale=vec_h[:, V_GQ + k:V_GQ + k + 1]),
                             reads=[pk[tbk]] + CK, writes=qlT_b.k(k * 512, (k + 1) * 512))
                    P.op("act", lambda e, i=i, tbk=tbk: e.activation(out=kvT_b.ap[:, i * 128:(i + 1) * 128], in_=bankbf(tbk)[:, 256:384], func=AF.Copy, scale=vec_h[:, V_GKV:V_GKV + 1]),
                         reads=[pk[tbk]] + CK, writes=kvT_b.k())
                    P.op("dve", lambda e, i=i, tbk=tbk: e.tensor_copy(out=krT_b.ap[0:32, i * 128:(i + 1) * 128], in_=bankbf(tbk)[0:32, 384:512]), reads=[pk[tbk]], writes=krT_b.k())
                    P.op("pe", lambda e, i=i, TQ=TQ: e.transpose(out=bank(2)[:, i * 128:(i + 1) * 128], in_=TQ.ap[:, i, :], identity=ident32), reads=TQ.k() + CK, writes=[pk[2]])
                P.op("act", lambda e: e.activation(out=TtT_b.ap, in_=bank(2), func=AF.Copy), reads=[pk[2]], writes=TtT_b.k())
                for rrow in range(2):
                    P.dma(lambda e, rrow=rrow, cs=cs: e.dma_start(out=sc_kr[rrow * 32:(rrow + 1) * 32, cs:cs + 512], in_=krT_b.ap[0:32, :]), reads=krT_b.k(), writes=[DK("kr", cs, rrow)])
                vst = vst_b[0]
                for i in range(4):
                    b = 4 + (rr["pb"] % 2)
                    rr["pb"] += 1
                    for k in range(8):
                        P.op("pe", lambda e, b=b, k=k, i=i, hT=hT: e.matmul(out=bank(b), lhsT=hT.ap[:, k, i * 128:(i + 1) * 128], rhs=win_b[k].ap[:, 1024:1536], start=(k == 0), stop=(k == 7)),
                             reads=win_b[k].k() + hT.k(), writes=[pk[b]])
                    if i % 2 == 0:
                        P.op("act", lambda e, b=b, i=i, vst=vst: e.activation(out=vst.ap[:, i, :], in_=bank(b), func=AF.Copy), reads=[pk[b]], writes=vst.k(i * 512, (i + 1) * 512))
                    else:
                        P.op("dve", lambda e, b=b, i=i, vst=vst: e.tensor_copy(out=vst.ap[:, i, :], in_=bank(b)), reads=[pk[b]], writes=vst.k(i * 512, (i + 1) * 512))
                P.dma(lambda e, vst=vst, cs=cs: e.dma_start(out=sc_vs[cs:cs + 512, :].rearrange("(i p) n -> p i n", p=128), in_=vst.ap), reads=vst.k(), writes=[DK("vs", cs)])
                for h in range(8):
                    b = [0, 1, 3][rr["pb"] % 3]
                    rr["pb"] += 1
                    for k in range(2):
                        P.op("pe", lambda e, b=b, k=k, h=h: e.matmul(out=bank(b), lhsT=wq_b.ap[:, k, h * 128:(h + 1) * 128], rhs=qlT_b.ap[:, k, :], start=(k == 0), stop=(k == 1)),
                             reads=wq_b.k() + qlT_b.k(), writes=[pk[b]])
                    sg = stg_b[rr["pb"] % 6]
                    P.op("dve", lambda e, b=b, sg=sg: e.tensor_tensor(out=sg.ap, in0=bank(b), in1=TtT_b.ap, op=ALU.mult), reads=[pk[b]] + TtT_b.k(), writes=sg.k())
                    P.dma(lambda e, sg=sg, h=h, cs=cs: e.dma_start(out=sc_qm[h][:, cs:cs + 512], in_=sg.ap), reads=sg.k(), writes=[DK("qm", h, cs)])
                for m in range(4):
                    b = [0, 1, 3][rr["pb"] % 3]
                    rr["pb"] += 1
                    P.op("pe", lambda e, b=b, m=m: e.matmul(out=bank(b), lhsT=wkv_b.ap[:, m * 128:(m + 1) * 128], rhs=kvT_b.ap, start=True, stop=True), reads=wkv_b.k() + kvT_b.k(), writes=[pk[b]])
                    sg = stg_b[rr["pb"] % 6]
                    P.op("act", lambda e, b=b, sg=sg: e.activation(out=sg.ap, in_=bank(b), func=AF.Copy), reads=[pk[b]], writes=sg.k())
                    P.dma(lambda e, sg=sg, m=m, cs=cs: e.dma_start(out=sc_km[m][:, cs:cs + 512], in_=sg.ap), reads=sg.k(), writes=[DK("km", m, cs)])
                vst = vst_b[1]
                for i in range(4):
                    b = [0, 1, 3][rr["pb"] % 3]
                    rr["pb"] += 1
                    P.op("pe", lambda e, b=b, i=i: e.matmul(out=bank(b), lhsT=kvT_b.ap[:, i * 128:(i + 1) * 128], rhs=wkv_b.ap[:, 512:1024], start=True, stop=True), reads=wkv_b.k() + kvT_b.k(), writes=[pk[b]])
                    P.op("dve", lambda e, b=b, i=i, vst=vst: e.tensor_copy(out=vst.ap[:, i, :], in_=bank(b)), reads=[pk[b]], writes=vst.k(i * 512, (i + 1) * 512))
                P.dma(lambda e, vst=vst, cs=cs: e.dma_start(out=sc_vm[cs:cs + 512, :].rearrange("(i p) n -> p i n", p=128), in_=vst.ap), reads=vst.k(), writes=[DK("vm", cs)])
                if s == 0 and j == 0:
                    load_resident_weights()

        conv_steps.extend([(lambda c=c: _conv_up(c)) for c in range(NCH)] + [(lambda c=c: _conv_dn(c)) for c in range(NCH)])
        junk1 = abuf(59392, 1024)

        def attn_blocks(qt):
            return [(kb, max(0, kb - 4 * qt) * 128) for kb in range(4 * qt + 3, -1, -1)]

        cnt = {"z": 0, "o": 0, "e": 0, "sp": 0, "a": 0, "oa": 0, "ost": 0}

        def run_pipeline(items, nst):
            n = len(items)
            for t in range(n + nst - 1):
                for k in range(nst):
                    i = t - k
                    if 0 <= i < n and items[i][k] is not None:
                        items[i][k]()

        gcount = {"g": 0, "b": 0}

        def stage2_sb(s):
            items = []

            def load_pair(m):
                QT = QT_b[m % 2]
                P.dma(lambda e: e.dma_start(out=QT.ap, in_=sc_qs[m]), reads=[DK("qs", m, c) for c in range(0, SEQ, 512)], writes=QT.k())
                for eh in range(2):
                    KTz, Vz = KTz_b[m % 2][eh], Vz_b[m % 2][eh]
                    Vz3 = Vz.ap.rearrange("p (kb n) -> p kb n", n=128)
                    oh = 1 - eh
                    P.dma(lambda e, KTz=KTz: e.dma_start(out=KTz.ap, in_=sc_ks[m]), reads=[DK("ks", m, c) for c in range(0, SEQ, 512)], writes=KTz.k())
                    P.op("pool", lambda e, KTz=KTz, oh=oh: e.memset(KTz.ap[oh * 64:(oh + 1) * 64, :], 0.0), reads=KTz.k(), writes=KTz.k())
                    P.dma(lambda e, Vz3=Vz3: e.dma_start(out=Vz3, in_=sc_vs[:, m * 128:(m + 1) * 128].rearrange("(kb p) n -> p kb n", p=128)),
                          reads=[DK("vs", c) for c in range(0, SEQ, 512)], writes=Vz.k())
                    P.op("pool", lambda e, Vz3=Vz3, oh=oh: e.memset(Vz3[:, :, oh * 64:(oh + 1) * 64], 0.0), reads=Vz.k(), writes=Vz.k())

            def make_item(m, qt, eh, bi, kb, c0, c0p, nblk, bo, pre):
                b = gcount["b"]
                gcount["b"] += 1
                QT, KT, V = QT_b[m % 2], KTz_b[m % 2][eh], Vz_b[m % 2][eh]
                V3 = V.ap.rearrange("p (kb n) -> p kb n", n=128)
                bz = b % 4
                et = et_b[b % 2]
                sp, spp = sp_b[b % 3], sp_b[(b - 1) % 3]
                At = At_b[b % 3]
                sa, sap = sacc_b[b % 3], sacc_b[(b - 1) % 3]
                diag = kb >= 4 * qt
                first = bi == 0
                last = bi == nblk - 1
                q0 = qt * 512 + c0
                q1 = (qt + 1) * 512

                def st0():
                    for f in pre:
                        f()
                    if first and eh == 0:
                        P.op("pe", lambda e: e.matmul(out=bank(bo), lhsT=zeros, rhs=QT.ap[:, 0:512], start=True, stop=False, skip_group_check=True),
                             reads=QT.k() + CK, writes=[pk[bo]])
                    P.op("pe", lambda e: e.matmul(out=bank(bz)[:, c0:512], lhsT=KT.ap[:, kb * 128:(kb + 1) * 128], rhs=QT.ap[:, q0:q1], start=True, stop=False, skip_group_check=True),
                         reads=KT.k() + QT.k(), writes=[pk[bz]])

                def st1():
                    P.op("act", lambda e: e.activation(out=et.ap[:, c0:512], in_=bank(bz)[:, c0:512], func=AF.Exp), reads=[pk[bz]], writes=et.k())
                    P.op("act", lambda e: e.activation(out=sp.ap[:, c0:512], in_=et.ap[:, c0:512], func=AF.Ln, bias=1.0), reads=et.k(), writes=sp.k())
                    if diag:
                        P.op("pool", lambda e: e.tensor_tensor(out=sp.ap[:, c0:c0 + 128], in0=sp.ap[:, c0:c0 + 128], in1=mstrict, op=ALU.mult), reads=sp.k() + CK, writes=sp.k())
                    if bi >= 1:
                        if c0 < c0p:
                            P.op("pool", lambda e: e.memset(sa.ap[:, c0:c0p], 0.0), writes=sa.k())
                        if bi == 1:
                            P.op("pool", lambda e: e.tensor_copy(out=sa.ap[:, c0p:512], in_=spp.ap[:, c0p:512]), reads=spp.k(), writes=sa.k())
                        else:
                            P.op("pool", lambda e: e.tensor_tensor(out=sa.ap[:, c0p:512], in0=sap.ap[:, c0p:512], in1=spp.ap[:, c0p:512], op=ALU.add), reads=sap.k() + spp.k(), writes=sa.k())

                def st2():
                    P.op("pe", lambda e: e.matmul(out=bank(bz)[:, c0:512], lhsT=negU, rhs=sp.ap[:, c0:512], start=False, stop=first, skip_group_check=True),
                         reads=sp.k() + CK, writes=[pk[bz]])
                    if not first:
                        P.op("pe", lambda e: e.matmul(out=bank(bz)[:, c0:512], lhsT=negOnes, rhs=sa.ap[:, c0:512], start=False, stop=True, skip_group_check=True),
                             reads=sa.k() + CK, writes=[pk[bz]])

                def st3():
                    P.op("act", lambda e: e.activation(out=At.ap[:, c0:512], in_=bank(bz)[:, c0:512], func=AF.Exp), reads=[pk[bz]], writes=At.k())
                    if diag:
                        P.op("pool", lambda e: e.tensor_tensor(out=At.ap[:, c0:c0 + 128], in0=At.ap[:, c0:c0 + 128], in1=mstrict, op=ALU.mult), reads=At.k() + CK, writes=At.k())

                def st4():
                    P.op("pe", lambda e: e.matmul(out=bank(bo)[:, c0:512], lhsT=V3[:, kb, :], rhs=At.ap[:, c0:512], start=False, stop=(last and eh == 1), skip_group_check=True),
                         reads=At.k() + V.k(), writes=[pk[bo]])
                    if last and eh == 1:
                        g = gcount["g"]
                        gcount["g"] += 1
                        ost = ost_b[g % 2]
                        P.op("dve", lambda e: e.tensor_copy(out=ost.ap, in_=bank(bo)), reads=[pk[bo]], writes=ost.k())
                        P.dma(lambda e: e.dma_start(out=sc_o[m][:, qt * 512:(qt + 1) * 512], in_=ost.ap), reads=ost.k(), writes=[DK("o", m, qt)])

                return [st0, st1, st2, st3, st4]

            load_pair(0)
            grp = 0
            for m in range(4):
                first_idx = len(items)
                for qt in range(4):
                    bo = 4 + (grp % 2)
                    grp += 1
                    for eh in range(2):
                        blocks = attn_blocks(qt)
                        for bi, (kb, c0) in enumerate(blocks):
                            pre = []
                            if m < 3 and len(items) == first_idx + 6:
                                pre.append(lambda m=m: load_pair(m + 1))
                            if conv_steps and len(items) % 7 == 3:
                                pre.append(conv_steps.pop(0))
                            c0p = blocks[bi - 1][1] if bi > 0 else 0
                            items.append(make_item(m, qt, eh, bi, kb, c0, c0p, len(blocks), bo, pre))
            run_pipeline(items, 5)

        def stage2_mla(s):
            items = []

            def load_head(h):
                QT, KT, V = QT_b[h % 2], KT_b[h % 2], V_b[h % 2]
                m, eh = h // 2, h % 2
                V3 = V.ap[:, 0:16 * 65].rearrange("p (kb n) -> p kb n", n=65)
                P.dma(lambda e: e.dma_start(out=QT.ap, in_=sc_qm[h]), reads=[DK("qm", h, c) for c in range(0, SEQ, 512)], writes=QT.k())
                P.dma(lambda e: e.dma_start(out=KT.ap[0:64, :], in_=sc_km[m][eh * 64:(eh + 1) * 64, :]), reads=[DK("km", m, c) for c in range(0, SEQ, 512)], writes=KT.k())
                P.dma(lambda e: e.dma_start(out=KT.ap[64:128, :], in_=sc_kr), reads=[DK("kr", c, r) for c in range(0, SEQ, 512) for r in range(2)], writes=KT.k())
                P.op("pool", lambda e: e.memset(V3[:, :, 64:65], 1.0), writes=V.k())
                P.dma(lambda e: e.dma_start(out=V3[:, :, 0:64], in_=sc_vm[:, h * 64:(h + 1) * 64].rearrange("(kb p) n -> p kb n", p=128)),
                      reads=[DK("vm", c) for c in range(0, SEQ, 512)], writes=V.k())

            def make_item(h, qt, bi, kb, c0, nblk, bo, g, pre):
                b = gcount["b"]
                gcount["b"] += 1
                QT, KT, V = QT_b[h % 2], KT_b[h % 2], V_b[h % 2]
                m, eh = h // 2, h % 2
                V3 = V.ap[:, 0:16 * 65].rearrange("p (kb n) -> p kb n", n=65)
                bz = b % 4
                At = At_b[b % 3]
                diag = kb >= 4 * qt
                first = bi == 0
                last = bi == nblk - 1
                q0 = qt * 512 + c0
                q1 = (qt + 1) * 512
                oa, rden, ost = oa_b[g % 2], rden_b[g % 2], ost_b[g % 2]

                def st0():
                    for f in pre:
                        f()
                    if first:
                        P.op("pe", lambda e: e.matmul(out=bank(bo)[0:65, :], lhsT=zeros[:, 0:65], rhs=QT.ap[:, 0:512], start=True, stop=False, skip_group_check=True),
                             reads=QT.k() + CK, writes=[pk[bo]])
                    P.op("pe", lambda e: e.matmul(out=bank(bz)[:, c0:512], lhsT=KT.ap[:, kb * 128:(kb + 1) * 128], rhs=QT.ap[:, q0:q1], start=True, stop=True),
                         reads=KT.k() + QT.k(), writes=[pk[bz]])

                def st1():
                    P.op("act", lambda e: e.activation(out=At.ap[:, c0:512], in_=bank(bz)[:, c0:512], func=AF.Exp, scale=SC_MLA), reads=[pk[bz]], writes=At.k())
                    if diag:
                        P.op("pool", lambda e: e.tensor_tensor(out=At.ap[:, c0:c0 + 128], in0=At.ap[:, c0:c0 + 128], in1=mle, op=ALU.mult), reads=At.k() + CK, writes=At.k())

                def st2():
                    P.op("pe", lambda e: e.matmul(out=bank(bo)[0:65, c0:512], lhsT=V3[:, kb, 0:65], rhs=At.ap[:, c0:512], start=False, stop=last, skip_group_check=True),
                         reads=At.k() + V.k(), writes=[pk[bo]])
                    if last:
                        P.op("act", lambda e: e.activation(out=oa.ap[0:65, :], in_=bank(bo)[0:65, :], func=AF.Copy), reads=[pk[bo]], writes=oa.k())

                def st3():
                    if last:
                        P.op("pe", lambda e: e.matmul(out=bank(6)[0:64, :], lhsT=esel[0:65, :], rhs=oa.ap[0:65, :], start=True, stop=True), reads=oa.k() + CK, writes=[pk[6]])

                def st4():
                    if last:
                        P.op("dve", lambda e: e.reciprocal(out=rden.ap[0:64, :], in_=bank(6)[0:64, :]), reads=[pk[6]], writes=rden.k())
                        P.op("dve", lambda e: e.tensor_tensor(out=ost.ap[0:64, :], in0=oa.ap[0:64, :], in1=rden.ap[0:64, :], op=ALU.mult), reads=oa.k() + rden.k(), writes=ost.k())
                        P.dma(lambda e: e.dma_start(out=sc_o[4 + m][eh * 64:(eh + 1) * 64, qt * 512:(qt + 1) * 512], in_=ost.ap[0:64, :]), reads=ost.k(), writes=[DK("o", 4 + m, qt, eh)])

                return [st0, st1, st2, st3, st4]

            load_head(0)
            for h in range(8):
                first_idx = len(items)
                for qt in range(4):
                    g = gcount["g"]
                    gcount["g"] += 1
                    bo = 4 + (g % 2)
                    blocks = attn_blocks(qt)
                    for bi, (kb, c0) in enumerate(blocks):
                        pre = []
                        if h < 7 and len(items) == first_idx + 6:
                            pre.append(lambda h=h: load_head(h + 1))
                        items.append(make_item(h, qt, bi, kb, c0, len(blocks), bo, g, pre))
            run_pipeline(items, 5)

        def norm_residual(src_ap, src_keys, gidx, xb, tA):
            ssc, ssk = newcol()
            P.op("act", lambda e: e.activation(out=junk_b.ap, in_=src_ap, func=AF.Square, accum_out=ssc), reads=src_keys, writes=junk_b.k() + [ssk])
            rc, rk = rstd_of(ssc, ssk, DM)
            P.op("dve", lambda e: e.scalar_tensor_tensor(out=tA.ap, in0=src_ap, scalar=rc, in1=bc_h[:, gidx, :], op0=ALU.mult, op1=ALU.mult), reads=src_keys + [rk] + CK, writes=tA.k())
            P.op("pool", lambda e: e.tensor_tensor(out=xb.ap, in0=xb.ap, in1=tA.ap, op=ALU.add), reads=xb.k() + tA.k(), writes=xb.k())

        def prenorm_T(xb, gidx, hb, i, tb_):
            ssc, ssk = newcol()
            P.op("act", lambda e: e.activation(out=junk_b.ap, in_=xb.ap, func=AF.Square, accum_out=ssc), reads=xb.k(), writes=junk_b.k() + [ssk])
            rc, rk = rstd_of(ssc, ssk, DM)
            P.op("dve", lambda e: e.scalar_tensor_tensor(out=hb.ap, in0=xb.ap, scalar=rc, in1=bc_h[:, gidx, :], op0=ALU.mult, op1=ALU.mult), reads=xb.k() + [rk] + CK, writes=hb.k())
            for k in range(8):
                P.op("pe", lambda e, k=k: e.transpose(out=bankbf(tb_)[:, k * 128:(k + 1) * 128], in_=hb.ap[:, k * 128:(k + 1) * 128], identity=ident), reads=hb.k() + CK, writes=[pk[tb_]])
            P.op("act", lambda e: e.activation(out=hT3_b.ap[:, :, i * 128:(i + 1) * 128], in_=bankbf(tb_).rearrange("p (k t) -> p k t", k=8), func=AF.Copy), reads=[pk[tb_]], writes=hT3_b.k())

        def nr_group(items, gidx):
            ss = []
            for src_ap, src_keys, xb, tA in items:
                ssc, ssk = newcol()
                ss.append((ssc, ssk))
                P.op("act", lambda e, src_ap=src_ap, ssc=ssc: e.activation(out=junk_b.ap, in_=src_ap, func=AF.Square, accum_out=ssc), reads=src_keys, writes=junk_b.k() + [ssk])
            rs = [rstd_of(ssc, ssk, DM) for ssc, ssk in ss]
            for (src_ap, src_keys, xb, tA), (rc, rk) in zip(items, rs):
                P.op("dve", lambda e, src_ap=src_ap, rc=rc, tA=tA: e.scalar_tensor_tensor(out=tA.ap, in0=src_ap, scalar=rc, in1=bc_h[:, gidx, :], op0=ALU.mult, op1=ALU.mult), reads=src_keys + [rk] + CK, writes=tA.k())
            for src_ap, src_keys, xb, tA in items:
                P.op("pool", lambda e, xb=xb, tA=tA: e.tensor_tensor(out=xb.ap, in0=xb.ap, in1=tA.ap, op=ALU.add), reads=xb.k() + tA.k(), writes=xb.k())

        def pn_group(items, gidx):
            ss = []
            for xb, hb, i, tb_ in items:
                ssc, ssk = newcol()
                ss.append((ssc, ssk))
                P.op("act", lambda e, xb=xb, ssc=ssc: e.activation(out=junk_b.ap, in_=xb.ap, func=AF.Square, accum_out=ssc), reads=xb.k(), writes=junk_b.k() + [ssk])
            rs = [rstd_of(ssc, ssk, DM) for ssc, ssk in ss]
            for (xb, hb, i, tb_), (rc, rk) in zip(items, rs):
                P.op("dve", lambda e, xb=xb, hb=hb, rc=rc: e.scalar_tensor_tensor(out=hb.ap, in0=xb.ap, scalar=rc, in1=bc_h[:, gidx, :], op0=ALU.mult, op1=ALU.mult), reads=xb.k() + [rk] + CK, writes=hb.k())
            for xb, hb, i, tb_ in items:
                for k in range(8):
                    P.op("pe", lambda e, k=k, hb=hb, tb_=tb_: e.transpose(out=bankbf(tb_)[:, k * 128:(k + 1) * 128], in_=hb.ap[:, k * 128:(k + 1) * 128], identity=ident), reads=hb.k() + CK, writes=[pk[tb_]])
            for xb, hb, i, tb_ in items:
                P.op("act", lambda e, i=i, tb_=tb_: e.activation(out=hT3_b.ap[:, :, i * 128:(i + 1) * 128], in_=bankbf(tb_).rearrange("p (k t) -> p k t", k=8), func=AF.Copy), reads=[pk[tb_]], writes=hT3_b.k())

        def wide(b0):
            return ps_h[:, b0 * 512:(b0 + 2) * 512], [pk[b0], pk[b0 + 1]]

        def stage3(s):
            tb = s * SEQ
            def prologue(j):
                cs = j * 512
                okeys = [DK("o", m, j) for m in range(4)] + [DK("o", 4 + m, j, eh) for m in range(4) for eh in range(2)]
                P.dma(lambda e, cs=cs: e.dma_start(out=OT_b.ap, in_=sc_o[:, :, cs:cs + 512].rearrange("c p t -> p c t")), reads=okeys, writes=OT_b.k())
                for g in range(2):
                    if g == 0:
                        P.op("act", lambda e: e.activation(out=sq_b.ap[:, 0:4, :], in_=OT_b.ap[:, 0:4, :], func=AF.Square), reads=OT_b.k(0, 2048), writes=sq_b.k(0, 2048))
                    else:
                        P.op("pool", lambda e: e.tensor_tensor(out=sq_b.ap[:, 4:8, :], in0=OT_b.ap[:, 4:8, :], in1=OT_b.ap[:, 4:8, :], op=ALU.mult), reads=OT_b.k(2048, 4096), writes=sq_b.k(2048, 4096))
                for g in range(2):
                    for c in range(4):
                        P.op("pe", lambda e, g=g, c=c: e.matmul(out=bank(6 + g), lhsT=gsum, rhs=sq_b.ap[:, g * 4 + c, :], start=(c == 0), stop=(c == 3)), reads=sq_b.k(g * 2048, (g + 1) * 2048) + CK, writes=[pk[6 + g]])
                    P.op("act", lambda e, g=g: e.activation(out=y_b[g].ap, in_=bank(6 + g), func=AF.Sqrt, bias=EPS), reads=[pk[6 + g]], writes=y_b[g].k())
                    P.op("dve", lambda e, g=g: e.reciprocal(out=y_b[g].ap, in_=y_b[g].ap), reads=y_b[g].k(), writes=y_b[g].k())
                for c in range(8):
                    P.op("dve", lambda e, c=c: e.scalar_tensor_tensor(out=OT_b.ap[:, c, :], in0=OT_b.ap[:, c, :], scalar=vec_h[:, V_GGRP + c:V_GGRP + c + 1], in1=y_b[c // 4].ap, op0=ALU.mult, op1=ALU.mult),
                         reads=OT_b.k(c * 512, (c + 1) * 512) + y_b[c // 4].k() + CK, writes=OT_b.k(c * 512, (c + 1) * 512))

            for j in range(4):
                cs = j * 512
                if j == 0:
                    prologue(0)
                for i in range(4):
                    P.dma(lambda e, i=i, cs=cs: e.dma_start(out=x1_b[i].ap, in_=x_c[tb + cs + i * 128:tb + cs + (i + 1) * 128, :]), writes=x1_b[i].k())
                for c in range(NCH):
                    P.dma(lambda e, c=c: e.dma_start(out=wd_b[c].ap, in_=sc_wd[c * 128:(c + 1) * 128, :]), reads=[DK("wd", c)], writes=wd_b[c].k())
                for pr in range(2):
                    grp = []
                    for i in (pr * 2, pr * 2 + 1):
                        b0 = (i % 2) * 2
                        for half in range(2):
                            for c in range(8):
                                P.op("pe", lambda e, half=half, c=c, i=i, b0=b0: e.matmul(out=bank(b0 + half), lhsT=OT_b.ap[:, c, i * 128:(i + 1) * 128], rhs=wo_h[:, c, half * 512:(half + 1) * 512], start=(c == 0), stop=(c == 7)),
                                     reads=OT_b.k(c * 512, (c + 1) * 512) + wo_b.keys, writes=[pk[b0 + half]])
                        wa, wk = wide(b0)
                        grp.append((wa, wk, x1_b[i], tmpA2[i % 2]))
                    nr_group(grp, GPOM)
                for pr in range(2):
                    pn_group([(x1_b[i], hb3_b[i % 2], i, 6 + (i % 2)) for i in (pr * 2, pr * 2 + 1)], GPF)
                for c in range(NCH):
                    wu = wu_b[c % 3]
                    P.dma(lambda e, wu=wu, c=c: e.dma_start(out=wu.ap.rearrange("p a k n -> p (a k n)"), in_=sc_wu[c]), reads=[DK("wu", c)], writes=wu.k())
                    ys = []
                    for gv in range(2):
                        b = 4 + (rr["pb"] % 4)
                        rr["pb"] += 1
                        for k in range(8):
                            P.op("pe", lambda e, b=b, k=k, gv=gv, wu=wu: e.matmul(out=bank(b), lhsT=wu.ap[:, gv, k, :], rhs=hT3_b.ap[:, k, :], start=(k == 0), stop=(k == 7)), reads=wu.k() + hT3_b.k(), writes=[pk[b]])
                        cc = gv * NCH + c
                        y = y_b[(c % 2) * 2 + gv]
                        w0 = vec_h[:, V_CW + cc * 3:V_CW + cc * 3 + 1]
                        w1 = vec_h[:, V_CW + cc * 3 + 1:V_CW + cc * 3 + 2]
                        w2 = vec_h[:, V_CW + cc * 3 + 2:V_CW + cc * 3 + 3]
                        bb = vec_h[:, V_CB + cc:V_CB + cc + 1]
                        P.op("act", lambda e, b=b, y=y, w2=w2, bb=bb: e.activation(out=y.ap, in_=bank(b), func=AF.Identity, scale=w2, bias=bb), reads=[pk[b]] + CK, writes=y.k())
                        P.op("dve", lambda e, b=b, y=y, w1=w1: e.scalar_tensor_tensor(out=y.ap[:, 1:512], in0=bank(b)[:, 0:511], scalar=w1, in1=y.ap[:, 1:512], op0=ALU.mult, op1=ALU.add), reads=[pk[b]] + y.k() + CK, writes=y.k())
                        P.op("dve", lambda e, b=b, y=y, w0=w0: e.scalar_tensor_tensor(out=y.ap[:, 2:512], in0=bank(b)[:, 0:510], scalar=w0, in1=y.ap[:, 2:512], op0=ALU.mult, op1=ALU.add), reads=[pk[b]] + y.k() + CK, writes=y.k())
                        if j > 0:
                            P.op("dve", lambda e, y=y, w0=w0, cc=cc: e.scalar_tensor_tensor(out=y.ap[:, 0:2], in0=halo_h[:, cc, :], scalar=w0, in1=y.ap[:, 0:2], op0=ALU.mult, op1=ALU.add), reads=[halo_keys[cc]] + y.k() + CK, writes=y.k())
                            P.op("dve", lambda e, y=y, w1=w1, cc=cc: e.scalar_tensor_tensor(out=y.ap[:, 0:1], in0=halo_h[:, cc, 1:2], scalar=w1, in1=y.ap[:, 0:1], op0=ALU.mult, op1=ALU.add), reads=[halo_keys[cc]] + y.k() + CK, writes=y.k())
                        P.op("dve", lambda e, b=b, cc=cc: e.tensor_copy(out=halo_h[:, cc, :], in_=bank(b)[:, 510:512]), reads=[pk[b]], writes=[halo_keys[cc]])
                        ys.append(y)
                    gg = gg_b[c % 2]
                    P.op("act", lambda e, gg=gg, y=ys[0]: e.activation(out=gg.ap, in_=y.ap, func=AF.Gelu_apprx_tanh), reads=ys[0].k(), writes=gg.k())
                    P.op("pool", lambda e, gg=gg, y=ys[1], c=c: e.tensor_tensor(out=fin_b.ap[:, c, :], in0=gg.ap, in1=y.ap, op=ALU.mult), reads=gg.k() + ys[1].k(), writes=fin_b.k(c * 512, (c + 1) * 512))
                for pp in range(2):
                    for c in range(NCH):
                        wd = wd_b[c]
                        for il in range(2):
                            i = pp * 2 + il
                            for half in range(2):
                                b = pp * 4 + il * 2 + half
                                P.op("pe", lambda e, b=b, c=c, i=i, half=half, wd=wd: e.matmul(out=bank(b), lhsT=fin_b.ap[:, c, i * 128:(i + 1) * 128], rhs=wd.ap[:, half * 512:(half + 1) * 512], start=(c == 0), stop=(c == NCH - 1)),
                                     reads=fin_b.k(c * 512, (c + 1) * 512) + wd.k(), writes=[pk[b]])
                for pr in range(2):
                    nr_group([(wide(i * 2)[0], wide(i * 2)[1], x1_b[i], tmpA2[i % 2]) for i in (pr * 2, pr * 2 + 1)], GPOF)
                for pr in range(2):
                    pn_group([(x1_b[i], hb3_b[i % 2], i, i % 2) for i in (pr * 2, pr * 2 + 1)], GPG)
                if j < 3:
                    prologue(j + 1)
                P.dma(lambda e, cs=cs: e.dma_start(out=pb_b.ap, in_=p_c[tb + cs:tb + cs + 512, :].rearrange("(i p) n -> p i n", p=128)), writes=pb_b.k(), queue="pool")
                for i in range(4):
                    tbk = 2 + (i % 2)
                    for k in range(2):
                        P.op("pe", lambda e, i=i, k=k, tbk=tbk: e.transpose(out=bankbf(tbk)[:, k * 128:(k + 1) * 128], in_=pb_b.ap[:, i, k * 128:(k + 1) * 128], identity=ident), reads=pb_b.k() + CK, writes=[pk[tbk]])
                    P.op("act", lambda e, i=i, tbk=tbk: e.activation(out=pT_b.ap[:, :, i * 128:(i + 1) * 128], in_=bankbf(tbk)[:, 0:256].rearrange("p (k t) -> p k t", k=2), func=AF.Copy), reads=[pk[tbk]], writes=pT_b.k())
                for pr in range(2):
                    subs = (pr * 2, pr * 2 + 1)
                    ctx = {}
                    for i in subs:
                        bs = (i % 2) * 4
                        for half in range(2):
                            for k in range(8):
                                P.op("pe", lambda e, half=half, k=k, i=i, bs=bs: e.matmul(out=bank(bs + half), lhsT=hT3_b.ap[:, k, i * 128:(i + 1) * 128], rhs=wpg_h[:, k, half * 512:(half + 1) * 512], start=(k == 0), stop=(k == 7)),
                                     reads=hT3_b.k() + wpg_b.keys, writes=[pk[bs + half]])
                            for k in range(2):
                                P.op("pe", lambda e, half=half, k=k, i=i, bs=bs: e.matmul(out=bank(bs + 2 + half), lhsT=pT_b.ap[:, k, i * 128:(i + 1) * 128], rhs=wple_h[:, k, half * 512:(half + 1) * 512], start=(k == 0), stop=(k == 1)),
                                     reads=pT_b.k() + wple_b.keys, writes=[pk[bs + 2 + half]])
                        ctx[i] = (wide(bs), wide(bs + 2), tmpA2[i % 2])
                    for i in subs:
                        (gla, glk), _, tA = ctx[i]
                        P.op("dve", lambda e, gla=gla, tA=tA: e.tensor_tensor(out=tA.ap, in0=gla, in1=bc_h[:, BPG, :], op=ALU.add), reads=glk + CK, writes=tA.k())
                    for i in subs:
                        tA = ctx[i][2]
                        P.op("act", lambda e, tA=tA: e.activation(out=tA.ap, in_=tA.ap, func=AF.Sigmoid), reads=tA.k(), writes=tA.k())
                    for i in subs:
                        _, (ea, ek), tA = ctx[i]
                        P.op("dve", lambda e, ea=ea, tA=tA: e.tensor_tensor(out=tA.ap, in0=ea, in1=tA.ap, op=ALU.mult), reads=ek + tA.k(), writes=tA.k())
                    sss = {}
                    for i in subs:
                        tA = ctx[i][2]
                        ssc, ssk = newcol()
                        sss[i] = (ssc, ssk)
                        P.op("act", lambda e, ssc=ssc, tA=tA: e.activation(out=junk_b.ap, in_=tA.ap, func=AF.Square, accum_out=ssc), reads=tA.k(), writes=junk_b.k() + [ssk])
                    rcs = {}
                    for i in subs:
                        rcs[i] = rstd_of(sss[i][0], sss[i][1], DM)
                    for i in subs:
                        tA = ctx[i][2]
                        rc, rk = rcs[i]
                        xb = x1_b[i]
                        t0 = tb + cs + i * 128
                        P.op("dve", lambda e, rc=rc, tA=tA: e.scalar_tensor_tensor(out=tA.ap, in0=tA.ap, scalar=rc, in1=bc_h[:, GPOP, :], op0=ALU.mult, op1=ALU.mult), reads=tA.k() + [rk] + CK, writes=tA.k())
                        P.op("pool", lambda e, xb=xb, tA=tA: e.tensor_tensor(out=xb.ap, in0=xb.ap, in1=tA.ap, op=ALU.add), reads=xb.k() + tA.k(), writes=xb.k())
                        P.dma(lambda e, xb=xb, t0=t0: e.dma_start(out=out_c[t0:t0 + 128, :], in_=xb.ap), reads=xb.k())

        for s in range(nseq):
            if upto >= 1:
                stage1(s)
            if upto >= 2:
                stage2_sb(s)
            if upto >= 3:
                stage2_mla(s)
            if upto >= 4:
                stage3(s)
        P.emit()
    return nc


def _consts():
    j = np.arange(128)[:, None]
    t = np.arange(128)[None, :]
    cb = np.zeros((128, 896), np.float32)
    cb[:, 0:128] = np.eye(128)
    cb[:, 128:256] = (j < t)
    cb[:, 256:384] = (j <= t)
    cb[:, 384:512] = -1.0 * (j >= t)
    cb[:, 512:640] = -1.0
    cb[:, 640:768] = 1.0 / 512
    cf = np.zeros((128, 448), np.float32)
    cf[:, 0:128] = np.eye(128)
    inv = (1.0 / (np.float32(10000.0) ** (np.arange(0, 32, 2, dtype=np.float32) / np.float32(32)))).astype(np.float32)
    cf[:, 128:384] = np.tile(inv, 16)[None, :]
    cf[64, 384:448] = 1.0
    return cb, cf


def prep_shared(w_in, g_pre_mix, g_q_lat, w_q_up, g_kv_lat, w_kv_up, g_grp_sb, g_grp_mla, w_o, g_post_mix,
                g_pre_ffn, w_up, conv_w, conv_b, w_down, g_post_ffn, w_ple, g_ple_gate, w_ple_gate, b_ple_gate, g_post_ple):
    f = np.float32
    c = np.ascontiguousarray
    wq = w_q_up[0].reshape(256, 8, 96)
    wqx = np.concatenate([wq, wq[:, :, 80:96], wq[:, :, 64:80]], axis=2).reshape(256, 1024)
    wkv = w_kv_up[0].reshape(128, 8, 128)
    wkvx = np.concatenate([wkv[:, :, 0:64].reshape(128, 512), wkv[:, :, 64:128].reshape(128, 512)], axis=1)
    wu = w_up[0].reshape(8, 128, 2, NCH, 128)
    wupr = c(wu.transpose(3, 1, 2, 0, 4)).reshape(NCH, 128, 2048)
    vec = np.zeros((128, 192), f)
    vec[:, 0:2] = g_q_lat[0].reshape(2, 128).T
    vec[:, 2:3] = g_kv_lat[0].reshape(1, 128).T
    vec[:, 3:11] = np.concatenate([g_grp_sb[0], g_grp_mla[0]]).reshape(8, 128).T
    cw = conv_w[0].reshape(3, 44, 128)
    vec[:, 11:143] = cw.transpose(2, 1, 0).reshape(128, 132)
    vec[:, 143:187] = conv_b[0].reshape(44, 128).T
    bcv = np.stack([g_pre_mix[0], g_post_mix[0], g_pre_ffn[0], g_post_ffn[0], g_ple_gate[0], g_post_ple[0], b_ple_gate[0]], 0)
    bcv = c(np.broadcast_to(bcv.reshape(1, 7 * 1024), (128, 7 * 1024))).astype(f)
    cb, cf = _consts()
    return {
        "w_in": c(w_in[0]), "w_qx": c(wqx), "w_kvx": c(wkvx), "w_o": c(w_o[0]), "w_pg": c(w_ple_gate[0]),
        "w_ple": c(w_ple[0]), "w_upr": wupr, "w_dn": c(w_down[0]), "vecs": vec, "bcv": bcv, "cf32": cf, "cbf": cb,
    }


def prep_core(x, p, positions, b0, nseq):
    xc = np.ascontiguousarray(x[b0:b0 + nseq].reshape(nseq * SEQ, DM))
    pc = np.ascontiguousarray(p[0, b0:b0 + nseq].reshape(nseq * SEQ, 256))
    pos = positions[b0:b0 + nseq].reshape(nseq, 16, 128).transpose(0, 2, 1)
    posr = np.ascontiguousarray(np.repeat(pos[:, :, :, None], 16, axis=3).reshape(nseq, 128, 256)).astype(np.int32)
    return {"x_c": xc, "p_c": pc, "posr": posr}


_NC_CACHE = {}


def kernel(x, p, positions, **w):
    x = np.asarray(x)
    p = np.asarray(p)
    positions = np.asarray(positions)
    w = {k: np.asarray(v) for k, v in w.items()}
    nseq = x.shape[0] // NCORES
    shared = prep_shared(**w)
    if nseq not in _NC_CACHE:
        _NC_CACHE[nseq] = build_program(nseq)
    nc = _NC_CACHE[nseq]
    in_maps = []
    for c in range(NCORES):
        m = dict(shared)
        m.update(prep_core(x, p, positions, c * nseq, nseq))
        in_maps.append(m)
    res = run_bass_kernel_spmd(nc, in_maps, core_ids=list(range(NCORES)))
    out = np.concatenate([r["out_c"].reshape(nseq, SEQ, DM) for r in res.results], axis=0)
    return out.astype(np.float32)
```
